# Optimizing a Trainium2 kernel written in Bass

```python
import jax, jax.numpy as jnp
from jax import lax
import numpy as np

D_MODEL = 2048
BATCH = 4
SEQ = 4096
DEPTH = 1
DEC_BATCH = 8
DEC_SEQ = 16
PAST_LEN = 2048

CHUNK = 64
D_MIX = D_MODEL
D_A = D_MIX // 2
D_B = D_MIX - D_A
HEAD_DIM = 128
H_A = D_A // HEAD_DIM
GMLP_CHUNK = 128
G_B = 8
D_GB = D_B // G_B
Q_BLOCK = 128
RMS_EPS = 1e-6
LN_EPS = 1e-5
FORGET_BIAS = 3.0
OFF_Q = 0
OFF_K = OFF_Q + D_A
OFF_V = OFF_K + D_A
OFF_F = OFF_V + D_A
OFF_GA = OFF_F + H_A
OFF_U = OFF_GA + D_A
OFF_VB = OFF_U + D_B
OFF_GB = OFF_VB + D_B
D_IN = OFF_GB + D_B

kernel_name = "hybrid_fox_gmlp_streaming_step"


def rmsnorm(x, g):
    xf = x.astype(jnp.float32)
    y = xf * lax.rsqrt(jnp.mean(xf * xf, axis=-1, keepdims=True) + RMS_EPS)
    return (y * g.astype(jnp.float32)).astype(x.dtype)


def layernorm(x, g, b):
    xf = x.astype(jnp.float32)
    mu = jnp.mean(xf, axis=-1, keepdims=True)
    var = jnp.mean(jnp.square(xf - mu), axis=-1, keepdims=True)
    y = (xf - mu) * lax.rsqrt(var + LN_EPS)
    return (y * g.astype(jnp.float32) + b.astype(jnp.float32)).astype(x.dtype)


def fox_prompt(q, k, v, logf):
    B, S, H, Dh = q.shape
    scale = HEAD_DIM ** -0.5
    cum = jnp.cumsum(logf, axis=1)
    cum_k = cum.transpose(0, 2, 1)
    nblk = S // Q_BLOCK
    qb = q.reshape(B, nblk, Q_BLOCK, H, Dh).transpose(1, 0, 2, 3, 4)
    cb = cum.reshape(B, nblk, Q_BLOCK, H).transpose(1, 0, 3, 2)
    pos_k = jnp.arange(S)

    def block(args):
        i, qi, ci = args
        s = jnp.einsum('bqhd,bkhd->bhqk', qi, k, preferred_element_type=jnp.float32) * scale
        s = s + ci[..., :, None] - cum_k[:, :, None, :]
        pos_q = i * Q_BLOCK + jnp.arange(Q_BLOCK)
        s = jnp.where(pos_k[None, :] <= pos_q[:, None], s, -jnp.inf)
        p = jax.nn.softmax(s, axis=-1)
        return jnp.einsum('bhqk,bkhd->bqhd', p.astype(v.dtype), v)

    out = lax.map(block, (jnp.arange(nblk), qb, cb))
    return out.transpose(1, 0, 2, 3, 4).reshape(B, S, H, Dh)


def fox_sample(q, k_new, v_new, logf_new, cache_k, cache_v, cache_logf):
    P = cache_k.shape[1]
    T = q.shape[1]
    scale = HEAD_DIM ** -0.5
    k_all = jnp.concatenate([cache_k.astype(k_new.dtype), k_new], axis=1)
    v_all = jnp.concatenate([cache_v.astype(v_new.dtype), v_new], axis=1)
    logf_all = jnp.concatenate([cache_logf.astype(jnp.float32), logf_new], axis=1)
    cum = jnp.cumsum(logf_all, axis=1).transpose(0, 2, 1)
    s = jnp.einsum('bqhd,bkhd->bhqk', q, k_all, preferred_element_type=jnp.float32) * scale
    s = s + cum[:, :, P:, None] - cum[:, :, None, :]
    pos_q = P + jnp.arange(T)
    pos_k = jnp.arange(P + T)
    s = jnp.where(pos_k[None, :] <= pos_q[:, None], s, -jnp.inf)
    p = jax.nn.softmax(s, axis=-1)
    return jnp.einsum('bhqk,bkhd->bqhd', p.astype(v_all.dtype), v_all)


def gmlp_mix(u, vn, w_s, b_s):
    B, S, _ = u.shape
    C = min(S, GMLP_CHUNK)
    n = S // C
    tri = jnp.tril(jnp.ones((C, C), dtype=w_s.dtype))
    w = w_s[:, :C, :C] * tri
    vc = vn.reshape(B, n, C, G_B, D_GB)
    mixed = jnp.einsum('gts,bnsgc->bntgc', w, vc) + b_s[:, :C].T[:, :, None]
    return u * mixed.reshape(B, S, D_B)


def mixer_layer(x, norm_g, w_in, b_f, ln_g, ln_b, w_s, b_s, w_out, caches=None):
    B, S, _ = x.shape
    h = rmsnorm(x, norm_g)
    z = h @ w_in
    q = z[..., OFF_Q:OFF_K].reshape(B, S, H_A, HEAD_DIM)
    k = z[..., OFF_K:OFF_V].reshape(B, S, H_A, HEAD_DIM)
    v = z[..., OFF_V:OFF_F].reshape(B, S, H_A, HEAD_DIM)
    logf = jax.nn.log_sigmoid(z[..., OFF_F:OFF_GA].astype(jnp.float32) + b_f.astype(jnp.float32))
    g_a = z[..., OFF_GA:OFF_U]
    u = jax.nn.gelu(z[..., OFF_U:OFF_VB])
    v_b = layernorm(jax.nn.gelu(z[..., OFF_VB:OFF_GB]), ln_g, ln_b)
    g_b = z[..., OFF_GB:D_IN]
    if caches is None:
        att = fox_prompt(q, k, v, logf)
    else:
        att = fox_sample(q, k, v, logf, *caches)
    out_a = att.reshape(B, S, D_A) * jax.nn.silu(g_a)
    out_b = gmlp_mix(u, v_b, w_s, b_s) * jax.nn.silu(g_b)
    y = x + jnp.concatenate([out_a, out_b], axis=-1) @ w_out
    return y, k, v, logf, v_b


def setup_inputs(seed: int = 0) -> dict:
    key = jax.random.key(seed)
    ks = jax.random.split(key, 16)
    f32 = jnp.float32
    x_prompt = jax.random.normal(ks[0], (BATCH, SEQ, D_MODEL), f32)
    x_sample = jax.random.normal(ks[1], (DEC_BATCH, DEC_SEQ, D_MODEL), f32)
    cache_k = jax.random.normal(ks[2], (DEPTH, DEC_BATCH, PAST_LEN, H_A, HEAD_DIM), f32)
    cache_v = jax.random.normal(ks[3], (DEPTH, DEC_BATCH, PAST_LEN, H_A, HEAD_DIM), f32)
    cache_logf = jax.nn.log_sigmoid(FORGET_BIAS + 0.5 * jax.random.normal(ks[4], (DEPTH, DEC_BATCH, PAST_LEN, H_A), f32))
    norm_g = 1.0 + 0.05 * jax.random.normal(ks[5], (DEPTH, D_MODEL), f32)
    w_in = jax.random.normal(ks[6], (DEPTH, D_MODEL, D_IN), f32) * D_MODEL ** -0.5
    b_f = FORGET_BIAS + 0.5 * jax.random.normal(ks[7], (DEPTH, H_A), f32)
    ln_g = 1.0 + 0.05 * jax.random.normal(ks[8], (DEPTH, D_B), f32)
    ln_b = 0.02 * jax.random.normal(ks[9], (DEPTH, D_B), f32)
    w_s = jax.random.normal(ks[10], (DEPTH, G_B, GMLP_CHUNK, GMLP_CHUNK), f32) * GMLP_CHUNK ** -0.5
    b_s = 1.0 + 0.1 * jax.random.normal(ks[11], (DEPTH, G_B, GMLP_CHUNK), f32)
    w_out = jax.random.normal(ks[12], (DEPTH, D_MIX, D_MODEL), f32) * D_MIX ** -0.5
    final_g = 1.0 + 0.05 * jax.random.normal(ks[13], (D_MODEL,), f32)
    return {"x_prompt": x_prompt, "x_sample": x_sample, "cache_k": cache_k, "cache_v": cache_v,
            "cache_logf": cache_logf, "norm_g": norm_g, "w_in": w_in, "b_f": b_f, "ln_g": ln_g,
            "ln_b": ln_b, "w_s": w_s, "b_s": b_s, "w_out": w_out, "final_g": final_g}


def reference(x_prompt, x_sample, cache_k, cache_v, cache_logf, norm_g, w_in, b_f, ln_g, ln_b,
              w_s, b_s, w_out, final_g):
    hp, hs = x_prompt, x_sample
    kp, vp, fp, ksm, vsm, fsm, gsm = [], [], [], [], [], [], []
    for l in range(DEPTH):
        params = (norm_g[l], w_in[l], b_f[l], ln_g[l], ln_b[l], w_s[l], b_s[l], w_out[l])
        hp, k1, v1, f1, _ = mixer_layer(hp, *params)
        hs, k2, v2, f2, g2 = mixer_layer(hs, *params, caches=(cache_k[l], cache_v[l], cache_logf[l]))
        kp.append(k1); vp.append(v1); fp.append(f1)
        ksm.append(k2); vsm.append(v2); fsm.append(f2); gsm.append(g2)
    y_prompt = rmsnorm(hp, final_g)
    y_sample = rmsnorm(hs, final_g)
    return (y_prompt, y_sample, jnp.stack(kp), jnp.stack(vp), jnp.stack(fp),
            jnp.stack(ksm), jnp.stack(vsm), jnp.stack(fsm), jnp.stack(gsm))
```

```python
import numpy as np
import concourse.bass as bass
import concourse.mybir as mybir
from concourse.bass_utils import run_bass_kernel_spmd

F32 = mybir.dt.float32
BF16 = mybir.dt.bfloat16
AF = mybir.ActivationFunctionType
ALU = mybir.AluOpType

D = 2048
KC = 16
NB = 16
NS = 16
TOWN = NB * 128 + NS
PAST = 2048
DIN = 7176
OFF_Q, OFF_K, OFF_V, OFF_F, OFF_GA, OFF_U, OFF_VB, OFF_GB = 0, 1024, 2048, 3072, 3080, 4104, 5128, 6152
SCALE = 128 ** -0.5
RMS_EPS = 1e-6
LN_EPS = 1e-5
NEG = -1.0e6
NTOK_S = 2 * NB * 128 + NS


class Res:
    def __init__(self, name, prev=(), excl=False):
        self.name = name
        self.excl = excl
        self.w = {}
        self.r = {}
        self.dsem = None
        for p in prev:
            for d in (p.w, p.r):
                for k, ev in d.items():
                    if k not in self.r or self.r[k][1] < ev[1]:
                        self.r[k] = ev


def _merge(d, ev):
    k = id(ev[0])
    if k not in d or d[k][1] < ev[1]:
        d[k] = ev


class DSem:
    def __init__(self, nc, name):
        self.sem = nc.alloc_semaphore(name)
        self.cnt = 0


class Eng:
    def __init__(self, nc, eng, name, is_pe=False, compute=True):
        self.nc = nc
        self.eng = eng
        self.name = name
        self.is_pe = is_pe
        self.sem = nc.alloc_semaphore("sem_" + name) if compute else None
        self.cnt = 0
        self.seen = {}
        self.last_unsig = False

    def _wait(self, ev, raw=True):
        sem, val = ev
        if sem is self.sem and self.is_pe:
            return
        if self.seen.get(id(sem), 0) >= val:
            return
        self.eng.wait_ge(sem, val)
        self.seen[id(sem)] = val

    def _deps(self, reads, writes, pwrites):
        for r in reads:
            for ev in r.w.values():
                self._wait(ev, raw=True)
            if r.excl:
                for ev in r.r.values():
                    self._wait(ev, raw=False)
        for w in writes:
            for ev in w.w.values():
                self._wait(ev, raw=False)
            for ev in w.r.values():
                self._wait(ev, raw=False)
        for w in pwrites:
            for ev in w.r.values():
                self._wait(ev, raw=False)

    def _update(self, ev, reads, writes, pwrites):
        for w in writes:
            w.w = {id(ev[0]): ev}
            w.r = {}
        for w in pwrites:
            _merge(w.w, ev)
        for r in reads:
            _merge(r.r, ev)

    def op(self, fn, reads=(), writes=(), pwrites=(), sig=True):
        self._deps(reads, writes, pwrites)
        ins = fn()
        if sig:
            ins.then_inc(self.sem, 1)
            self.cnt += 1
            ev = (self.sem, self.cnt)
            self.last_unsig = False
        else:
            ev = (self.sem, self.cnt + 1)
            self.last_unsig = True
        self._update(ev, reads, writes, pwrites)
        return ev

    def dma(self, out, in_, dsem, reads=(), writes=(), pwrites=()):
        self._deps(reads, writes, pwrites)
        self.eng.dma_start(out=out, in_=in_).then_inc(dsem.sem, 16)
        dsem.cnt += 16
        ev = (dsem.sem, dsem.cnt)
        self._update(ev, reads, writes, pwrites)
        return ev


def build_program(debug=False, stop_after=None):
    nc = bass.Bass("TRN2", target_bir_lowering=False)
    all_dsems = []

    def din(name, shape):
        return nc.dram_tensor(name, list(shape), F32, kind="ExternalInput").ap()

    def dout(name, shape):
        return nc.dram_tensor(name, list(shape), F32, kind="ExternalOutput").ap()

    skind = "ExternalOutput" if debug else "Internal"

    def dscr(name, shape, dt=BF16):
        return nc.dram_tensor(name, list(shape), dt, kind=skind).ap()

    x_own = din("x_own", [NB * 128, D])
    x_oth = din("x_oth", [NB * 128, D])
    x_smp = din("x_smp", [NS, D])
    ck = din("ck", [PAST, 1024])
    cv = din("cv", [PAST, 1024])
    clf = din("clf", [PAST, 8])
    w_in = din("w_in", [D, DIN])
    w_out = din("w_out", [D, D])
    norm_g = din("norm_g", [D])
    b_f = din("b_f", [8])
    ln_g = din("ln_g", [1024])
    ln_b = din("ln_b", [1024])
    w_s = din("w_s", [8, 128, 128])
    b_s = din("b_s", [8, 128])
    final_g = din("final_g", [D])
    fo_in = din("fo", [128, 128])
    fe_in = din("fe", [128, 128])
    mno_in = din("mno", [2, 128, 128])

    y_own = dout("y_own", [NB * 128, D])
    y_smp = dout("y_smp", [NS, D])
    k_own = dout("k_own", [NB * 128, 1024])
    v_own = dout("v_own", [NB * 128, 1024])
    lf_own = dout("lf_own", [NB * 128, 8])
    k_smp = dout("k_smp", [NS, 1024])
    v_smp = dout("v_smp", [NS, 1024])
    lf_smp = dout("lf_smp", [NS, 8])
    gv_smp = dout("gv_smp", [NS, 1024])

    KT_s = dscr("KT_s", [8, 128, NTOK_S])
    V_s = dscr("V_s", [NTOK_S, 1024])
    QT_s = dscr("QT_s", [8, 128, TOWN])
    GA_s = dscr("GA_s", [8, 128, TOWN])

    PE = Eng(nc, nc.tensor, "pe", is_pe=True)
    ACT = Eng(nc, nc.scalar, "act")
    DVE = Eng(nc, nc.vector, "dve")
    POOL = Eng(nc, nc.gpsimd, "pool")
    SP = Eng(nc, nc.sync, "sp", compute=False)
    store_events = []

    def dsem_of(res):
        if res.dsem is None:
            res.dsem = DSem(nc, "d_" + res.name)
        if res.dsem not in all_dsems:
            all_dsems.append(res.dsem)
        return res.dsem

    def drain():
        for ds in all_dsems:
            if ds.cnt > 0:
                SP._wait((ds.sem, ds.cnt))
        for e in (PE, ACT, DVE, POOL):
            if e.cnt > 0:
                SP._wait((e.sem, e.cnt))

    def regroup(res_list):
        for r in res_list:
            ds = r.dsem
            r.w = {id(ds.sem): (ds.sem, ds.cnt)}

    def load(q, out, in_, res, extra_reads=()):
        return q.dma(out, in_, dsem_of(res), reads=list(extra_reads), writes=[res])

    def store(q, out, in_, res, dram_res=None, final=True):
        ev = q.dma(out, in_, dsem_of(res), reads=[res], pwrites=[dram_res] if dram_res is not None else [])
        if final:
            store_events.append(ev)
        return ev

    PSB = [nc.alloc_psum_tensor("psb%d" % i, [128, 512], F32) for i in range(8)]
    PSR = [Res("psb%d" % i, excl=True) for i in range(8)]

    def sb(name, shape, dt=F32):
        return nc.alloc_sbuf_tensor(name, list(shape), dt)

    ident_f = sb("ident_f", [128, 128]); ident_b = sb("ident_b", [128, 128], BF16)
    utri_f = sb("utri_f", [128, 128])
    ones_f = sb("ones_f", [128, 128]); ones_b = sb("ones_b", [128, 128], BF16)
    mneg_d = sb("mneg_d", [128, 128])
    mneg_o = sb("mneg_o", [128, 2, 128])
    mneg_b = sb("mneg_b", [128, 3, 128], BF16)
    fo_t = sb("fo_t", [128, 128]); fe_t = sb("fe_t", [128, 128])
    ng16 = sb("ng16", [16, 128]); gcol = sb("gcol", [128, 16])
    bfb = sb("bfb", [128, 8])
    bsb = sb("bsb", [128, 8, 128])
    WT = sb("WT", [128, 8, 128], BF16)
    wf = sb("wf", [128, 16, 8], BF16)
    tz = sb("tz", [128, 33, 8])
    lfc = sb("lfc", [128, 16, 8])
    cum = sb("cum", [128, 49, 8])
    ncum = sb("ncum", [128, 49, 8])
    tmpA = sb("tmpA", [128, 128]); tmpB = sb("tmpB", [128, 128]); tmpC = sb("tmpC", [128, 128])
    tmpD = sb("tmpD", [128, 128]); tmpE = sb("tmpE", [128, 128])
    ones16 = sb("ones16", [128, 16])
    sml = sb("sml", [128, 64])
    R_consts = Res("consts")
    R_tz = [Res("tz%d" % i) for i in range(33)]
    R_cum = Res("cum")

    rem = nc.sbuf_bytes_remaining
    HB = KC * TOWN * 2
    WB = HB
    MB = (rem - HB - WB - 64) // 64 * 64
    assert MB >= 59600, MB
    arena = nc.alloc_sbuf_tensor("arena", [128, (HB + WB + MB) // 4], F32)
    H_OFF, W_OFF, M_OFF = 0, HB, HB + WB

    def view(off, shape, dt):
        es = 4 if dt == F32 else 2
        n = int(np.prod(shape))
        nbytes = n * es
        assert off % 4 == 0 and nbytes % 4 == 0, (off, shape)
        ap = arena[:, off // 4:(off + nbytes) // 4]
        if dt != F32:
            ap = ap.bitcast(dt)
        if len(shape) == 2:
            ap = ap.rearrange("p (a b) -> p a b", a=shape[0])
        elif len(shape) == 3:
            ap = ap.rearrange("p (a b c) -> p a b c", a=shape[0], b=shape[1])
        return ap

    class Bump:
        def __init__(self, base, size):
            self.base, self.size, self.off = base, size, 0

        def take(self, shape, dt):
            es = 4 if dt == F32 else 2
            nbytes = (int(np.prod(shape)) * es + 31) // 32 * 32
            assert self.off + nbytes <= self.size, ("arena overflow", self.off, nbytes, self.size)
            v = view(self.base + self.off, shape, dt)
            self.off += nbytes
            return v

    H = view(H_OFF, [KC, TOWN], BF16)
    R_H = [Res("H%d" % j) for j in range(NB + 1)]

    wsf = view(M_OFF, [8, 128], F32)
    g = nc.gpsimd

    def pool(fn, reads=(), writes=()):
        return POOL.op(fn, reads=reads, writes=writes)

    R_c = {n: Res(n) for n in ["ident_f", "ident_b", "utri", "ones_f", "ones_b", "mneg_d",
                               "ones16", "mneg_o", "fo", "fe", "ng16", "gcol", "bfb", "bsb", "wsf", "WT", "wf", "lfc"]}
    pool(lambda: g.memset(ident_f[:], 1.0), writes=[R_c["ident_f"]])
    pool(lambda: g.affine_select(out=ident_f[:], in_=ident_f[:], pattern=[[-1, 128]], compare_op=ALU.is_equal,
                                 fill=0.0, base=0, channel_multiplier=1),
         reads=[R_c["ident_f"]], writes=[R_c["ident_f"]])
    pool(lambda: g.tensor_copy(out=ident_b[:], in_=ident_f[:]), reads=[R_c["ident_f"]], writes=[R_c["ident_b"]])
    pool(lambda: g.memset(utri_f[:], 1.0), writes=[R_c["utri"]])
    pool(lambda: g.affine_select(out=utri_f[:], in_=utri_f[:], pattern=[[1, 128]], compare_op=ALU.is_ge,
                                 fill=0.0, base=0, channel_multiplier=-1),
         reads=[R_c["utri"]], writes=[R_c["utri"]])
    pool(lambda: g.memset(ones_f[:], 1.0), writes=[R_c["ones_f"]])
    pool(lambda: g.memset(ones_b[:], 1.0), writes=[R_c["ones_b"]])
    pool(lambda: g.memset(ones16[:], 1.0), writes=[R_c["ones16"]])
    pool(lambda: g.memset(cum[:, :, :].rearrange("p j h -> p (j h)"), 0.0), writes=[R_cum])
    pool(lambda: g.memset(mneg_d[:], 0.0), writes=[R_c["mneg_d"]])
    pool(lambda: g.affine_select(out=mneg_d[:], in_=mneg_d[:], pattern=[[1, 128]], compare_op=ALU.is_ge,
                                 fill=NEG, base=0, channel_multiplier=-1),
         reads=[R_c["mneg_d"]], writes=[R_c["mneg_d"]])

    ds_setup = DSem(nc, "d_setup")
    for nm in ["mneg_o", "fo", "fe", "ng16", "bfb", "bsb", "lfc", "wsf"]:
        R_c[nm].dsem = ds_setup
    load(SP, mneg_o[:], mno_in.rearrange("r k q -> k r q"), R_c["mneg_o"])
    load(SP, fo_t[:], fo_in, R_c["fo"])
    load(SP, fe_t[:], fe_in, R_c["fe"])
    load(SP, ng16[:], norm_g.rearrange("(k p) -> k p", p=128), R_c["ng16"])
    load(SP, bfb[:], b_f.partition_broadcast(128), R_c["bfb"])
    load(SP, bsb[:].rearrange("p g t -> p (g t)"), b_s.rearrange("g t -> (g t)").partition_broadcast(128), R_c["bsb"])
    load(SP, lfc[:], clf.rearrange("(j p) h -> p j h", p=128), R_c["lfc"])
    load(SP, wsf[:, :, :], w_s.rearrange("g t s -> t g s"), R_c["wsf"])
    regroup([R_c[nm] for nm in ["mneg_o", "fo", "fe", "ng16", "bfb", "bsb", "lfc", "wsf"]])

    R_mnb = Res("mneg_b")
    pool(lambda: g.tensor_copy(out=mneg_b[:, 0, :], in_=mneg_d[:, :]), reads=[R_c["mneg_d"]], writes=[R_mnb])
    pool(lambda: g.tensor_copy(out=mneg_b[:, 1:3, :], in_=mneg_o[:, :, :]), reads=[R_c["mneg_o"], R_mnb], writes=[R_mnb])
    WK = view(W_OFF, [KC, 1024], BF16)
    WV = view(W_OFF + 32768, [KC, 1024], BF16)
    R_WK = [Res("WK%d" % k) for k in range(KC)]
    R_WV = [Res("WV%d" % k) for k in range(KC)]
    ds_wk = [DSem(nc, "d_wk%d" % i) for i in range(4)]
    ds_wv = [DSem(nc, "d_wv%d" % i) for i in range(4)]
    for k in range(KC):
        R_WK[k].dsem = ds_wk[k // 4]
        R_WV[k].dsem = ds_wv[k // 4]
    for k in range(KC):
        load(POOL, WK[:, k, :], w_in[k * 128:(k + 1) * 128, OFF_K:OFF_K + 1024], R_WK[k])
    regroup(R_WK)
    load(POOL, wf[:], w_in.rearrange("(k p) c -> p k c", p=128)[:, :, OFF_F:OFF_F + 8], R_c["wf"])
    for k in range(KC):
        load(POOL, WV[:, k, :], w_in[k * 128:(k + 1) * 128, OFF_V:OFF_V + 1024], R_WV[k])
    regroup(R_WV)

    PE.op(lambda: nc.tensor.transpose(PSB[7][:, 0:16], ng16[:, :], ident_f[0:16, 0:16]),
          reads=[R_c["ng16"], R_c["ident_f"]], writes=[PSR[7]])
    DVE.op(lambda: nc.vector.tensor_copy(out=gcol[:], in_=PSB[7][:, 0:16]), reads=[PSR[7]], writes=[R_c["gcol"]])
    for half in range(2):
        for gg in range(4):
            gi = half * 4 + gg
            PE.op(lambda gi=gi, gg=gg, half=half: nc.tensor.transpose(
                PSB[5 + half][:, gg * 128:(gg + 1) * 128], wsf[:, gi, :], ident_f[:, :]),
                reads=[R_c["wsf"], R_c["ident_f"]], pwrites=[PSR[5 + half]], sig=(gg == 3))
        DVE.op(lambda half=half: nc.vector.tensor_tensor(
            out=WT[:, half * 4:(half + 1) * 4, :],
            in0=PSB[5 + half][:, :].rearrange("p (g t) -> p g t", g=4),
            in1=utri_f[:, :].unsqueeze(1).broadcast_to([128, 4, 128]), op=ALU.mult),
            reads=[PSR[5 + half], R_c["utri"]], pwrites=[R_c["WT"]])

    if stop_after == "setup":
        drain(); return nc
    MA = Bump(M_OFF, MB)
    xbuf = [MA.take([D], F32) for _ in range(2)]
    xn = [MA.take([D], BF16) for _ in range(2)]
    hTb = [MA.take([KC, 128], BF16) for _ in range(2)]
    Kf32 = MA.take([1024], F32)
    Vf32 = MA.take([1024], F32)
    Kb = [MA.take([1024], BF16) for _ in range(2)]
    Vb = [MA.take([1024], BF16) for _ in range(2)]
    KTsb = [MA.take([8, 128], BF16) for _ in range(2)]
    R_xbuf = [Res("xbuf%d" % i, prev=[R_c["wsf"]]) for i in range(2)]
    R_xn = [Res("xn%d" % i) for i in range(2)]
    R_hTb = [Res("hTb%d" % i) for i in range(2)]
    R_Kf32, R_Vf32 = Res("Kf32"), Res("Vf32")
    R_Kb = [Res("Kb%d" % i) for i in range(2)]
    R_Vb = [Res("Vb%d" % i) for i in range(2)]
    R_KTsb = [Res("KTsb%d" % i) for i in range(2)]
    R_ss = [Res("ss%d" % i) for i in range(2)]
    R_KTs = Res("KT_s"); R_Vs = Res("V_s"); R_QTs = Res("QT_s"); R_GAs = Res("GA_s")

    blocks = [("own", j) for j in range(NB)] + [("smp", 0)] + [("oth", j) for j in range(NB)]
    NBLK = len(blocks)
    psT = [PSB[0][:, :].bitcast(BF16).rearrange("p (k t) -> p k t", k=8),
           PSB[1][:, :].bitcast(BF16).rearrange("p (k t) -> p k t", k=8)]
    psKT = PSB[6][:, :].bitcast(BF16).rearrange("p (h t) -> p h t", h=8)

    def binfo(n):
        kind, j = blocks[n]
        rows = NS if kind == "smp" else 128
        if kind == "own":
            xsrc = x_own[j * 128:(j + 1) * 128, :]
            tcol = j * 128
            tzi = j
        elif kind == "oth":
            xsrc = x_oth[j * 128:(j + 1) * 128, :]
            tcol = 2048 + j * 128
            tzi = 16 + j
        else:
            xsrc = x_smp
            tcol = 4096
            tzi = 32
        return kind, j, rows, xsrc, tcol, tzi

    def hT_of(n):
        kind, j, rows, _, _, _ = binfo(n)
        if kind == "own":
            return H[:, :, j * 128:(j + 1) * 128], R_H[j]
        if kind == "smp":
            return H[:, :, 2048:2048 + NS], R_H[NB]
        return hTb[n % 2][:, :, :], R_hTb[n % 2]

    def a_load(n):
        kind, j, rows, xsrc, _, _ = binfo(n)
        load(SP, xbuf[n % 2][0:rows, :], xsrc, R_xbuf[n % 2])

    def a_norm(n):
        kind, j, rows, _, _, _ = binfo(n)
        p = n % 2
        ss = sml[0:rows, p * 4:p * 4 + 1]
        sd = sml[0:rows, p * 4 + 1:p * 4 + 2]
        rs = sml[0:rows, p * 4 + 2:p * 4 + 3]
        ACT.op(lambda: nc.scalar.activation(out=xn[p][0:rows, :], in_=xbuf[p][0:rows, :], func=AF.Square,
                                            accum_out=ss), reads=[R_xbuf[p]], writes=[R_xn[p], R_ss[p]])
        ACT.op(lambda: nc.scalar.activation(out=sd, in_=ss, func=AF.Sqrt, scale=1.0 / D, bias=RMS_EPS),
               reads=[R_ss[p]], writes=[R_ss[p]])
        DVE.op(lambda: nc.vector.reciprocal(out=rs, in_=sd), reads=[R_ss[p]], writes=[R_ss[p]])
        ACT.op(lambda: nc.scalar.activation(out=xn[p][0:rows, :], in_=xbuf[p][0:rows, :], func=AF.Copy, scale=rs),
               reads=[R_xbuf[p], R_ss[p]], writes=[R_xn[p]])

    def a_transp(n):
        kind, j, rows, _, _, _ = binfo(n)
        p = n % 2
        for k in range(KC):
            PE.op(lambda k=k: nc.tensor.transpose(psT[k // 8][:, k % 8, 0:rows], xn[p][0:rows, k * 128:(k + 1) * 128],
                                                  ident_b[0:rows, 0:rows]),
                  reads=[R_xn[p], R_c["ident_b"]], pwrites=[PSR[0], PSR[1]] if k == 0 else [], sig=(k == KC - 1))
        dst, rdst = hT_of(n)
        for half in range(2):
            DVE.op(lambda half=half: nc.vector.tensor_tensor(
                out=dst[:, half * 8:(half + 1) * 8, :], in0=psT[half][:, :, 0:rows],
                in1=gcol[:, half * 8:(half + 1) * 8].unsqueeze(2).broadcast_to([128, 8, rows]), op=ALU.mult),
                reads=[PSR[half], R_c["gcol"]], pwrites=[rdst])

    def a_Kmm(n):
        kind, j, rows, _, _, _ = binfo(n)
        hT, rh = hT_of(n)
        for k in range(KC):
            for c in range(2):
                PE.op(lambda k=k, c=c: nc.tensor.matmul(PSB[2 + c][0:rows, :], hT[:, k, :], WK[:, k, c * 512:(c + 1) * 512],
                                                        start=(k == 0), stop=(k == KC - 1)),
                      reads=[rh, R_WK[k]], writes=[PSR[2 + c]] if k == 0 else [], pwrites=[] if k == 0 else [PSR[2 + c]],
                      sig=(k == KC - 1))

    def a_Vmm(n):
        kind, j, rows, _, _, _ = binfo(n)
        hT, rh = hT_of(n)
        for k in range(KC):
            for c in range(2):
                PE.op(lambda k=k, c=c: nc.tensor.matmul(PSB[4 + c][0:rows, :], hT[:, k, :], WV[:, k, c * 512:(c + 1) * 512],
                                                        start=(k == 0), stop=(k == KC - 1)),
                      reads=[rh, R_WV[k]], writes=[PSR[4 + c]] if k == 0 else [], pwrites=[] if k == 0 else [PSR[4 + c]],
                      sig=(k == KC - 1))
            PE.op(lambda k=k: nc.tensor.matmul(PSB[7][0:rows, 0:8], hT[:, k, :], wf[:, k, :],
                                               start=(k == 0), stop=(k == KC - 1)),
                  reads=[rh, R_c["wf"]], writes=[PSR[7]] if k == 0 else [], pwrites=[] if k == 0 else [PSR[7]],
                  sig=(k == KC - 1))

    def a_Kepi(n):
        kind, j, rows, _, _, _ = binfo(n)
        p = n % 2
        if kind != "oth":
            for c in range(2):
                ACT.op(lambda c=c: nc.scalar.activation(out=Kf32[0:rows, c * 512:(c + 1) * 512], in_=PSB[2 + c][0:rows, :],
                                                        func=AF.Copy),
                       reads=[PSR[2 + c]], writes=[R_Kf32] if c == 0 else [], pwrites=[] if c == 0 else [R_Kf32])
            POOL.op(lambda: g.tensor_copy(out=Kb[p][0:rows, :], in_=Kf32[0:rows, :]), reads=[R_Kf32], writes=[R_Kb[p]])
        else:
            for c in range(2):
                DVE.op(lambda c=c: nc.vector.tensor_copy(out=Kb[p][0:rows, c * 512:(c + 1) * 512], in_=PSB[2 + c][0:rows, :]),
                       reads=[PSR[2 + c]], writes=[R_Kb[p]] if c == 0 else [], pwrites=[] if c == 0 else [R_Kb[p]])

    def a_KTtr(n):
        kind, j, rows, _, _, _ = binfo(n)
        p = n % 2
        for h in range(8):
            PE.op(lambda h=h: nc.tensor.transpose(psKT[:, h, 0:rows], Kb[p][0:rows, h * 128:(h + 1) * 128],
                                                  ident_b[0:rows, 0:rows]),
                  reads=[R_Kb[p], R_c["ident_b"]], writes=[PSR[6]] if h == 0 else [], pwrites=[] if h == 0 else [PSR[6]],
                  sig=(h == 7))

    def a_Vepi(n):
        kind, j, rows, _, _, tzi = binfo(n)
        p = n % 2
        if kind != "oth":
            for c in range(2):
                ACT.op(lambda c=c: nc.scalar.activation(out=Vf32[0:rows, c * 512:(c + 1) * 512], in_=PSB[4 + c][0:rows, :],
                                                        func=AF.Copy),
                       reads=[PSR[4 + c]], writes=[R_Vf32] if c == 0 else [], pwrites=[] if c == 0 else [R_Vf32])
            POOL.op(lambda: g.tensor_copy(out=Vb[p][0:rows, :], in_=Vf32[0:rows, :]), reads=[R_Vf32], writes=[R_Vb[p]])
        else:
            for c in range(2):
                DVE.op(lambda c=c: nc.vector.tensor_copy(out=Vb[p][0:rows, c * 512:(c + 1) * 512], in_=PSB[4 + c][0:rows, :]),
                       reads=[PSR[4 + c]], writes=[R_Vb[p]] if c == 0 else [], pwrites=[] if c == 0 else [R_Vb[p]])
        DVE.op(lambda: nc.vector.tensor_tensor(out=tz[0:rows, tzi, :], in0=PSB[7][0:rows, 0:8], in1=bfb[0:rows, :], op=ALU.add),
               reads=[PSR[7], R_c["bfb"]], writes=[R_tz[tzi]])
        DVE.op(lambda: nc.vector.tensor_copy(out=KTsb[p][:, :, 0:rows], in_=psKT[:, :, 0:rows]),
               reads=[PSR[6]], writes=[R_KTsb[p]])

    def a_stores(n):
        kind, j, rows, _, tcol, _ = binfo(n)
        p = n % 2
        if kind == "own":
            store(SP, k_own[j * 128:(j + 1) * 128, :], Kf32[:, :], R_Kf32)
            store(SP, v_own[j * 128:(j + 1) * 128, :], Vf32[:, :], R_Vf32)
        elif kind == "smp":
            store(SP, k_smp, Kf32[0:rows, :], R_Kf32)
            store(SP, v_smp, Vf32[0:rows, :], R_Vf32)
        store(SP, KT_s.rearrange("h d t -> d h t")[:, :, tcol:tcol + rows], KTsb[p][:, :, 0:rows], R_KTsb[p], R_KTs, final=False)
        store(SP, V_s[tcol:tcol + rows, :], Vb[p][0:rows, :], R_Vb[p], R_Vs, final=False)

    a_load(0); a_load(1)
    a_norm(0); a_transp(0)
    for n in range(NBLK):
        if n + 2 < NBLK:
            a_load(n + 2)
        if n + 1 < NBLK:
            a_norm(n + 1)
        a_Kmm(n)
        if n + 1 < NBLK:
            a_transp(n + 1)
        a_Kepi(n)
        a_Vmm(n)
        a_KTtr(n)
        a_Vepi(n)
        a_stores(n)

    if stop_after == "A":
        drain(); return nc
    v = nc.vector
    tzf = tz[:, :, :].rearrange("p j h -> p (j h)")
    ACT.op(lambda: nc.scalar.activation(out=tzf[:, 0:256], in_=tzf[:, 0:256], func=AF.Exp, scale=-1.0),
           reads=R_tz[0:16] + R_tz[17:33], writes=[R_cum])
    ACT.op(lambda: nc.scalar.activation(out=tzf[0:NS, 256:264], in_=tzf[0:NS, 256:264], func=AF.Exp, scale=-1.0),
           reads=[R_tz[32]], pwrites=[R_cum])
    ACT.op(lambda: nc.scalar.activation(out=tzf[:, 0:256], in_=tzf[:, 0:256], func=AF.Ln, bias=1.0),
           reads=[R_cum], pwrites=[R_cum])
    ACT.op(lambda: nc.scalar.activation(out=tzf[0:NS, 256:264], in_=tzf[0:NS, 256:264], func=AF.Ln, bias=1.0),
           reads=[R_cum], pwrites=[R_cum])
    DVE.op(lambda: v.tensor_scalar(out=tzf[:, 0:256], in0=tzf[:, 0:256], scalar1=-1.0, scalar2=None, op0=ALU.mult),
           reads=[R_cum], pwrites=[R_cum])
    DVE.op(lambda: v.tensor_scalar(out=tzf[0:NS, 256:264], in0=tzf[0:NS, 256:264], scalar1=-1.0, scalar2=None, op0=ALU.mult),
           reads=[R_cum], pwrites=[R_cum])
    R_lf = Res("lf")
    store(SP, lf_own.rearrange("(j p) h -> p j h", p=128), tz[:, 0:16, :], R_cum)
    store(SP, lf_smp, tz[0:NS, 32, :], R_cum)
    cumv = {}
    prevA_ext = []
    prevA = R_xbuf + R_xn + R_hTb + [R_Kf32, R_Vf32] + R_Kb + R_Vb + R_KTsb
    mm = nc.tensor.matmul
    def cum_stage2():
        LO = tzf[:, 0:128]
        LT = tzf[:, 128:256]
        LS = tzf[0:NS, 256:264]
        lfcf = lfc[:, :, :].rearrange("p j h -> p (j h)")
        cumf = cum[:, :, :].rearrange("p j h -> p (j h)")
        mm = nc.tensor.matmul
        PE.op(lambda: mm(PSB[7][:, 0:128], utri_f[:, :], LO, start=True, stop=True), reads=[R_cum, R_c["utri"]], writes=[PSR[7]], sig=False)
        PE.op(lambda: mm(PSB[7][:, 128:256], utri_f[:, :], LT, start=True, stop=True), pwrites=[PSR[7]], sig=False)
        PE.op(lambda: mm(PSB[7][:, 256:384], ones_f[:, :], LO, start=True, stop=True), reads=[R_c["ones_f"]], pwrites=[PSR[7]], sig=False)
        PE.op(lambda: mm(PSB[7][:, 384:512], ones_f[:, :], LT, start=True, stop=True), pwrites=[PSR[7]])
        PE.op(lambda: mm(PSB[6][:, 0:128], utri_f[:, :], lfcf, start=True, stop=True), reads=[R_c["lfc"]], writes=[PSR[6]], sig=False)
        PE.op(lambda: mm(PSB[6][:, 128:256], ones_f[:, :], lfcf, start=True, stop=True), pwrites=[PSR[6]], sig=False)
        PE.op(lambda: mm(PSB[6][0:NS, 256:264], utri_f[0:NS, 0:NS], LS, start=True, stop=True), pwrites=[PSR[6]])
        cumv.update(LO=LO, LT=LT, LS=LS, lfcf=lfcf, cumf=cumf)

    def cum_stage3():
        cumf = cumv['cumf']
        R_t = Res("cumtmp")
        DVE.op(lambda: v.tensor_copy(out=tmpC[:, :], in_=PSB[7][:, 384:512]), reads=[PSR[7]], writes=[R_t])
        DVE.op(lambda: v.tensor_tensor(out=tmpA[:, :], in0=PSB[7][:, 256:384], in1=tmpC[:, :], op=ALU.add), reads=[R_t], pwrites=[R_t])
        tA = tmpA[:, :].rearrange("p (j h) -> p h j", h=8)
        tB = tmpB[:, :].rearrange("p (j h) -> p h j", h=8)
        for h in range(8):
            DVE.op(lambda h=h: v.tensor_tensor_scan(out=tB[:, h, :], data0=ones16[:, :], data1=tA[:, h, :], initial=0.0,
                                                    op0=ALU.mult, op1=ALU.add), reads=[R_t, R_c["ones16"]], pwrites=[R_t])
        DVE.op(lambda: v.tensor_tensor(out=tmpB[:, :], in0=tmpB[:, :], in1=tmpA[:, :], op=ALU.subtract), reads=[R_t], pwrites=[R_t])
        DVE.op(lambda: v.tensor_tensor(out=tmpD[:, :], in0=tmpC[:, :], in1=fo_t[:, :], op=ALU.mult), reads=[R_t, R_c["fo"]], pwrites=[R_t])
        DVE.op(lambda: v.tensor_tensor(out=tmpD[:, :], in0=tmpD[:, :], in1=tmpB[:, :], op=ALU.add), reads=[R_t], pwrites=[R_t])
        DVE.op(lambda: v.tensor_tensor(out=cumf[:, 0:128], in0=PSB[7][:, 0:128], in1=tmpD[:, :], op=ALU.add), reads=[R_t], pwrites=[R_t])
        DVE.op(lambda: v.tensor_tensor(out=tmpE[:, :], in0=PSB[7][:, 256:384], in1=fe_t[:, :], op=ALU.mult), reads=[R_c["fe"]], pwrites=[R_t])
        DVE.op(lambda: v.tensor_tensor(out=tmpE[:, :], in0=tmpE[:, :], in1=tmpB[:, :], op=ALU.add), reads=[R_t], pwrites=[R_t])
        DVE.op(lambda: v.tensor_tensor(out=cumf[:, 128:256], in0=PSB[7][:, 128:256], in1=tmpE[:, :], op=ALU.add), reads=[R_t], pwrites=[R_t])
        DVE.op(lambda: v.tensor_copy(out=tmpA[:, :], in_=PSB[6][:, 128:256]), reads=[PSR[6], R_t], pwrites=[R_t])
        for h in range(8):
            DVE.op(lambda h=h: v.tensor_tensor_scan(out=tB[:, h, :], data0=ones16[:, :], data1=tA[:, h, :], initial=0.0,
                                                    op0=ALU.mult, op1=ALU.add), reads=[R_t], pwrites=[R_t])
        DVE.op(lambda: v.tensor_tensor(out=tmpC[:, :], in0=tmpB[:, :], in1=tmpA[:, :], op=ALU.subtract), reads=[R_t], pwrites=[R_t])
        DVE.op(lambda: v.tensor_tensor(out=cumf[:, 33 * 8:49 * 8], in0=PSB[6][:, 0:128], in1=tmpC[:, :], op=ALU.add), reads=[R_t], pwrites=[R_t])
        DVE.op(lambda: v.tensor_tensor(out=cumf[0:NS, 256:264], in0=PSB[6][0:NS, 256:264], in1=tmpB[0:NS, 120:128], op=ALU.add),
               reads=[R_t], pwrites=[R_t])
        R_cum2 = Res("cum2")
        DVE.op(lambda: v.tensor_scalar(out=ncum[:, :, :].rearrange("p j h -> p (j h)"), in0=cumf[:, :], scalar1=-1.0 / SCALE,
                                       scalar2=None, op0=ALU.mult), reads=[R_t], writes=[R_cum2])
        MCUM = Bump(M_OFF + MB - 13312, 13312)
        xk = MCUM.take([392], F32); xq = MCUM.take([136], F32)
        xr1 = MCUM.take([392], F32); xf = MCUM.take([392], F32)
        spl = [MCUM.take([528], BF16) for i in range(3)]
        rows6 = MCUM.take([3, 6, 128], BF16)
        R_spl = Res("spl", prev=prevA)
        DVE.op(lambda: v.tensor_copy(out=xk[:, :].rearrange("p (h b) -> p b h", h=8), in_=ncum[:, :, :]), reads=[R_cum2], writes=[R_spl])
        DVE.op(lambda: v.memset(xq[:, :], 0.0), pwrites=[R_spl])
        xq3 = xq[:, :].rearrange("p (h b) -> p b h", h=8)
        DVE.op(lambda: v.tensor_scalar(out=xq3[:, 0:16, :], in0=cum[:, 0:16, :], scalar1=1.0 / SCALE, scalar2=None, op0=ALU.mult),
               reads=[R_t, R_spl], pwrites=[R_spl])
        DVE.op(lambda: v.tensor_scalar(out=xq3[0:NS, 16, :], in0=cum[0:NS, 32, :], scalar1=1.0 / SCALE, scalar2=None, op0=ALU.mult),
               reads=[R_t, R_spl], pwrites=[R_spl])
        for (src, c0, n) in ((xk, 0, 392), (xq, 392, 136)):
            cur = src
            for si in range(3):
                DVE.op(lambda cur=cur, si=si: v.tensor_copy(out=spl[si][:, c0:c0 + n], in_=cur[:, 0:n]), reads=[R_spl], pwrites=[R_spl])
                if si < 2:
                    DVE.op(lambda si=si: v.tensor_copy(out=xf[:, 0:n], in_=spl[si][:, c0:c0 + n]), reads=[R_spl], pwrites=[R_spl])
                    DVE.op(lambda cur=cur: v.tensor_tensor(out=xr1[:, 0:n], in0=cur[:, 0:n], in1=xf[:, 0:n], op=ALU.subtract),
                           reads=[R_spl], pwrites=[R_spl])
                    cur = xr1
        cumv.update(R_spl=R_spl, spl=spl, rows6=rows6, R_cum2=R_cum2)
        prevA_ext.extend([R_spl])

    def cum_stage4():
        R_spl, spl, rows6 = cumv['R_spl'], cumv['spl'], cumv['rows6']
        R_rows = Res("rows", prev=prevA)
        chunks_t = [(0, 128), (128, 128), (256, 128), (384, 8), (392, 128), (520, 8)]
        NCK_s = dscr("NCK_s", [3, 392, 128]); CQ3_s = dscr("CQ3_s", [3, 136, 128])
        R_NCKs, R_CQ3s = Res("NCK_s"), Res("CQ3_s")
        for si in range(3):
            pbank = PSB[5 + si][:, :].bitcast(BF16)
            for ci, (c0, n) in enumerate(chunks_t):
                PE.op(lambda si=si, ci=ci, c0=c0, n=n, pbank=pbank: nc.tensor.transpose(pbank[0:n, ci * 128:(ci + 1) * 128], spl[si][:, c0:c0 + n], ident_b[:, :]),
                      reads=[R_spl, R_c["ident_b"]], writes=[PSR[5 + si]] if ci == 0 else [], pwrites=[] if ci == 0 else [PSR[5 + si]],
                      sig=(ci == len(chunks_t) - 1))
            for ci, (c0, n) in enumerate(chunks_t):
                DVE.op(lambda si=si, ci=ci, n=n, pbank=pbank: v.tensor_copy(out=rows6[0:n, si, ci, :], in_=pbank[0:n, ci * 128:(ci + 1) * 128]),
                       reads=[PSR[5 + si]], pwrites=[R_rows])
        first = True
        for si in range(3):
            for ci, (c0, n) in enumerate(chunks_t):
                if c0 < 392:
                    dst, rd = NCK_s[si, c0:c0 + n, :], R_NCKs
                else:
                    dst, rd = CQ3_s[si, c0 - 392:c0 - 392 + n, :], R_CQ3s
                SP.dma(dst, rows6[0:n, si, ci, :], dsem_of(R_rows), reads=[R_rows], pwrites=[rd])

        cumv.update(R_NCKs=R_NCKs, R_CQ3s=R_CQ3s, NCK_s=NCK_s, CQ3_s=CQ3_s)
        prevA_ext.extend([R_rows])

    RING0 = W_OFF + HB // 2
    ring = [view(RING0 + s * 4096, [KC, 128], BF16) for s in range(8)]
    R_ring = [Res("ring%d" % s, prev=R_WV) for s in range(8)]
    WVB = view(W_OFF, [KC, 1024], BF16)
    R_WVB = [Res("WVB%d" % k, prev=R_WK) for k in range(KC)]
    for k in range(KC):
        R_WVB[k].dsem = ds_wk[k // 4]
    w_in_r = w_in.rearrange("(k p) c -> p k c", p=128)

    chunksC1 = [("q", h, OFF_Q + h * 128) for h in range(8)] + [("ga", h, OFF_GA + h * 128) for h in range(8)]
    chunksC2 = []
    for gi in range(8):
        chunksC2.append(("u", gi, OFF_U + gi * 128))
        chunksC2.append(("gb", gi, OFF_GB + gi * 128))
    allchunks = chunksC1 + chunksC2
    ring_state = {"next": 0}

    def ring_load(ci):
        kind, idx, col = allchunks[ci]
        s = ci % 8
        load(POOL, ring[s][:, :, :], w_in_r[:, :, col:col + 128], R_ring[s])

    tiles = []
    for (c0_, n_) in [(0, 416), (416, 416), (832, 416), (1248, 416), (1664, 400)]:
        blks = sorted(set(min(c // 128, NB) for c in range(c0_, c0_ + n_, 16)))
        tiles.append((c0_, n_, [R_H[b_] for b_ in blks]))
    bank_ctr = {"n": 0}

    def next_bank(lo=0, hi=6):
        b = lo + bank_ctr["n"] % (hi - lo)
        bank_ctr["n"] += 1
        return b

    MC = Bump(M_OFF, MB)
    vn = MC.take([NB, 1024], BF16)
    vn_s = MC.take([1024], BF16)
    R_vn = [Res("vn%d" % i, prev=prevA) for i in range(NB + 1)]
    stg = [MC.take([512], BF16) for _ in range(4)]
    R_stg = [Res("stg%d" % i, prev=prevA) for i in range(4)]
    mark_c = MC.off

    def c_chunk(ci, dst_sb=None, r_dst=None):
        kind, idx, col = allchunks[ci]
        s = ci % 8
        for ti, (c0, n, rh) in enumerate(tiles):
            b = next_bank()
            for k in range(KC):
                PE.op(lambda k=k, b=b, c0=c0, n=n: mm(PSB[b][:, 0:n], ring[s][:, k, :], H[:, k, c0:c0 + n],
                                                       start=(k == 0), stop=(k == KC - 1)),
                      reads=rh + [R_ring[s]], writes=[PSR[b]] if k == 0 else [], pwrites=[] if k == 0 else [PSR[b]],
                      sig=(k == KC - 1))
            func = {"q": AF.Copy, "ga": AF.Silu, "u": AF.Gelu_apprx_tanh, "gb": AF.Silu}[kind]
            if kind in ("q", "ga"):
                u = c_chunk.ctr % 4
                c_chunk.ctr += 1
                ACT.op(lambda b=b, n=n, u=u: nc.scalar.activation(out=stg[u][:, 0:n], in_=PSB[b][:, 0:n], func=func),
                       reads=[PSR[b]], writes=[R_stg[u]])
                dst = (QT_s if kind == "q" else GA_s)[idx, :, c0:c0 + n]
                store(SP, dst, stg[u][:, 0:n], R_stg[u], R_QTs if kind == "q" else R_GAs, final=False)
            else:
                ACT.op(lambda b=b, n=n, c0=c0: nc.scalar.activation(out=dst_sb[:, c0:c0 + n], in_=PSB[b][:, 0:n], func=func),
                       reads=[PSR[b]], writes=[r_dst] if ti == 0 else [], pwrites=[] if ti == 0 else [r_dst])
    c_chunk.ctr = 0

    for ci in range(4):
        ring_load(ci)
    for ci in range(16):
        if ci + 4 < len(allchunks):
            ring_load(ci + 4)
        load(POOL, WVB[:, ci, :], w_in[ci * 128:(ci + 1) * 128, OFF_VB:OFF_VB + 1024], R_WVB[ci])
        c_chunk(ci)
        if ci == 1:
            cum_stage2()
            cum_stage3()
        if ci == 3:
            cum_stage4()
    R_cum2, R_NCKs, R_CQ3s, NCK_s, CQ3_s = cumv["R_cum2"], cumv["R_NCKs"], cumv["R_CQ3s"], cumv["NCK_s"], cumv["CQ3_s"]

    if stop_after == "C1":
        drain(); return nc
    regroup(R_WVB)
    gxs = [MC.take([1024], F32) for _ in range(2)]
    vt = MC.take([1024], F32)
    lngb = MC.take([1024], F32)
    lnbb = MC.take([1024], F32)
    prevA = prevA + prevA_ext
    R_gxs = [Res("gx%d" % i, prev=prevA) for i in range(2)]
    R_vt = Res("vt", prev=prevA)
    R_gx, R_junk = R_gxs[0], R_gxs[1]
    R_ln = Res("ln", prev=prevA)
    R_st = [Res("st%d" % i) for i in range(2)]
    load(SP, lngb[:, :], ln_g.partition_broadcast(128), R_ln)
    ev = SP.dma(lnbb[:, :], ln_b.partition_broadcast(128), dsem_of(R_ln), pwrites=[R_ln])
    junkP = [PSB[4], PSB[5]]

    def b_vars(n):
        rows = 128 if n < NB else NS
        so = 16 + (n % 2) * 8
        return rows, n * 128, 2 * (n % 2), [sml[0:rows, so + i:so + i + 1] for i in range(8)], R_st[n % 2], gxs[n % 2], R_gxs[n % 2]

    def b_front(n):
        rows, c0, pb, (s1a, s1b, s2, msum, mean, msq, var, rstd), rst, gx, rgx = b_vars(n)
        for k in range(KC):
            for c in range(2):
                PE.op(lambda k=k, c=c: mm(PSB[pb + c][0:rows, :], H[:, k, c0:c0 + rows], WVB[:, k, c * 512:(c + 1) * 512],
                                          start=(k == 0), stop=(k == KC - 1)),
                      reads=[R_H[n], R_WVB[k]], writes=[PSR[pb + c]] if k == 0 else [], pwrites=[] if k == 0 else [PSR[pb + c]],
                      sig=(k == KC - 1))
        ACT.op(lambda: nc.scalar.activation(out=gx[0:rows, 0:512], in_=PSB[pb][0:rows, :], func=AF.Gelu_apprx_tanh, accum_out=s1a),
               reads=[PSR[pb]], writes=[rgx, rst])
        ACT.op(lambda: nc.scalar.activation(out=gx[0:rows, 512:1024], in_=PSB[pb + 1][0:rows, :], func=AF.Gelu_apprx_tanh, accum_out=s1b),
               reads=[PSR[pb + 1]], pwrites=[rgx, rst])
        for c in range(2):
            sq = s2 if c == 0 else msq
            ACT.op(lambda c=c, sq=sq: nc.scalar.activation(out=junkP[c][0:rows, :], in_=gx[0:rows, c * 512:(c + 1) * 512], func=AF.Square,
                                                           accum_out=sq),
                   reads=[rgx], writes=[PSR[4 + c]], pwrites=[rst])
        DVE.op(lambda: v.tensor_tensor(out=msum, in0=s1a, in1=s1b, op=ALU.add), reads=[rst], pwrites=[rst])
        DVE.op(lambda: v.tensor_scalar(out=mean, in0=msum, scalar1=1.0 / 1024, scalar2=None, op0=ALU.mult), reads=[rst], pwrites=[rst])
        DVE.op(lambda: v.tensor_tensor(out=s2, in0=s2, in1=msq, op=ALU.add), reads=[rst], pwrites=[rst])
        DVE.op(lambda: v.tensor_tensor(out=msq, in0=mean, in1=mean, op=ALU.mult), reads=[rst], pwrites=[rst])
        DVE.op(lambda: v.scalar_tensor_tensor(out=var, in0=s2, scalar=1.0 / 1024, in1=msq, op0=ALU.mult, op1=ALU.subtract),
               reads=[rst], pwrites=[rst])

    def b_sqrt(n):
        rows, c0, pb, (s1a, s1b, s2, msum, mean, msq, var, rstd), rst, gx, rgx = b_vars(n)
        ACT.op(lambda: nc.scalar.activation(out=var, in_=var, func=AF.Sqrt, bias=LN_EPS), reads=[rst], pwrites=[rst])

    def b_back(n):
        rows, c0, pb, (s1a, s1b, s2, msum, mean, msq, var, rstd), rst, gx, rgx = b_vars(n)
        DVE.op(lambda: v.reciprocal(out=rstd, in_=var), reads=[rst], pwrites=[rst])
        DVE.op(lambda: v.tensor_scalar(out=vt[0:rows, :], in0=gx[0:rows, :], scalar1=mean, scalar2=rstd, op0=ALU.subtract, op1=ALU.mult),
               reads=[rgx, rst], writes=[R_vt])
        POOL.op(lambda: g.tensor_tensor(out=vt[0:rows, :], in0=vt[0:rows, :], in1=lngb[0:rows, :], op=ALU.mult),
                reads=[R_vt, R_ln], writes=[R_vt])
        if n < NB:
            DVE.op(lambda: v.tensor_tensor(out=vn[:, n, :], in0=vt[:, :], in1=lnbb[:, :], op=ALU.add),
                   reads=[R_vt, R_ln], writes=[R_vn[n]])
        else:
            DVE.op(lambda: v.tensor_tensor(out=vt[0:rows, :], in0=vt[0:rows, :], in1=lnbb[0:rows, :], op=ALU.add),
                   reads=[R_vt, R_ln], writes=[R_vt])
            store(SP, gv_smp, vt[0:rows, :], R_vt)
            DVE.op(lambda: v.tensor_copy(out=vn_s[0:rows, :], in_=vt[0:rows, :]), reads=[R_vt], writes=[R_vn[NB]])

    for n0 in range(0, NB + 1, 2):
        grp_b = [n for n in (n0, n0 + 1) if n < NB + 1]
        for n in grp_b:
            b_front(n)
        for n in grp_b:
            b_sqrt(n)
        for n in grp_b:
            b_back(n)

    if stop_after == "B":
        drain(); return nc
    prevB = [R_gx, R_vt, R_junk, R_ln] + prevA
    MC2 = Bump(M_OFF + mark_c, MB - mark_c)
    Usb = [MC2.take([TOWN], BF16) for _ in range(2)]
    GBsb = [MC2.take([TOWN], BF16) for _ in range(2)]
    t1 = [MC2.take([512], F32) for _ in range(2)]
    R_U = [Res("U%d" % i, prev=prevB) for i in range(2)]
    R_GB = [Res("GB%d" % i, prev=prevB) for i in range(2)]
    R_t1 = [Res("t1_%d" % i, prev=prevB) for i in range(2)]
    oTB = view(W_OFF, [8, TOWN], BF16)
    R_oTB = [Res("oTB%d" % i, prev=R_WVB + R_WK + R_WV[0:1]) for i in range(8)]
    mixctr = {"n": 0}

    def mixing(gi):
        p = gi % 2
        for ti in range(5):
            b = next_bank()
            if ti < 4:
                for i in range(4):
                    blk = ti * 4 + i
                    PE.op(lambda blk=blk, i=i, b=b: mm(PSB[b][:, i * 128:(i + 1) * 128], vn[:, blk, gi * 128:(gi + 1) * 128], WT[:, gi, :],
                                                        start=True, stop=True),
                          reads=[R_vn[blk], R_c["WT"]], writes=[PSR[b]] if i == 0 else [], pwrites=[] if i == 0 else [PSR[b]],
                          sig=(i == 3))
                n, c0 = 512, ti * 512
                nb_ = 4
                bs_ap = bsb[:, gi, :].unsqueeze(1).broadcast_to([128, 4, 128])
                pin = PSB[b][:, :].rearrange("p (a t) -> p a t", a=4)
            else:
                PE.op(lambda b=b: mm(PSB[b][:, 0:NS], vn_s[0:NS, gi * 128:(gi + 1) * 128], WT[0:NS, gi, 0:NS], start=True, stop=True),
                      reads=[R_vn[NB], R_c["WT"]], writes=[PSR[b]])
                n, c0 = NS, 2048
                bs_ap = bsb[:, gi, 0:NS]
                pin = PSB[b][:, 0:NS]
            u = mixctr["n"] % 2
            mixctr["n"] += 1
            if ti < 4:
                t1v = t1[u][:, :].rearrange("p (a t) -> p a t", a=4)
            else:
                t1v = t1[u][:, 0:NS]
            DVE.op(lambda: v.tensor_tensor(out=t1v, in0=pin, in1=bs_ap, op=ALU.add), reads=[PSR[b], R_c["bsb"]], writes=[R_t1[u]])
            DVE.op(lambda: v.tensor_tensor(out=t1[u][:, 0:n], in0=t1[u][:, 0:n], in1=Usb[p][:, c0:c0 + n], op=ALU.mult),
                   reads=[R_t1[u], R_U[p]], writes=[R_t1[u]])
            DVE.op(lambda: v.tensor_tensor(out=oTB[:, gi, c0:c0 + n], in0=t1[u][:, 0:n], in1=GBsb[p][:, c0:c0 + n], op=ALU.mult),
                   reads=[R_t1[u], R_GB[p]], writes=[R_oTB[gi]] if ti == 0 else [], pwrites=[] if ti == 0 else [R_oTB[gi]])

    for ci in range(16, 32):
        if ci + 4 < len(allchunks):
            ring_load(ci + 4)
        kind, gi, col = allchunks[ci]
        if kind == "u":
            c_chunk(ci, Usb[gi % 2], R_U[gi % 2])
            if gi >= 1:
                mixing(gi - 1)
        else:
            c_chunk(ci, GBsb[gi % 2], R_GB[gi % 2])
    H2 = Bump(H_OFF + HB // 2, HB // 2)
    KTh0 = H2.take([NTOK_S], BF16)
    Vh0 = H2.take([33, 128], BF16)
    Qh0 = H2.take([TOWN], BF16)
    R_KTh0 = Res("KTh0", prev=R_H)
    R_Vh0 = Res("Vh0", prev=R_H)
    R_Qh0 = Res("Qh0", prev=R_H)

    def e_loads_big(h, KT_t, V_t, Q_t, rK, rV, rQ):
        load(SP, KT_t[:, :], KT_s[h, :, :], rK, extra_reads=[R_KTs])
        for q4 in range(4):
            SP.dma(V_t[:, q4 * 8:(q4 + 1) * 8, :],
                   V_s[q4 * 1024:(q4 + 1) * 1024, h * 128:(h + 1) * 128].rearrange("(b t) c -> t b c", t=128),
                   dsem_of(rV), reads=[R_Vs], writes=[rV] if q4 == 0 else [], pwrites=[] if q4 == 0 else [rV])
        SP.dma(V_t[0:NS, 32, :], V_s[4096:4096 + NS, h * 128:(h + 1) * 128], dsem_of(rV), pwrites=[rV])
        load(SP, Q_t[:, :], QT_s[h, :, :], rQ, extra_reads=[R_QTs])
    e_loads_big(0, KTh0, Vh0, Qh0, R_KTh0, R_Vh0, R_Qh0)
    mixing(7)

    if stop_after == "C2":
        drain(); return nc
    prevC = R_vn + R_stg + [R_gx, R_vt, R_junk, R_ln] + R_U + R_GB + R_t1
    prevH = R_H
    ME = Bump(M_OFF, MB)
    KTh = [KTh0, ME.take([NTOK_S], BF16)]
    Vh = [Vh0, ME.take([33, 128], BF16)]
    Qh = [Qh0, ME.take([TOWN], BF16)]
    GAh1 = H2.take([TOWN], BF16)
    GAh = [GAh1, GAh1]
    KTc = H2.take([PAST], BF16)
    kc = ME.take([16, 128], BF16)
    vc = ME.take([16, 128], BF16)
    Pb = [ME.take([512], BF16) for _ in range(5)]
    ptmp = H2.take([512], BF16)
    ptmp2 = H2.take([512], BF16)
    ptmp3 = H2.take([512], BF16)
    R_ptmp = Res("ptmp", prev=R_H)
    R_ptmp2 = Res("ptmp2", prev=R_H)
    R_ptmp3 = Res("ptmp3", prev=R_H)
    grp = {}
    fin = [ME.take([512], F32) for _ in range(2)]
    pv = [prevH, prevC]
    R_KTh = [R_KTh0, Res("KTh1", prev=prevC)]
    R_Vh = [R_Vh0, Res("Vh1", prev=prevC)]
    R_Qh = [R_Qh0, Res("Qh1", prev=prevC)]
    R_GAh1 = Res("GAh", prev=prevH)
    R_GAh = [R_GAh1, R_GAh1]
    R_KTc = Res("KTc", prev=prevH)
    R_kc, R_vc = Res("kc", prev=prevC), Res("vc", prev=prevC)
    A_all = ME.take([4096], BF16)
    Bh = [ME.take([2048], BF16) for _ in range(2)]
    As = ME.take([2176 + 128], BF16)
    R_Aall = Res("A_all", prev=prevC)
    R_Bh = [Res("Bh%d" % i, prev=prevC) for i in range(2)]
    R_Bh_ms = [Res("Bh_ms%d" % i) for i in range(2)]
    R_As_ms = Res("As_ms")
    R_As = Res("As", prev=prevC)
    R_Aall_ms = Res("A_all_ms", prev=prevC)
    DVE.op(lambda: v.memset(A_all[:, :], 1.0), writes=[R_Aall_ms, R_Aall])
    DVE.op(lambda: v.memset(As[:, :], 1.0), writes=[R_As])
    for hh in range(8):
        SP.dma(A_all[6 * hh + 3:6 * hh + 6, :], NCK_s[:, hh * 49:hh * 49 + 32, :].rearrange("s b t -> s (b t)"), dsem_of(R_Aall),
               reads=[R_NCKs, R_Aall_ms], pwrites=[R_Aall])
    ones_src3 = bass.AP(tensor=ones_b.tensor if hasattr(ones_b, "tensor") else ones_b, offset=0, ap=[[128, 3], [0, 16], [1, 128]])
    R_Pb = [Res("Pb%d" % i, prev=prevC) for i in range(5)]
    R_fin = [Res("fin%d" % i, prev=prevC) for i in range(2)]
    oTA = view(H_OFF, [8, TOWN], BF16)
    R_oTA = [Res("oTA%d" % i, prev=prevH) for i in range(8)]

    wo = [view(RING0 + k * 4096, [D], BF16) for k in range(8)]
    R_wo = [Res("wo%d" % k, prev=R_ring) for k in range(8)]
    for k in range(8):
        R_wo[k].dsem = R_ring[k].dsem
    wo2 = [view(H_OFF + HB // 2 + k * 4096, [D], BF16) for k in range(8)]
    R_wo2 = []
    ds_wo2 = [DSem(nc, "d_wo2_%d" % i) for i in range(2)]

    def wo_load(k):
        load(POOL, wo[k][:, :], w_out[k * 128:(k + 1) * 128, :], R_wo[k])

    def e_loads(h):
        p = h % 2
        if h > 0:
            e_loads_big(h, KTh[p], Vh[p], Qh[p], R_KTh[p], R_Vh[p], R_Qh[p])
        ACT.op(lambda: nc.scalar.memzero(Bh[p][:, :]), writes=[R_Bh_ms[p], R_Bh[p]])
        SP.dma(Bh[p][6 * h:6 * h + 3, :], CQ3_s[:, h * 17:h * 17 + 16, :].rearrange("s b t -> s (b t)"), dsem_of(R_Bh[p]),
               reads=[R_CQ3s, R_Bh_ms[p]], pwrites=[R_Bh[p]])
        SP.dma(Bh[p][6 * h + 3:6 * h + 6, :].rearrange("r (a t) -> r a t", t=128), ones_src3, dsem_of(R_Bh[p]),
               reads=[R_c["ones_b"], R_Bh_ms[p]], pwrites=[R_Bh[p]])

    ones_src1 = bass.AP(tensor=ones_b.tensor if hasattr(ones_b, "tensor") else ones_b, offset=0, ap=[[128, 3], [1, 128]])

    def e_loads2(h):
        load(SP, GAh1[:, :], GA_s[h, :, :], R_GAh1, extra_reads=[R_GAs])
        ACT.op(lambda: nc.scalar.memzero(As[:, 2176:2304]), writes=[R_As_ms, R_As])
        SP.dma(As[3:6, 0:2176], NCK_s[:, h * 49 + 32:h * 49 + 49, :].rearrange("s b t -> s (b t)"), dsem_of(R_As),
               reads=[R_NCKs, R_As_ms], pwrites=[R_As])
        SP.dma(As[0:3, 2176:2304], CQ3_s[:, h * 17 + 16, :], dsem_of(R_As), reads=[R_CQ3s, R_As_ms], pwrites=[R_As])
        SP.dma(As[3:6, 2176:2304], ones_src1, dsem_of(R_As), reads=[R_c["ones_b"], R_As_ms], pwrites=[R_As])
        load(POOL, kc[:, :, :], ck[:, h * 128:(h + 1) * 128].rearrange("(b t) c -> t b c", t=128), R_kc)
        load(POOL, vc[:, :, :], cv[:, h * 128:(h + 1) * 128].rearrange("(b t) c -> t b c", t=128), R_vc)

    units = []
    tiles_e = []
    for h in range(8):
        for T in range(5):
            tid = len(tiles_e)
            tiles_e.append((h, T))
            if T == 3:
                units.append(dict(h=h, T=T, tid=tid, kind="ktc", j=0, first=False, last=False, smp=False))
                units.append(dict(h=h, T=T, tid=tid, kind="ktc", j=1, first=False, last=False, smp=False))
            if T < 4:
                kbs = []
                for jp in range(4 * T + 4):
                    kbs.append(("own", jp))
                    kbs.append(("oth", jp))
            else:
                kbs = [("cacheall", 0), ("new", 16)]
            for i, (kk, jp) in enumerate(kbs):
                units.append(dict(h=h, T=T, tid=tid, kind=kk, j=jp, first=(i == 0), last=(i == len(kbs) - 1), smp=(T == 4)))

    NSB = 5
    NPB = len(Pb)

    def e_S(i):
        u = units[i]
        h, T, p = u["h"], u["T"], u["h"] % 2
        b = i % NSB
        u["sb"] = b
        if u["kind"] == "ktc":
            half = u["j"]
            for q4 in range(8):
                blk = half * 8 + q4
                PE.op(lambda blk=blk, q4=q4: nc.tensor.transpose(
                    PSB[b][:, :].bitcast(BF16)[:, q4 * 128:(q4 + 1) * 128], kc[:, blk, :], ident_b[:, :]),
                    reads=[R_kc, R_c["ident_b"]], writes=[PSR[b]] if q4 == 0 else [], pwrites=[] if q4 == 0 else [PSR[b]],
                    sig=(q4 == 7))
        elif not u["smp"]:
            jlo = max(u["j"], 4 * T)
            off = (jlo - 4 * T) * 128
            n = 512 - off
            kb = (u["j"] if u["kind"] == "own" else 16 + u["j"])
            kcol = kb * 128
            q0 = 4 * T * 128 + off
            u.update(off=off, n=n, kb=kb, rows=128)
            diag = u["j"] >= 4 * T
            PE.op(lambda: mm(PSB[b][:, off:off + n], KTh[p][:, kcol:kcol + 128], Qh[p][:, q0:q0 + n], start=True, stop=False),
                  reads=[R_KTh[p], R_Qh[p]], writes=[PSR[b]], sig=False)
            PE.op(lambda: mm(PSB[b][:, off:off + n], A_all[:, kcol:kcol + 128], Bh[p][:, q0:q0 + n], start=False, stop=not diag),
                  reads=[R_Aall, R_Bh[p]], pwrites=[PSR[b]], sig=not diag)
            if diag:
                mi = 0 if u["kind"] == "own" else 1 + u["j"] % 2
                PE.op(lambda: mm(PSB[b][:, off:off + 128], ident_b[:, :], mneg_b[:, mi, :], start=False, stop=True),
                      reads=[R_c["ident_b"], R_mnb], pwrites=[PSR[b]])
        elif u["kind"] == "cacheall":
            u.update(off=0, n=16 * NS, rows=128)
            for jj in range(16):
                PE.op(lambda jj=jj: mm(PSB[b][:, jj * NS:(jj + 1) * NS], KTc[:, jj * 128:(jj + 1) * 128], Qh[p][:, 2048:2048 + NS],
                                       start=(jj == 0), stop=False, skip_group_check=True),
                      reads=[R_KTc, R_Qh[p]], writes=[PSR[b]] if jj == 0 else [], pwrites=[] if jj == 0 else [PSR[b]], sig=False)
            for jj in range(16):
                kcol = (1 + jj) * 128
                PE.op(lambda jj=jj, kcol=kcol: mm(PSB[b][:, jj * NS:(jj + 1) * NS], As[:, kcol:kcol + 128], As[:, 2176:2176 + NS],
                                                 start=False, stop=(jj == 15), skip_group_check=True),
                      reads=[R_As], pwrites=[PSR[b]], sig=(jj == 15))
        else:
            u.update(off=0, n=NS, rows=NS)
            PE.op(lambda: mm(PSB[b][0:NS, 0:NS], KTh[p][:, 4096:4096 + NS], Qh[p][:, 2048:2048 + NS], start=True, stop=False),
                  reads=[R_KTh[p], R_Qh[p]], writes=[PSR[b]], sig=False)
            PE.op(lambda: mm(PSB[b][0:NS, 0:NS], As[:, 0:NS], As[:, 2176:2176 + NS], start=False, stop=False),
                  reads=[R_As], pwrites=[PSR[b]], sig=False)
            PE.op(lambda: mm(PSB[b][0:NS, 0:NS], ident_b[0:NS, 0:NS], mneg_b[0:NS, 0, 0:NS], start=False, stop=True),
                  reads=[R_c["ident_b"], R_mnb], pwrites=[PSR[b]])

    def e_soft(i):
        u = units[i]
        b = u["sb"]
        if u["kind"] == "ktc":
            half = u["j"]
            ACT.op(lambda: nc.scalar.activation(out=KTc[:, half * 1024:(half + 1) * 1024], in_=PSB[b][:, :].bitcast(BF16), func=AF.Copy),
                   reads=[PSR[b]], writes=[R_KTc] if half == 0 else [], pwrites=[] if half == 0 else [R_KTc])
            return
        off, n, rows = u["off"], u["n"], u["rows"]
        pi = i % NPB
        u["pi"] = pi
        tp = u["tid"] % 2
        ACT.op(lambda: nc.scalar.activation(out=Pb[pi][0:rows, off:off + n], in_=PSB[b][0:rows, off:off + n], func=AF.Exp, scale=SCALE),
               reads=[PSR[b]], writes=[R_Pb[pi]])
        if not u["smp"]:
            if u["kind"] == "oth":
                pa = units[i - 1]["pi"]
                m = u["j"]
                pair_in = dict(in0=Pb[pa][:, off:off + n], in1=Pb[pi][:, off:off + n], op=ALU.add)
                rd = [R_Pb[pa], R_Pb[pi]]
                q = m % 4
                if q == 0:
                    DVE.op(lambda: v.tensor_tensor(out=ptmp[:, off:off + n], **pair_in), reads=rd, writes=[R_ptmp])
                    grp["o0"], grp["n0"] = off, n
                elif q == 1:
                    DVE.op(lambda: v.tensor_tensor(out=ptmp2[:, off:off + n], **pair_in), reads=rd, writes=[R_ptmp2])
                    DVE.op(lambda: v.tensor_tensor(out=ptmp[:, off:off + n], in0=ptmp[:, off:off + n], in1=ptmp2[:, off:off + n], op=ALU.add),
                           reads=[R_ptmp2, R_ptmp], writes=[R_ptmp])
                elif q == 2:
                    DVE.op(lambda: v.tensor_tensor(out=ptmp3[:, off:off + n], **pair_in), reads=rd, writes=[R_ptmp3])
                    grp["o2"], grp["n2"] = off, n
                else:
                    o0, n0, o2, n2 = grp["o0"], grp["n0"], grp["o2"], grp["n2"]
                    DVE.op(lambda: v.tensor_tensor(out=ptmp2[:, off:off + n], **pair_in), reads=rd, writes=[R_ptmp2])
                    DVE.op(lambda: v.tensor_tensor(out=ptmp3[:, off:off + n], in0=ptmp3[:, off:off + n], in1=ptmp2[:, off:off + n], op=ALU.add),
                           reads=[R_ptmp2, R_ptmp3], writes=[R_ptmp3])
                    if m == 3 and o2 == 0:
                        DVE.op(lambda: v.tensor_tensor(out=fin[tp][:, :], in0=ptmp[:, :], in1=ptmp3[:, :], op=ALU.add),
                               reads=[R_ptmp, R_ptmp3], writes=[R_fin[tp]])
                    elif m == 3:
                        ACT.op(lambda: nc.scalar.activation(out=fin[tp][:, :], in_=ptmp[:, :], func=AF.Copy), reads=[R_ptmp], writes=[R_fin[tp]])
                        DVE.op(lambda: v.tensor_tensor(out=fin[tp][:, o2:o2 + n2], in0=fin[tp][:, o2:o2 + n2], in1=ptmp3[:, o2:o2 + n2], op=ALU.add),
                               reads=[R_ptmp3, R_fin[tp]], writes=[R_fin[tp]])
                    else:
                        DVE.op(lambda: v.tensor_tensor(out=ptmp[:, o2:o2 + n2], in0=ptmp[:, o2:o2 + n2], in1=ptmp3[:, o2:o2 + n2], op=ALU.add),
                               reads=[R_ptmp3, R_ptmp], writes=[R_ptmp])
                        DVE.op(lambda: v.tensor_tensor(out=fin[tp][:, o0:o0 + n0], in0=fin[tp][:, o0:o0 + n0], in1=ptmp[:, o0:o0 + n0], op=ALU.add),
                               reads=[R_ptmp, R_fin[tp]], writes=[R_fin[tp]])

    def e_PV(i):
        u = units[i]
        if u["kind"] == "ktc":
            return
        h, T, p = u["h"], u["T"], u["h"] % 2
        off, n, pi, rows = u["off"], u["n"], u["pi"], u["rows"]
        tp = u["tid"] % 2
        ab, db = 5 + tp, 7
        if not u["smp"]:
            PE.op(lambda: mm(PSB[ab][:, off:off + n], Vh[p][:, u["kb"], :], Pb[pi][:, off:off + n], start=u["first"], stop=u["last"]),
                  reads=[R_Vh[p], R_Pb[pi]], writes=[PSR[ab]] if u["first"] else [], pwrites=[] if u["first"] else [PSR[ab]], sig=True)
            if u["last"]:
                def _den(db=db, tp=tp):
                    PE.op(lambda: mm(PSB[db][:, :], ones_f[:, :], fin[tp][:, :], start=True, stop=True),
                          reads=[R_c["ones_f"], R_fin[tp]], writes=[PSR[db]])
                pending.append((i + 4, _den))
        elif u["kind"] == "cacheall":
            for jj in range(16):
                PE.op(lambda jj=jj: mm(PSB[ab][:, 0:NS], vc[:, jj, :], Pb[pi][:, jj * NS:(jj + 1) * NS], start=(jj == 0), stop=False),
                      reads=[R_vc, R_Pb[pi]], writes=[PSR[ab]] if jj == 0 else [], pwrites=[] if jj == 0 else [PSR[ab]], sig=False)
                PE.op(lambda jj=jj: mm(PSB[db][:, 0:NS], ones_b[:, :], Pb[pi][:, jj * NS:(jj + 1) * NS], start=(jj == 0), stop=False),
                      reads=[R_c["ones_b"], R_Pb[pi]], writes=[PSR[db]] if jj == 0 else [], pwrites=[] if jj == 0 else [PSR[db]],
                      sig=(jj == 15))
        else:
            PE.op(lambda: mm(PSB[ab][:, 0:NS], Vh[p][0:NS, 32, :], Pb[pi][0:NS, 0:NS], start=False, stop=True),
                  reads=[R_Vh[p], R_Pb[pi]], pwrites=[PSR[ab]], sig=False)
            PE.op(lambda: mm(PSB[db][:, 0:NS], ones_b[0:NS, :], Pb[pi][0:NS, 0:NS], start=False, stop=True),
                  reads=[R_c["ones_b"], R_Pb[pi]], pwrites=[PSR[db]], sig=True)
        if u["last"]:
            n_t = NS if u["smp"] else 512
            c0 = 2048 if u["smp"] else T * 512
            f = fin[tp]

            def _fin(db=db, ab=ab, tp=tp, n_t=n_t, c0=c0, f=f, h=h, T=T):
                DVE.op(lambda: v.reciprocal(out=f[:, 0:n_t], in_=PSB[db][:, 0:n_t]), reads=[PSR[db]], writes=[R_fin[tp]])
                DVE.op(lambda: v.tensor_tensor(out=f[:, 0:n_t], in0=PSB[ab][:, 0:n_t], in1=f[:, 0:n_t], op=ALU.mult),
                       reads=[PSR[ab], R_fin[tp]], writes=[R_fin[tp]])
                POOL.op(lambda: g.tensor_tensor(out=oTA[:, h, c0:c0 + n_t], in0=f[:, 0:n_t], in1=GAh1[:, c0:c0 + n_t], op=ALU.mult),
                        reads=[R_fin[tp], R_GAh1], writes=[R_oTA[h]] if T == 0 else [], pwrites=[] if T == 0 else [R_oTA[h]])
            if u["smp"]:
                _fin()
            else:
                pending.append((i + 4, _fin))

    pending = []

    def flush(upto):
        keep = []
        for (at, fn) in pending:
            if at <= upto:
                fn()
            else:
                keep.append((at, fn))
        pending[:] = keep

    e_loads(0)
    NU = len(units)
    LA = 4
    seen_heads = set()
    for i in range(NU + LA):
        if i < NU:
            e_S(i)
        jx = i - LA
        if jx >= 0:
            u = units[jx]
            if u["h"] not in seen_heads:
                flush(10 ** 9)
                seen_heads.add(u["h"])
                if u["h"] + 1 < 8:
                    e_loads(u["h"] + 1)
                e_loads2(u["h"])
                if u["h"] >= 1:
                    wo_load(u["h"] - 1)
                if u["h"] == 7:
                    wo_load(7)
                    for k in range(5):
                        R_wo2.append(Res("wo2_%d" % k, prev=[R_KTh[0], R_Vh[0], R_Qh[0]]))
                        R_wo2[k].dsem = ds_wo2[k // 4]
                        load(POOL, wo2[k][:, :], w_out[(8 + k) * 128:(9 + k) * 128, :], R_wo2[k])
            e_soft(jx)
            e_PV(jx)
            flush(jx)
    flush(10 ** 9)

    if stop_after == "E":
        drain(); return nc
    prevE2 = R_KTh[0:1] + R_Vh[0:1] + R_Qh[0:1] + [R_GAh1, R_KTc, R_ptmp, R_ptmp2, R_ptmp3]
    prevEM = R_KTh[1:2] + R_Vh[1:2] + R_Qh[1:2] + [R_kc, R_vc, R_Aall] + R_Bh + [R_mnb, R_As] + R_Pb + R_fin
    R_wo2.extend([Res("wo2_%d" % k, prev=prevE2) for k in range(5, 8)])
    for k in range(5, 8):
        R_wo2[k].dsem = ds_wo2[k // 4]
        load(POOL, wo2[k][:, :], w_out[(8 + k) * 128:(9 + k) * 128, :], R_wo2[k])
    regroup(R_wo2)
    wo_all = wo + wo2
    R_wo_all = R_wo + R_wo2
    MF = Bump(M_OFF, MB)
    xr = [MF.take([D], F32) for _ in range(2)]
    yb = [MF.take([D], F32) for _ in range(2)]
    fgb = MF.take([D], F32)
    junkF = MF.take([D], BF16)
    R_xr = [Res("xr%d" % i, prev=prevEM) for i in range(2)]
    R_yb = [Res("yb%d" % i, prev=prevEM) for i in range(2)]
    R_fg = Res("fg", prev=prevEM)
    R_junkF = Res("junkF", prev=prevEM)
    R_ssF = [Res("ssF%d" % i) for i in range(2)]
    load(SP, fgb[:, :], final_g.partition_broadcast(128), R_fg)

    def f_load(n):
        rows = 128 if n < NB else NS
        src = x_own[n * 128:(n + 1) * 128, :] if n < NB else x_smp
        load(SP, xr[n % 2][0:rows, :], src, R_xr[n % 2])

    f_load(0)
    for n in range(NB + 1):
        rows = 128 if n < NB else NS
        c0 = n * 128
        p = n % 2
        if n + 1 < NB + 1:
            f_load(n + 1)
        for k in range(KC):
            lhs = oTA[:, k, c0:c0 + rows] if k < 8 else oTB[:, k - 8, c0:c0 + rows]
            rl = R_oTA[k] if k < 8 else R_oTB[k - 8]
            for cg in range(4):
                b = 4 * p + cg
                PE.op(lambda lhs=lhs, k=k, cg=cg, b=b: mm(PSB[b][0:rows, :], lhs, wo_all[k][:, cg * 512:(cg + 1) * 512],
                                                          start=(k == 0), stop=(k == KC - 1)),
                      reads=[rl, R_wo_all[k]], writes=[PSR[b]] if k == 0 else [], pwrites=[] if k == 0 else [PSR[b]],
                      sig=(k == KC - 1))
        for cg in range(4):
            b = 4 * p + cg
            DVE.op(lambda cg=cg, b=b: v.tensor_tensor(out=yb[p][0:rows, cg * 512:(cg + 1) * 512], in0=PSB[b][0:rows, :],
                                                      in1=xr[p][0:rows, cg * 512:(cg + 1) * 512], op=ALU.add),
                   reads=[PSR[b], R_xr[p]], writes=[R_yb[p]] if cg == 0 else [], pwrites=[] if cg == 0 else [R_yb[p]])
        so = 40 + p * 4
        ss, sd, rs = [sml[0:rows, so + i:so + i + 1] for i in range(3)]
        ACT.op(lambda: nc.scalar.activation(out=junkF[0:rows, :], in_=yb[p][0:rows, :], func=AF.Square, accum_out=ss),
               reads=[R_yb[p]], writes=[R_junkF, R_ssF[p]])
        ACT.op(lambda: nc.scalar.activation(out=sd, in_=ss, func=AF.Sqrt, scale=1.0 / D, bias=RMS_EPS), reads=[R_ssF[p]], writes=[R_ssF[p]])
        DVE.op(lambda: v.reciprocal(out=rs, in_=sd), reads=[R_ssF[p]], writes=[R_ssF[p]])
        DVE.op(lambda: v.scalar_tensor_tensor(out=yb[p][0:rows, :], in0=yb[p][0:rows, :], scalar=rs, in1=fgb[0:rows, :],
                                              op0=ALU.mult, op1=ALU.mult),
                reads=[R_yb[p], R_ssF[p], R_fg], writes=[R_yb[p]])
        dst = y_own[n * 128:(n + 1) * 128, :] if n < NB else y_smp
        store(SP, dst, yb[p][0:rows, :], R_yb[p])

    drain()
    return nc


_NC_CACHE = {}


def _own_blocks(p):
    gown = [2 * j + ((j + p) % 2) for j in range(NB)]
    goth = [2 * j + 1 - ((j + p) % 2) for j in range(NB)]
    return gown, goth


def make_in_maps(inputs):
    f32 = np.float32
    xp = np.ascontiguousarray(inputs["x_prompt"], dtype=f32)
    xs = np.ascontiguousarray(inputs["x_sample"], dtype=f32)
    cache_k = np.asarray(inputs["cache_k"], dtype=f32)
    cache_v = np.asarray(inputs["cache_v"], dtype=f32)
    cache_lf = np.asarray(inputs["cache_logf"], dtype=f32)
    shared = dict(
        w_in=np.ascontiguousarray(inputs["w_in"][0], dtype=f32), w_out=np.ascontiguousarray(inputs["w_out"][0], dtype=f32),
        norm_g=np.ascontiguousarray(inputs["norm_g"][0], dtype=f32), b_f=np.ascontiguousarray(inputs["b_f"][0], dtype=f32),
        ln_g=np.ascontiguousarray(inputs["ln_g"][0], dtype=f32), ln_b=np.ascontiguousarray(inputs["ln_b"][0], dtype=f32),
        w_s=np.ascontiguousarray(inputs["w_s"][0], dtype=f32), b_s=np.ascontiguousarray(inputs["b_s"][0], dtype=f32),
        final_g=np.ascontiguousarray(inputs["final_g"], dtype=f32))
    in_maps = []
    for c in range(8):
        b, p = c // 2, c % 2
        gown, goth = _own_blocks(p)
        xb = xp[b].reshape(32, 128, D)
        fo = np.zeros((128, 16, 8), f32)
        mno = np.zeros((2, 128, 128), f32)
        for j in range(NB):
            if (j + p) % 2 == 1:
                fo[:, j, :] = 1.0
        for r in range(2):
            mno[r] = 0.0 if (r + p) % 2 == 1 else NEG
        m = dict(shared)
        m.update(
            x_own=np.ascontiguousarray(xb[gown].reshape(NB * 128, D)),
            x_oth=np.ascontiguousarray(xb[goth].reshape(NB * 128, D)),
            x_smp=np.ascontiguousarray(xs[c]),
            ck=np.ascontiguousarray(cache_k[0, c].reshape(PAST, 1024)),
            cv=np.ascontiguousarray(cache_v[0, c].reshape(PAST, 1024)),
            clf=np.ascontiguousarray(cache_lf[0, c]),
            fo=fo.reshape(128, 128), fe=(1.0 - fo).reshape(128, 128), mno=mno)
        in_maps.append(m)
    return in_maps


def assemble(results):
    f32 = np.float32
    y_prompt = np.zeros((4, 32, 128, D), f32)
    k_prompt = np.zeros((1, 4, 32, 128, 8, 128), f32)
    v_prompt = np.zeros((1, 4, 32, 128, 8, 128), f32)
    lf_prompt = np.zeros((1, 4, 32, 128, 8), f32)
    y_sample = np.zeros((8, NS, D), f32)
    k_sample = np.zeros((1, 8, NS, 8, 128), f32)
    v_sample = np.zeros((1, 8, NS, 8, 128), f32)
    lf_sample = np.zeros((1, 8, NS, 8), f32)
    gv_sample = np.zeros((1, 8, NS, 1024), f32)
    for c in range(8):
        r = results[c]
        b, p = c // 2, c % 2
        gown, _ = _own_blocks(p)
        y_prompt[b, gown] = np.asarray(r["y_own"]).reshape(NB, 128, D)
        k_prompt[0, b, gown] = np.asarray(r["k_own"]).reshape(NB, 128, 8, 128)
        v_prompt[0, b, gown] = np.asarray(r["v_own"]).reshape(NB, 128, 8, 128)
        lf_prompt[0, b, gown] = np.asarray(r["lf_own"]).reshape(NB, 128, 8)
        y_sample[c] = np.asarray(r["y_smp"])
        k_sample[0, c] = np.asarray(r["k_smp"]).reshape(NS, 8, 128)
        v_sample[0, c] = np.asarray(r["v_smp"]).reshape(NS, 8, 128)
        lf_sample[0, c] = np.asarray(r["lf_smp"])
        gv_sample[0, c] = np.asarray(r["gv_smp"])
    return (y_prompt.reshape(4, 4096, D), y_sample, k_prompt.reshape(1, 4, 4096, 8, 128),
            v_prompt.reshape(1, 4, 4096, 8, 128), lf_prompt.reshape(1, 4, 4096, 8),
            k_sample, v_sample, lf_sample, gv_sample)


def kernel(**inputs):
    if "nc" not in _NC_CACHE:
        _NC_CACHE["nc"] = build_program()
    nc = _NC_CACHE["nc"]
    in_maps = make_in_maps(inputs)
    res = run_bass_kernel_spmd(nc, in_maps, core_ids=list(range(8)))
    return assemble(res.results)
```

```python
import numpy as np
import concourse.bass as bass
import concourse.mybir as mybir
from concourse.bass_utils import run_bass_kernel_spmd

F32 = mybir.dt.float32
BF16 = mybir.dt.bfloat16
AF = mybir.ActivationFunctionType
ALU = mybir.AluOpType

D = 2048
KC = 16
NB = 16
NS = 16
TOWN = NB * 128 + NS
PAST = 2048
DIN = 7176
OFF_Q, OFF_K, OFF_V, OFF_F, OFF_GA, OFF_U, OFF_VB, OFF_GB = 0, 1024, 2048, 3072, 3080, 4104, 5128, 6152
SCALE = 128 ** -0.5
RMS_EPS = 1e-6
LN_EPS = 1e-5
NEG = -1.0e6
NTOK_S = 2 * NB * 128 + NS


class Res:
    def __init__(self, name, prev=(), excl=False):
        self.name = name
        self.excl = excl
        self.w = {}
        self.r = {}
        self.dsem = None
        for p in prev:
            for d in (p.w, p.r):
                for k, ev in d.items():
                    if k not in self.r or self.r[k][1] < ev[1]:
                        self.r[k] = ev


def _merge(d, ev):
    k = id(ev[0])
    if k not in d or d[k][1] < ev[1]:
        d[k] = ev


class DSem:
    def __init__(self, nc, name):
        self.sem = nc.alloc_semaphore(name)
        self.cnt = 0


class Eng:
    def __init__(self, nc, eng, name, is_pe=False, compute=True):
        self.nc = nc
        self.eng = eng
        self.name = name
        self.is_pe = is_pe
        self.sem = nc.alloc_semaphore("sem_" + name) if compute else None
        self.cnt = 0
        self.seen = {}
        self.last_unsig = False

    def _wait(self, ev, raw=True):
        sem, val = ev
        if sem is self.sem and self.is_pe:
            return
        if self.seen.get(id(sem), 0) >= val:
            return
        self.eng.wait_ge(sem, val)
        self.seen[id(sem)] = val

    def _deps(self, reads, writes, pwrites):
        for r in reads:
            for ev in r.w.values():
                self._wait(ev, raw=True)
            if r.excl:
                for ev in r.r.values():
                    self._wait(ev, raw=False)
        for w in writes:
            for ev in w.w.values():
                self._wait(ev, raw=False)
            for ev in w.r.values():
                self._wait(ev, raw=False)
        for w in pwrites:
            for ev in w.r.values():
                self._wait(ev, raw=False)

    def _update(self, ev, reads, writes, pwrites):
        for w in writes:
            w.w = {id(ev[0]): ev}
            w.r = {}
        for w in pwrites:
            _merge(w.w, ev)
        for r in reads:
            _merge(r.r, ev)

    def op(self, fn, reads=(), writes=(), pwrites=(), sig=True):
        self._deps(reads, writes, pwrites)
        ins = fn()
        if sig:
            ins.then_inc(self.sem, 1)
            self.cnt += 1
            ev = (self.sem, self.cnt)
            self.last_unsig = False
        else:
            ev = (self.sem, self.cnt + 1)
            self.last_unsig = True
        self._update(ev, reads, writes, pwrites)
        return ev

    def dma(self, out, in_, dsem, reads=(), writes=(), pwrites=()):
        self._deps(reads, writes, pwrites)
        self.eng.dma_start(out=out, in_=in_).then_inc(dsem.sem, 16)
        dsem.cnt += 16
        ev = (dsem.sem, dsem.cnt)
        self._update(ev, reads, writes, pwrites)
        return ev


def build_program(debug=False, stop_after=None):
    nc = bass.Bass("TRN2", target_bir_lowering=False)
    all_dsems = []

    def din(name, shape):
        return nc.dram_tensor(name, list(shape), F32, kind="ExternalInput").ap()

    def dout(name, shape):
        return nc.dram_tensor(name, list(shape), F32, kind="ExternalOutput").ap()

    skind = "ExternalOutput" if debug else "Internal"

    def dscr(name, shape, dt=BF16):
        return nc.dram_tensor(name, list(shape), dt, kind=skind).ap()

    x_own = din("x_own", [NB * 128, D])
    x_oth = din("x_oth", [NB * 128, D])
    x_smp = din("x_smp", [NS, D])
    ck = din("ck", [PAST, 1024])
    cv = din("cv", [PAST, 1024])
    clf = din("clf", [PAST, 8])
    w_in = din("w_in", [D, DIN])
    w_out = din("w_out", [D, D])
    norm_g = din("norm_g", [D])
    b_f = din("b_f", [8])
    ln_g = din("ln_g", [1024])
    ln_b = din("ln_b", [1024])
    w_s = din("w_s", [8, 128, 128])
    b_s = din("b_s", [8, 128])
    final_g = din("final_g", [D])
    fo_in = din("fo", [128, 128])
    fe_in = din("fe", [128, 128])
    mno_in = din("mno", [2, 128, 128])

    y_own = dout("y_own", [NB * 128, D])
    y_smp = dout("y_smp", [NS, D])
    k_own = dout("k_own", [NB * 128, 1024])
    v_own = dout("v_own", [NB * 128, 1024])
    lf_own = dout("lf_own", [NB * 128, 8])
    k_smp = dout("k_smp", [NS, 1024])
    v_smp = dout("v_smp", [NS, 1024])
    lf_smp = dout("lf_smp", [NS, 8])
    gv_smp = dout("gv_smp", [NS, 1024])

    KT_s = dscr("KT_s", [8, 128, NTOK_S])
    V_s = dscr("V_s", [NTOK_S, 1024])
    QT_s = dscr("QT_s", [8, 128, TOWN])
    GA_s = dscr("GA_s", [8, 128, TOWN])

    PE = Eng(nc, nc.tensor, "pe", is_pe=True)
    ACT = Eng(nc, nc.scalar, "act")
    DVE = Eng(nc, nc.vector, "dve")
    POOL = Eng(nc, nc.gpsimd, "pool")
    SP = Eng(nc, nc.sync, "sp", compute=False)
    store_events = []

    def dsem_of(res):
        if res.dsem is None:
            res.dsem = DSem(nc, "d_" + res.name)
        if res.dsem not in all_dsems:
            all_dsems.append(res.dsem)
        return res.dsem

    def drain():
        for ds in all_dsems:
            if ds.cnt > 0:
                SP._wait((ds.sem, ds.cnt))
        for e in (PE, ACT, DVE, POOL):
            if e.cnt > 0:
                SP._wait((e.sem, e.cnt))

    def regroup(res_list):
        for r in res_list:
            ds = r.dsem
            r.w = {id(ds.sem): (ds.sem, ds.cnt)}

    def load(q, out, in_, res, extra_reads=()):
        return q.dma(out, in_, dsem_of(res), reads=list(extra_reads), writes=[res])

    def store(q, out, in_, res, dram_res=None, final=True):
        ev = q.dma(out, in_, dsem_of(res), reads=[res], pwrites=[dram_res] if dram_res is not None else [])
        if final:
            store_events.append(ev)
        return ev

    PSB = [nc.alloc_psum_tensor("psb%d" % i, [128, 512], F32) for i in range(8)]
    PSR = [Res("psb%d" % i, excl=True) for i in range(8)]

    def sb(name, shape, dt=F32):
        return nc.alloc_sbuf_tensor(name, list(shape), dt)

    ident_f = sb("ident_f", [128, 128]); ident_b = sb("ident_b", [128, 128], BF16)
    utri_f = sb("utri_f", [128, 128])
    ones_f = sb("ones_f", [128, 128]); ones_b = sb("ones_b", [128, 128], BF16)
    mneg_d = sb("mneg_d", [128, 128])
    mneg_o = sb("mneg_o", [128, 2, 128])
    mneg_b = sb("mneg_b", [128, 3, 128], BF16)
    fo_t = sb("fo_t", [128, 128]); fe_t = sb("fe_t", [128, 128])
    ng16 = sb("ng16", [16, 128]); gcol = sb("gcol", [128, 16])
    bfb = sb("bfb", [128, 8])
    bsb = sb("bsb", [128, 8, 128])
    WT = sb("WT", [128, 8, 128], BF16)
    wf = sb("wf", [128, 16, 8], BF16)
    tz = sb("tz", [128, 33, 8])
    lfc = sb("lfc", [128, 16, 8])
    cum = sb("cum", [128, 49, 8])
    ncum = sb("ncum", [128, 49, 8])
    tmpA = sb("tmpA", [128, 128]); tmpB = sb("tmpB", [128, 128]); tmpC = sb("tmpC", [128, 128])
    tmpD = sb("tmpD", [128, 128]); tmpE = sb("tmpE", [128, 128])
    ones16 = sb("ones16", [128, 16])
    sml = sb("sml", [128, 64])
    R_consts = Res("consts")
    R_tz = [Res("tz%d" % i) for i in range(33)]
    R_cum = Res("cum")

    rem = nc.sbuf_bytes_remaining
    HB = KC * TOWN * 2
    WB = HB
    MB = (rem - HB - WB - 64) // 64 * 64
    assert MB >= 59600, MB
    arena = nc.alloc_sbuf_tensor("arena", [128, (HB + WB + MB) // 4], F32)
    H_OFF, W_OFF, M_OFF = 0, HB, HB + WB

    def view(off, shape, dt):
        es = 4 if dt == F32 else 2
        n = int(np.prod(shape))
        nbytes = n * es
        assert off % 4 == 0 and nbytes % 4 == 0, (off, shape)
        ap = arena[:, off // 4:(off + nbytes) // 4]
        if dt != F32:
            ap = ap.bitcast(dt)
        if len(shape) == 2:
            ap = ap.rearrange("p (a b) -> p a b", a=shape[0])
        elif len(shape) == 3:
            ap = ap.rearrange("p (a b c) -> p a b c", a=shape[0], b=shape[1])
        return ap

    class Bump:
        def __init__(self, base, size):
            self.base, self.size, self.off = base, size, 0

        def take(self, shape, dt):
            es = 4 if dt == F32 else 2
            nbytes = (int(np.prod(shape)) * es + 31) // 32 * 32
            assert self.off + nbytes <= self.size, ("arena overflow", self.off, nbytes, self.size)
            v = view(self.base + self.off, shape, dt)
            self.off += nbytes
            return v

    H = view(H_OFF, [KC, TOWN], BF16)
    R_H = [Res("H%d" % j) for j in range(NB + 1)]

    wsf = view(M_OFF, [8, 128], F32)
    g = nc.gpsimd

    def pool(fn, reads=(), writes=()):
        return POOL.op(fn, reads=reads, writes=writes)

    R_c = {n: Res(n) for n in ["ident_f", "ident_b", "utri", "ones_f", "ones_b", "mneg_d",
                               "ones16", "mneg_o", "fo", "fe", "ng16", "gcol", "bfb", "bsb", "wsf", "WT", "wf", "lfc"]}
    pool(lambda: g.memset(ident_f[:], 1.0), writes=[R_c["ident_f"]])
    pool(lambda: g.affine_select(out=ident_f[:], in_=ident_f[:], pattern=[[-1, 128]], compare_op=ALU.is_equal,
                                 fill=0.0, base=0, channel_multiplier=1),
         reads=[R_c["ident_f"]], writes=[R_c["ident_f"]])
    pool(lambda: g.tensor_copy(out=ident_b[:], in_=ident_f[:]), reads=[R_c["ident_f"]], writes=[R_c["ident_b"]])
    pool(lambda: g.memset(utri_f[:], 1.0), writes=[R_c["utri"]])
    pool(lambda: g.affine_select(out=utri_f[:], in_=utri_f[:], pattern=[[1, 128]], compare_op=ALU.is_ge,
                                 fill=0.0, base=0, channel_multiplier=-1),
         reads=[R_c["utri"]], writes=[R_c["utri"]])
    pool(lambda: g.memset(ones_f[:], 1.0), writes=[R_c["ones_f"]])
    pool(lambda: g.memset(ones_b[:], 1.0), writes=[R_c["ones_b"]])
    pool(lambda: g.memset(ones16[:], 1.0), writes=[R_c["ones16"]])
    pool(lambda: g.memset(cum[:, :, :].rearrange("p j h -> p (j h)"), 0.0), writes=[R_cum])
    pool(lambda: g.memset(mneg_d[:], 0.0), writes=[R_c["mneg_d"]])
    pool(lambda: g.affine_select(out=mneg_d[:], in_=mneg_d[:], pattern=[[1, 128]], compare_op=ALU.is_ge,
                                 fill=NEG, base=0, channel_multiplier=-1),
         reads=[R_c["mneg_d"]], writes=[R_c["mneg_d"]])

    ds_setup = DSem(nc, "d_setup")
    for nm in ["mneg_o", "fo", "fe", "ng16", "bfb", "bsb", "lfc", "wsf"]:
        R_c[nm].dsem = ds_setup
    load(SP, mneg_o[:], mno_in.rearrange("r k q -> k r q"), R_c["mneg_o"])
    load(SP, fo_t[:], fo_in, R_c["fo"])
    load(SP, fe_t[:], fe_in, R_c["fe"])
    load(SP, ng16[:], norm_g.rearrange("(k p) -> k p", p=128), R_c["ng16"])
    load(SP, bfb[:], b_f.partition_broadcast(128), R_c["bfb"])
    load(SP, bsb[:].rearrange("p g t -> p (g t)"), b_s.rearrange("g t -> (g t)").partition_broadcast(128), R_c["bsb"])
    load(SP, lfc[:], clf.rearrange("(j p) h -> p j h", p=128), R_c["lfc"])
    load(SP, wsf[:, :, :], w_s.rearrange("g t s -> t g s"), R_c["wsf"])
    regroup([R_c[nm] for nm in ["mneg_o", "fo", "fe", "ng16", "bfb", "bsb", "lfc", "wsf"]])

    R_mnb = Res("mneg_b")
    pool(lambda: g.tensor_copy(out=mneg_b[:, 0, :], in_=mneg_d[:, :]), reads=[R_c["mneg_d"]], writes=[R_mnb])
    pool(lambda: g.tensor_copy(out=mneg_b[:, 1:3, :], in_=mneg_o[:, :, :]), reads=[R_c["mneg_o"], R_mnb], writes=[R_mnb])
    WK = view(W_OFF, [KC, 1024], BF16)
    WV = view(W_OFF + 32768, [KC, 1024], BF16)
    R_WK = [Res("WK%d" % k) for k in range(KC)]
    R_WV = [Res("WV%d" % k) for k in range(KC)]
    ds_wk = [DSem(nc, "d_wk%d" % i) for i in range(4)]
    ds_wv = [DSem(nc, "d_wv%d" % i) for i in range(4)]
    for k in range(KC):
        R_WK[k].dsem = ds_wk[k // 4]
        R_WV[k].dsem = ds_wv[k // 4]
    for k in range(KC):
        load(POOL, WK[:, k, :], w_in[k * 128:(k + 1) * 128, OFF_K:OFF_K + 1024], R_WK[k])
    regroup(R_WK)
    load(POOL, wf[:], w_in.rearrange("(k p) c -> p k c", p=128)[:, :, OFF_F:OFF_F + 8], R_c["wf"])
    for k in range(KC):
        load(POOL, WV[:, k, :], w_in[k * 128:(k + 1) * 128, OFF_V:OFF_V + 1024], R_WV[k])
    regroup(R_WV)

    PE.op(lambda: nc.tensor.transpose(PSB[7][:, 0:16], ng16[:, :], ident_f[0:16, 0:16]),
          reads=[R_c["ng16"], R_c["ident_f"]], writes=[PSR[7]])
    DVE.op(lambda: nc.vector.tensor_copy(out=gcol[:], in_=PSB[7][:, 0:16]), reads=[PSR[7]], writes=[R_c["gcol"]])
    for half in range(2):
        for gg in range(4):
            gi = half * 4 + gg
            PE.op(lambda gi=gi, gg=gg, half=half: nc.tensor.transpose(
                PSB[5 + half][:, gg * 128:(gg + 1) * 128], wsf[:, gi, :], ident_f[:, :]),
                reads=[R_c["wsf"], R_c["ident_f"]], pwrites=[PSR[5 + half]], sig=(gg == 3))
        DVE.op(lambda half=half: nc.vector.tensor_tensor(
            out=WT[:, half * 4:(half + 1) * 4, :],
            in0=PSB[5 + half][:, :].rearrange("p (g t) -> p g t", g=4),
            in1=utri_f[:, :].unsqueeze(1).broadcast_to([128, 4, 128]), op=ALU.mult),
            reads=[PSR[5 + half], R_c["utri"]], pwrites=[R_c["WT"]])

    if stop_after == "setup":
        drain(); return nc
    MA = Bump(M_OFF, MB)
    xbuf = [MA.take([D], F32) for _ in range(2)]
    xn = [MA.take([D], BF16) for _ in range(2)]
    hTb = [MA.take([KC, 128], BF16) for _ in range(2)]
    Kf32 = MA.take([1024], F32)
    Vf32 = MA.take([1024], F32)
    Kb = [MA.take([1024], BF16) for _ in range(2)]
    Vb = [MA.take([1024], BF16) for _ in range(2)]
    KTsb = [MA.take([8, 128], BF16) for _ in range(2)]
    R_xbuf = [Res("xbuf%d" % i, prev=[R_c["wsf"]]) for i in range(2)]
    R_xn = [Res("xn%d" % i) for i in range(2)]
    R_hTb = [Res("hTb%d" % i) for i in range(2)]
    R_Kf32, R_Vf32 = Res("Kf32"), Res("Vf32")
    R_Kb = [Res("Kb%d" % i) for i in range(2)]
    R_Vb = [Res("Vb%d" % i) for i in range(2)]
    R_KTsb = [Res("KTsb%d" % i) for i in range(2)]
    R_ss = [Res("ss%d" % i) for i in range(2)]
    R_KTs = Res("KT_s"); R_Vs = Res("V_s"); R_QTs = Res("QT_s"); R_GAs = Res("GA_s")

    blocks = [("own", j) for j in range(NB)] + [("smp", 0)] + [("oth", j) for j in range(NB)]
    NBLK = len(blocks)
    psT = [PSB[0][:, :].bitcast(BF16).rearrange("p (k t) -> p k t", k=8),
           PSB[1][:, :].bitcast(BF16).rearrange("p (k t) -> p k t", k=8)]
    psKT = PSB[6][:, :].bitcast(BF16).rearrange("p (h t) -> p h t", h=8)

    def binfo(n):
        kind, j = blocks[n]
        rows = NS if kind == "smp" else 128
        if kind == "own":
            xsrc = x_own[j * 128:(j + 1) * 128, :]
            tcol = j * 128
            tzi = j
        elif kind == "oth":
            xsrc = x_oth[j * 128:(j + 1) * 128, :]
            tcol = 2048 + j * 128
            tzi = 16 + j
        else:
            xsrc = x_smp
            tcol = 4096
            tzi = 32
        return kind, j, rows, xsrc, tcol, tzi

    def hT_of(n):
        kind, j, rows, _, _, _ = binfo(n)
        if kind == "own":
            return H[:, :, j * 128:(j + 1) * 128], R_H[j]
        if kind == "smp":
            return H[:, :, 2048:2048 + NS], R_H[NB]
        return hTb[n % 2][:, :, :], R_hTb[n % 2]

    def a_load(n):
        kind, j, rows, xsrc, _, _ = binfo(n)
        load(SP, xbuf[n % 2][0:rows, :], xsrc, R_xbuf[n % 2])

    def a_norm(n):
        kind, j, rows, _, _, _ = binfo(n)
        p = n % 2
        ss = sml[0:rows, p * 4:p * 4 + 1]
        sd = sml[0:rows, p * 4 + 1:p * 4 + 2]
        rs = sml[0:rows, p * 4 + 2:p * 4 + 3]
        ACT.op(lambda: nc.scalar.activation(out=xn[p][0:rows, :], in_=xbuf[p][0:rows, :], func=AF.Square,
                                            accum_out=ss), reads=[R_xbuf[p]], writes=[R_xn[p], R_ss[p]])
        ACT.op(lambda: nc.scalar.activation(out=sd, in_=ss, func=AF.Sqrt, scale=1.0 / D, bias=RMS_EPS),
               reads=[R_ss[p]], writes=[R_ss[p]])
        DVE.op(lambda: nc.vector.reciprocal(out=rs, in_=sd), reads=[R_ss[p]], writes=[R_ss[p]])
        ACT.op(lambda: nc.scalar.activation(out=xn[p][0:rows, :], in_=xbuf[p][0:rows, :], func=AF.Copy, scale=rs),
               reads=[R_xbuf[p], R_ss[p]], writes=[R_xn[p]])

    def a_transp(n):
        kind, j, rows, _, _, _ = binfo(n)
        p = n % 2
        for k in range(KC):
            PE.op(lambda k=k: nc.tensor.transpose(psT[k // 8][:, k % 8, 0:rows], xn[p][0:rows, k * 128:(k + 1) * 128],
                                                  ident_b[0:rows, 0:rows]),
                  reads=[R_xn[p], R_c["ident_b"]], pwrites=[PSR[0], PSR[1]] if k == 0 else [], sig=(k == KC - 1))
        dst, rdst = hT_of(n)
        for half in range(2):
            DVE.op(lambda half=half: nc.vector.tensor_tensor(
                out=dst[:, half * 8:(half + 1) * 8, :], in0=psT[half][:, :, 0:rows],
                in1=gcol[:, half * 8:(half + 1) * 8].unsqueeze(2).broadcast_to([128, 8, rows]), op=ALU.mult),
                reads=[PSR[half], R_c["gcol"]], pwrites=[rdst])

    def a_Kmm(n):
        kind, j, rows, _, _, _ = binfo(n)
        hT, rh = hT_of(n)
        for k in range(KC):
            for c in range(2):
                PE.op(lambda k=k, c=c: nc.tensor.matmul(PSB[2 + c][0:rows, :], hT[:, k, :], WK[:, k, c * 512:(c + 1) * 512],
                                                        start=(k == 0), stop=(k == KC - 1)),
                      reads=[rh, R_WK[k]], writes=[PSR[2 + c]] if k == 0 else [], pwrites=[] if k == 0 else [PSR[2 + c]],
                      sig=(k == KC - 1))

    def a_Vmm(n):
        kind, j, rows, _, _, _ = binfo(n)
        hT, rh = hT_of(n)
        for k in range(KC):
            for c in range(2):
                PE.op(lambda k=k, c=c: nc.tensor.matmul(PSB[4 + c][0:rows, :], hT[:, k, :], WV[:, k, c * 512:(c + 1) * 512],
                                                        start=(k == 0), stop=(k == KC - 1)),
                      reads=[rh, R_WV[k]], writes=[PSR[4 + c]] if k == 0 else [], pwrites=[] if k == 0 else [PSR[4 + c]],
                      sig=(k == KC - 1))
            PE.op(lambda k=k: nc.tensor.matmul(PSB[7][0:rows, 0:8], hT[:, k, :], wf[:, k, :],
                                               start=(k == 0), stop=(k == KC - 1)),
                  reads=[rh, R_c["wf"]], writes=[PSR[7]] if k == 0 else [], pwrites=[] if k == 0 else [PSR[7]],
                  sig=(k == KC - 1))

    def a_Kepi(n):
        kind, j, rows, _, _, _ = binfo(n)
        p = n % 2
        if kind != "oth":
            for c in range(2):
                ACT.op(lambda c=c: nc.scalar.activation(out=Kf32[0:rows, c * 512:(c + 1) * 512], in_=PSB[2 + c][0:rows, :],
                                                        func=AF.Copy),
                       reads=[PSR[2 + c]], writes=[R_Kf32] if c == 0 else [], pwrites=[] if c == 0 else [R_Kf32])
            POOL.op(lambda: g.tensor_copy(out=Kb[p][0:rows, :], in_=Kf32[0:rows, :]), reads=[R_Kf32], writes=[R_Kb[p]])
        else:
            for c in range(2):
                DVE.op(lambda c=c: nc.vector.tensor_copy(out=Kb[p][0:rows, c * 512:(c + 1) * 512], in_=PSB[2 + c][0:rows, :]),
                       reads=[PSR[2 + c]], writes=[R_Kb[p]] if c == 0 else [], pwrites=[] if c == 0 else [R_Kb[p]])

    def a_KTtr(n):
        kind, j, rows, _, _, _ = binfo(n)
        p = n % 2
        for h in range(8):
            PE.op(lambda h=h: nc.tensor.transpose(psKT[:, h, 0:rows], Kb[p][0:rows, h * 128:(h + 1) * 128],
                                                  ident_b[0:rows, 0:rows]),
                  reads=[R_Kb[p], R_c["ident_b"]], writes=[PSR[6]] if h == 0 else [], pwrites=[] if h == 0 else [PSR[6]],
                  sig=(h == 7))

    def a_Vepi(n):
        kind, j, rows, _, _, tzi = binfo(n)
        p = n % 2
        if kind != "oth":
            for c in range(2):
                ACT.op(lambda c=c: nc.scalar.activation(out=Vf32[0:rows, c * 512:(c + 1) * 512], in_=PSB[4 + c][0:rows, :],
                                                        func=AF.Copy),
                       reads=[PSR[4 + c]], writes=[R_Vf32] if c == 0 else [], pwrites=[] if c == 0 else [R_Vf32])
            POOL.op(lambda: g.tensor_copy(out=Vb[p][0:rows, :], in_=Vf32[0:rows, :]), reads=[R_Vf32], writes=[R_Vb[p]])
        else:
            for c in range(2):
                DVE.op(lambda c=c: nc.vector.tensor_copy(out=Vb[p][0:rows, c * 512:(c + 1) * 512], in_=PSB[4 + c][0:rows, :]),
                       reads=[PSR[4 + c]], writes=[R_Vb[p]] if c == 0 else [], pwrites=[] if c == 0 else [R_Vb[p]])
        DVE.op(lambda: nc.vector.tensor_tensor(out=tz[0:rows, tzi, :], in0=PSB[7][0:rows, 0:8], in1=bfb[0:rows, :], op=ALU.add),
               reads=[PSR[7], R_c["bfb"]], writes=[R_tz[tzi]])
        DVE.op(lambda: nc.vector.tensor_copy(out=KTsb[p][:, :, 0:rows], in_=psKT[:, :, 0:rows]),
               reads=[PSR[6]], writes=[R_KTsb[p]])

    def a_stores(n):
        kind, j, rows, _, tcol, _ = binfo(n)
        p = n % 2
        if kind == "own":
            store(SP, k_own[j * 128:(j + 1) * 128, :], Kf32[:, :], R_Kf32)
            store(SP, v_own[j * 128:(j + 1) * 128, :], Vf32[:, :], R_Vf32)
        elif kind == "smp":
            store(SP, k_smp, Kf32[0:rows, :], R_Kf32)
            store(SP, v_smp, Vf32[0:rows, :], R_Vf32)
        store(SP, KT_s.rearrange("h d t -> d h t")[:, :, tcol:tcol + rows], KTsb[p][:, :, 0:rows], R_KTsb[p], R_KTs, final=False)
        store(SP, V_s[tcol:tcol + rows, :], Vb[p][0:rows, :], R_Vb[p], R_Vs, final=False)

    a_load(0); a_load(1)
    a_norm(0); a_transp(0)
    for n in range(NBLK):
        if n + 2 < NBLK:
            a_load(n + 2)
        if n + 1 < NBLK:
            a_norm(n + 1)
        a_Kmm(n)
        if n + 1 < NBLK:
            a_transp(n + 1)
        a_Kepi(n)
        a_Vmm(n)
        a_KTtr(n)
        a_Vepi(n)
        a_stores(n)

    if stop_after == "A":
        drain(); return nc
    v = nc.vector
    tzf = tz[:, :, :].rearrange("p j h -> p (j h)")
    ACT.op(lambda: nc.scalar.activation(out=tzf[:, 0:256], in_=tzf[:, 0:256], func=AF.Exp, scale=-1.0),
           reads=R_tz[0:16] + R_tz[17:33], writes=[R_cum])
    ACT.op(lambda: nc.scalar.activation(out=tzf[0:NS, 256:264], in_=tzf[0:NS, 256:264], func=AF.Exp, scale=-1.0),
           reads=[R_tz[32]], pwrites=[R_cum])
    ACT.op(lambda: nc.scalar.activation(out=tzf[:, 0:256], in_=tzf[:, 0:256], func=AF.Ln, bias=1.0),
           reads=[R_cum], pwrites=[R_cum])
    ACT.op(lambda: nc.scalar.activation(out=tzf[0:NS, 256:264], in_=tzf[0:NS, 256:264], func=AF.Ln, bias=1.0),
           reads=[R_cum], pwrites=[R_cum])
    DVE.op(lambda: v.tensor_scalar(out=tzf[:, 0:256], in0=tzf[:, 0:256], scalar1=-1.0, scalar2=None, op0=ALU.mult),
           reads=[R_cum], pwrites=[R_cum])
    DVE.op(lambda: v.tensor_scalar(out=tzf[0:NS, 256:264], in0=tzf[0:NS, 256:264], scalar1=-1.0, scalar2=None, op0=ALU.mult),
           reads=[R_cum], pwrites=[R_cum])
    R_lf = Res("lf")
    store(SP, lf_own.rearrange("(j p) h -> p j h", p=128), tz[:, 0:16, :], R_cum)
    store(SP, lf_smp, tz[0:NS, 32, :], R_cum)
    cumv = {}
    prevA_ext = []
    prevA = R_xbuf + R_xn + R_hTb + [R_Kf32, R_Vf32] + R_Kb + R_Vb + R_KTsb
    mm = nc.tensor.matmul
    def cum_stage2():
        LO = tzf[:, 0:128]
        LT = tzf[:, 128:256]
        LS = tzf[0:NS, 256:264]
        lfcf = lfc[:, :, :].rearrange("p j h -> p (j h)")
        cumf = cum[:, :, :].rearrange("p j h -> p (j h)")
        mm = nc.tensor.matmul
        PE.op(lambda: mm(PSB[7][:, 0:128], utri_f[:, :], LO, start=True, stop=True), reads=[R_cum, R_c["utri"]], writes=[PSR[7]], sig=False)
        PE.op(lambda: mm(PSB[7][:, 128:256], utri_f[:, :], LT, start=True, stop=True), pwrites=[PSR[7]], sig=False)
        PE.op(lambda: mm(PSB[7][:, 256:384], ones_f[:, :], LO, start=True, stop=True), reads=[R_c["ones_f"]], pwrites=[PSR[7]], sig=False)
        PE.op(lambda: mm(PSB[7][:, 384:512], ones_f[:, :], LT, start=True, stop=True), pwrites=[PSR[7]])
        PE.op(lambda: mm(PSB[6][:, 0:128], utri_f[:, :], lfcf, start=True, stop=True), reads=[R_c["lfc"]], writes=[PSR[6]], sig=False)
        PE.op(lambda: mm(PSB[6][:, 128:256], ones_f[:, :], lfcf, start=True, stop=True), pwrites=[PSR[6]], sig=False)
        PE.op(lambda: mm(PSB[6][0:NS, 256:264], utri_f[0:NS, 0:NS], LS, start=True, stop=True), pwrites=[PSR[6]])
        cumv.update(LO=LO, LT=LT, LS=LS, lfcf=lfcf, cumf=cumf)

    def cum_stage3():
        cumf = cumv['cumf']
        R_t = Res("cumtmp")
        DVE.op(lambda: v.tensor_copy(out=tmpC[:, :], in_=PSB[7][:, 384:512]), reads=[PSR[7]], writes=[R_t])
        DVE.op(lambda: v.tensor_tensor(out=tmpA[:, :], in0=PSB[7][:, 256:384], in1=tmpC[:, :], op=ALU.add), reads=[R_t], pwrites=[R_t])
        tA = tmpA[:, :].rearrange("p (j h) -> p h j", h=8)
        tB = tmpB[:, :].rearrange("p (j h) -> p h j", h=8)
        for h in range(8):
            DVE.op(lambda h=h: v.tensor_tensor_scan(out=tB[:, h, :], data0=ones16[:, :], data1=tA[:, h, :], initial=0.0,
                                                    op0=ALU.mult, op1=ALU.add), reads=[R_t, R_c["ones16"]], pwrites=[R_t])
        DVE.op(lambda: v.tensor_tensor(out=tmpB[:, :], in0=tmpB[:, :], in1=tmpA[:, :], op=ALU.subtract), reads=[R_t], pwrites=[R_t])
        DVE.op(lambda: v.tensor_tensor(out=tmpD[:, :], in0=tmpC[:, :], in1=fo_t[:, :], op=ALU.mult), reads=[R_t, R_c["fo"]], pwrites=[R_t])
        DVE.op(lambda: v.tensor_tensor(out=tmpD[:, :], in0=tmpD[:, :], in1=tmpB[:, :], op=ALU.add), reads=[R_t], pwrites=[R_t])
        DVE.op(lambda: v.tensor_tensor(out=cumf[:, 0:128], in0=PSB[7][:, 0:128], in1=tmpD[:, :], op=ALU.add), reads=[R_t], pwrites=[R_t])
        DVE.op(lambda: v.tensor_tensor(out=tmpE[:, :], in0=PSB[7][:, 256:384], in1=fe_t[:, :], op=ALU.mult), reads=[R_c["fe"]], pwrites=[R_t])
        DVE.op(lambda: v.tensor_tensor(out=tmpE[:, :], in0=tmpE[:, :], in1=tmpB[:, :], op=ALU.add), reads=[R_t], pwrites=[R_t])
        DVE.op(lambda: v.tensor_tensor(out=cumf[:, 128:256], in0=PSB[7][:, 128:256], in1=tmpE[:, :], op=ALU.add), reads=[R_t], pwrites=[R_t])
        DVE.op(lambda: v.tensor_copy(out=tmpA[:, :], in_=PSB[6][:, 128:256]), reads=[PSR[6], R_t], pwrites=[R_t])
        for h in range(8):
            DVE.op(lambda h=h: v.tensor_tensor_scan(out=tB[:, h, :], data0=ones16[:, :], data1=tA[:, h, :], initial=0.0,
                                                    op0=ALU.mult, op1=ALU.add), reads=[R_t], pwrites=[R_t])
        DVE.op(lambda: v.tensor_tensor(out=tmpC[:, :], in0=tmpB[:, :], in1=tmpA[:, :], op=ALU.subtract), reads=[R_t], pwrites=[R_t])
        DVE.op(lambda: v.tensor_tensor(out=cumf[:, 33 * 8:49 * 8], in0=PSB[6][:, 0:128], in1=tmpC[:, :], op=ALU.add), reads=[R_t], pwrites=[R_t])
        DVE.op(lambda: v.tensor_tensor(out=cumf[0:NS, 256:264], in0=PSB[6][0:NS, 256:264], in1=tmpB[0:NS, 120:128], op=ALU.add),
               reads=[R_t], pwrites=[R_t])
        R_cum2 = Res("cum2")
        DVE.op(lambda: v.tensor_scalar(out=ncum[:, :, :].rearrange("p j h -> p (j h)"), in0=cumf[:, :], scalar1=-1.0 / SCALE,
                                       scalar2=None, op0=ALU.mult), reads=[R_t], writes=[R_cum2])
        MCUM = Bump(M_OFF + MB - 13312, 13312)
        xk = MCUM.take([392], F32); xq = MCUM.take([136], F32)
        xr1 = MCUM.take([392], F32); xf = MCUM.take([392], F32)
        spl = [MCUM.take([528], BF16) for i in range(3)]
        rows6 = MCUM.take([3, 6, 128], BF16)
        R_spl = Res("spl", prev=prevA)
        DVE.op(lambda: v.tensor_copy(out=xk[:, :].rearrange("p (h b) -> p b h", h=8), in_=ncum[:, :, :]), reads=[R_cum2], writes=[R_spl])
        DVE.op(lambda: v.memset(xq[:, :], 0.0), pwrites=[R_spl])
        xq3 = xq[:, :].rearrange("p (h b) -> p b h", h=8)
        DVE.op(lambda: v.tensor_scalar(out=xq3[:, 0:16, :], in0=cum[:, 0:16, :], scalar1=1.0 / SCALE, scalar2=None, op0=ALU.mult),
               reads=[R_t, R_spl], pwrites=[R_spl])
        DVE.op(lambda: v.tensor_scalar(out=xq3[0:NS, 16, :], in0=cum[0:NS, 32, :], scalar1=1.0 / SCALE, scalar2=None, op0=ALU.mult),
               reads=[R_t, R_spl], pwrites=[R_spl])
        for (src, c0, n) in ((xk, 0, 392), (xq, 392, 136)):
            cur = src
            for si in range(3):
                DVE.op(lambda cur=cur, si=si: v.tensor_copy(out=spl[si][:, c0:c0 + n], in_=cur[:, 0:n]), reads=[R_spl], pwrites=[R_spl])
                if si < 2:
                    DVE.op(lambda si=si: v.tensor_copy(out=xf[:, 0:n], in_=spl[si][:, c0:c0 + n]), reads=[R_spl], pwrites=[R_spl])
                    DVE.op(lambda cur=cur: v.tensor_tensor(out=xr1[:, 0:n], in0=cur[:, 0:n], in1=xf[:, 0:n], op=ALU.subtract),
                           reads=[R_spl], pwrites=[R_spl])
                    cur = xr1
        cumv.update(R_spl=R_spl, spl=spl, rows6=rows6, R_cum2=R_cum2)
        prevA_ext.extend([R_spl])

    def cum_stage4():
        R_spl, spl, rows6 = cumv['R_spl'], cumv['spl'], cumv['rows6']
        R_rows = Res("rows", prev=prevA)
        chunks_t = [(0, 128), (128, 128), (256, 128), (384, 8), (392, 128), (520, 8)]
        NCK_s = dscr("NCK_s", [3, 392, 128]); CQ3_s = dscr("CQ3_s", [3, 136, 128])
        R_NCKs, R_CQ3s = Res("NCK_s"), Res("CQ3_s")
        for si in range(3):
            pbank = PSB[5 + si][:, :].bitcast(BF16)
            for ci, (c0, n) in enumerate(chunks_t):
                PE.op(lambda si=si, ci=ci, c0=c0, n=n, pbank=pbank: nc.tensor.transpose(pbank[0:n, ci * 128:(ci + 1) * 128], spl[si][:, c0:c0 + n], ident_b[:, :]),
                      reads=[R_spl, R_c["ident_b"]], writes=[PSR[5 + si]] if ci == 0 else [], pwrites=[] if ci == 0 else [PSR[5 + si]],
                      sig=(ci == len(chunks_t) - 1))
            for ci, (c0, n) in enumerate(chunks_t):
                DVE.op(lambda si=si, ci=ci, n=n, pbank=pbank: v.tensor_copy(out=rows6[0:n, si, ci, :], in_=pbank[0:n, ci * 128:(ci + 1) * 128]),
                       reads=[PSR[5 + si]], pwrites=[R_rows])
        first = True
        for si in range(3):
            for ci, (c0, n) in enumerate(chunks_t):
                if c0 < 392:
                    dst, rd = NCK_s[si, c0:c0 + n, :], R_NCKs
                else:
                    dst, rd = CQ3_s[si, c0 - 392:c0 - 392 + n, :], R_CQ3s
                SP.dma(dst, rows6[0:n, si, ci, :], dsem_of(R_rows), reads=[R_rows], pwrites=[rd])

        cumv.update(R_NCKs=R_NCKs, R_CQ3s=R_CQ3s, NCK_s=NCK_s, CQ3_s=CQ3_s)
        prevA_ext.extend([R_rows])

    RING0 = W_OFF + HB // 2
    ring = [view(RING0 + s * 4096, [KC, 128], BF16) for s in range(8)]
    R_ring = [Res("ring%d" % s, prev=R_WV) for s in range(8)]
    WVB = view(W_OFF, [KC, 1024], BF16)
    R_WVB = [Res("WVB%d" % k, prev=R_WK) for k in range(KC)]
    for k in range(KC):
        R_WVB[k].dsem = ds_wk[k // 4]
    w_in_r = w_in.rearrange("(k p) c -> p k c", p=128)

    chunksC1 = [("q", h, OFF_Q + h * 128) for h in range(8)] + [("ga", h, OFF_GA + h * 128) for h in range(8)]
    chunksC2 = []
    for gi in range(8):
        chunksC2.append(("u", gi, OFF_U + gi * 128))
        chunksC2.append(("gb", gi, OFF_GB + gi * 128))
    allchunks = chunksC1 + chunksC2
    ring_state = {"next": 0}

    def ring_load(ci):
        kind, idx, col = allchunks[ci]
        s = ci % 8
        load(POOL, ring[s][:, :, :], w_in_r[:, :, col:col + 128], R_ring[s])

    tiles = []
    for (c0_, n_) in [(0, 416), (416, 416), (832, 416), (1248, 416), (1664, 400)]:
        blks = sorted(set(min(c // 128, NB) for c in range(c0_, c0_ + n_, 16)))
        tiles.append((c0_, n_, [R_H[b_] for b_ in blks]))
    bank_ctr = {"n": 0}

    def next_bank(lo=0, hi=6):
        b = lo + bank_ctr["n"] % (hi - lo)
        bank_ctr["n"] += 1
        return b

    MC = Bump(M_OFF, MB)
    vn = MC.take([NB, 1024], BF16)
    vn_s = MC.take([1024], BF16)
    R_vn = [Res("vn%d" % i, prev=prevA) for i in range(NB + 1)]
    stg = [MC.take([512], BF16) for _ in range(4)]
    R_stg = [Res("stg%d" % i, prev=prevA) for i in range(4)]
    mark_c = MC.off

    def c_chunk(ci, dst_sb=None, r_dst=None):
        kind, idx, col = allchunks[ci]
        s = ci % 8
        for ti, (c0, n, rh) in enumerate(tiles):
            b = next_bank()
            for k in range(KC):
                PE.op(lambda k=k, b=b, c0=c0, n=n: mm(PSB[b][:, 0:n], ring[s][:, k, :], H[:, k, c0:c0 + n],
                                                       start=(k == 0), stop=(k == KC - 1)),
                      reads=rh + [R_ring[s]], writes=[PSR[b]] if k == 0 else [], pwrites=[] if k == 0 else [PSR[b]],
                      sig=(k == KC - 1))
            func = {"q": AF.Copy, "ga": AF.Silu, "u": AF.Gelu_apprx_tanh, "gb": AF.Silu}[kind]
            if kind in ("q", "ga"):
                u = c_chunk.ctr % 4
                c_chunk.ctr += 1
                ACT.op(lambda b=b, n=n, u=u: nc.scalar.activation(out=stg[u][:, 0:n], in_=PSB[b][:, 0:n], func=func),
                       reads=[PSR[b]], writes=[R_stg[u]])
                dst = (QT_s if kind == "q" else GA_s)[idx, :, c0:c0 + n]
                store(SP, dst, stg[u][:, 0:n], R_stg[u], R_QTs if kind == "q" else R_GAs, final=False)
            else:
                ACT.op(lambda b=b, n=n, c0=c0: nc.scalar.activation(out=dst_sb[:, c0:c0 + n], in_=PSB[b][:, 0:n], func=func),
                       reads=[PSR[b]], writes=[r_dst] if ti == 0 else [], pwrites=[] if ti == 0 else [r_dst])
    c_chunk.ctr = 0

    for ci in range(4):
        ring_load(ci)
    for ci in range(16):
        if ci + 4 < len(allchunks):
            ring_load(ci + 4)
        load(POOL, WVB[:, ci, :], w_in[ci * 128:(ci + 1) * 128, OFF_VB:OFF_VB + 1024], R_WVB[ci])
        c_chunk(ci)
        if ci == 1:
            cum_stage2()
            cum_stage3()
        if ci == 3:
            cum_stage4()
    R_cum2, R_NCKs, R_CQ3s, NCK_s, CQ3_s = cumv["R_cum2"], cumv["R_NCKs"], cumv["R_CQ3s"], cumv["NCK_s"], cumv["CQ3_s"]

    if stop_after == "C1":
        drain(); return nc
    regroup(R_WVB)
    gxs = [MC.take([1024], F32) for _ in range(2)]
    vt = MC.take([1024], F32)
    lngb = MC.take([1024], F32)
    lnbb = MC.take([1024], F32)
    prevA = prevA + prevA_ext
    R_gxs = [Res("gx%d" % i, prev=prevA) for i in range(2)]
    R_vt = Res("vt", prev=prevA)
    R_gx, R_junk = R_gxs[0], R_gxs[1]
    R_ln = Res("ln", prev=prevA)
    R_st = [Res("st%d" % i) for i in range(2)]
    load(SP, lngb[:, :], ln_g.partition_broadcast(128), R_ln)
    ev = SP.dma(lnbb[:, :], ln_b.partition_broadcast(128), dsem_of(R_ln), pwrites=[R_ln])
    junkP = [PSB[4], PSB[5]]

    def b_vars(n):
        rows = 128 if n < NB else NS
        so = 16 + (n % 2) * 8
        return rows, n * 128, 2 * (n % 2), [sml[0:rows, so + i:so + i + 1] for i in range(8)], R_st[n % 2], gxs[n % 2], R_gxs[n % 2]

    def b_front(n):
        rows, c0, pb, (s1a, s1b, s2, msum, mean, msq, var, rstd), rst, gx, rgx = b_vars(n)
        for k in range(KC):
            for c in range(2):
                PE.op(lambda k=k, c=c: mm(PSB[pb + c][0:rows, :], H[:, k, c0:c0 + rows], WVB[:, k, c * 512:(c + 1) * 512],
                                          start=(k == 0), stop=(k == KC - 1)),
                      reads=[R_H[n], R_WVB[k]], writes=[PSR[pb + c]] if k == 0 else [], pwrites=[] if k == 0 else [PSR[pb + c]],
                      sig=(k == KC - 1))
        ACT.op(lambda: nc.scalar.activation(out=gx[0:rows, 0:512], in_=PSB[pb][0:rows, :], func=AF.Gelu_apprx_tanh, accum_out=s1a),
               reads=[PSR[pb]], writes=[rgx, rst])
        ACT.op(lambda: nc.scalar.activation(out=gx[0:rows, 512:1024], in_=PSB[pb + 1][0:rows, :], func=AF.Gelu_apprx_tanh, accum_out=s1b),
               reads=[PSR[pb + 1]], pwrites=[rgx, rst])
        for c in range(2):
            sq = s2 if c == 0 else msq
            ACT.op(lambda c=c, sq=sq: nc.scalar.activation(out=junkP[c][0:rows, :], in_=gx[0:rows, c * 512:(c + 1) * 512], func=AF.Square,
                                                           accum_out=sq),
                   reads=[rgx], writes=[PSR[4 + c]], pwrites=[rst])
        DVE.op(lambda: v.tensor_tensor(out=msum, in0=s1a, in1=s1b, op=ALU.add), reads=[rst], pwrites=[rst])
        DVE.op(lambda: v.tensor_scalar(out=mean, in0=msum, scalar1=1.0 / 1024, scalar2=None, op0=ALU.mult), reads=[rst], pwrites=[rst])
        DVE.op(lambda: v.tensor_tensor(out=s2, in0=s2, in1=msq, op=ALU.add), reads=[rst], pwrites=[rst])
        DVE.op(lambda: v.tensor_tensor(out=msq, in0=mean, in1=mean, op=ALU.mult), reads=[rst], pwrites=[rst])
        DVE.op(lambda: v.scalar_tensor_tensor(out=var, in0=s2, scalar=1.0 / 1024, in1=msq, op0=ALU.mult, op1=ALU.subtract),
               reads=[rst], pwrites=[rst])

    def b_sqrt(n):
        rows, c0, pb, (s1a, s1b, s2, msum, mean, msq, var, rstd), rst, gx, rgx = b_vars(n)
        ACT.op(lambda: nc.scalar.activation(out=var, in_=var, func=AF.Sqrt, bias=LN_EPS), reads=[rst], pwrites=[rst])

    def b_back(n):
        rows, c0, pb, (s1a, s1b, s2, msum, mean, msq, var, rstd), rst, gx, rgx = b_vars(n)
        DVE.op(lambda: v.reciprocal(out=rstd, in_=var), reads=[rst], pwrites=[rst])
        DVE.op(lambda: v.tensor_scalar(out=vt[0:rows, :], in0=gx[0:rows, :], scalar1=mean, scalar2=rstd, op0=ALU.subtract, op1=ALU.mult),
               reads=[rgx, rst], writes=[R_vt])
        POOL.op(lambda: g.tensor_tensor(out=vt[0:rows, :], in0=vt[0:rows, :], in1=lngb[0:rows, :], op=ALU.mult),
                reads=[R_vt, R_ln], writes=[R_vt])
        if n < NB:
            DVE.op(lambda: v.tensor_tensor(out=vn[:, n, :], in0=vt[:, :], in1=lnbb[:, :], op=ALU.add),
                   reads=[R_vt, R_ln], writes=[R_vn[n]])
        else:
            DVE.op(lambda: v.tensor_tensor(out=vt[0:rows, :], in0=vt[0:rows, :], in1=lnbb[0:rows, :], op=ALU.add),
                   reads=[R_vt, R_ln], writes=[R_vt])
            store(SP, gv_smp, vt[0:rows, :], R_vt)
            DVE.op(lambda: v.tensor_copy(out=vn_s[0:rows, :], in_=vt[0:rows, :]), reads=[R_vt], writes=[R_vn[NB]])

    for n0 in range(0, NB + 1, 2):
        grp_b = [n for n in (n0, n0 + 1) if n < NB + 1]
        for n in grp_b:
            b_front(n)
        for n in grp_b:
            b_sqrt(n)
        for n in grp_b:
            b_back(n)

    if stop_after == "B":
        drain(); return nc
    prevB = [R_gx, R_vt, R_junk, R_ln] + prevA
    MC2 = Bump(M_OFF + mark_c, MB - mark_c)
    Usb = [MC2.take([TOWN], BF16) for _ in range(2)]
    GBsb = [MC2.take([TOWN], BF16) for _ in range(2)]
    t1 = [MC2.take([512], F32) for _ in range(2)]
    R_U = [Res("U%d" % i, prev=prevB) for i in range(2)]
    R_GB = [Res("GB%d" % i, prev=prevB) for i in range(2)]
    R_t1 = [Res("t1_%d" % i, prev=prevB) for i in range(2)]
    oTB = view(W_OFF, [8, TOWN], BF16)
    R_oTB = [Res("oTB%d" % i, prev=R_WVB + R_WK + R_WV[0:1]) for i in range(8)]
    mixctr = {"n": 0}

    def mixing(gi):
        p = gi % 2
        for ti in range(5):
            b = next_bank()
            if ti < 4:
                for i in range(4):
                    blk = ti * 4 + i
                    PE.op(lambda blk=blk, i=i, b=b: mm(PSB[b][:, i * 128:(i + 1) * 128], vn[:, blk, gi * 128:(gi + 1) * 128], WT[:, gi, :],
                                                        start=True, stop=True),
                          reads=[R_vn[blk], R_c["WT"]], writes=[PSR[b]] if i == 0 else [], pwrites=[] if i == 0 else [PSR[b]],
                          sig=(i == 3))
                n, c0 = 512, ti * 512
                nb_ = 4
                bs_ap = bsb[:, gi, :].unsqueeze(1).broadcast_to([128, 4, 128])
                pin = PSB[b][:, :].rearrange("p (a t) -> p a t", a=4)
            else:
                PE.op(lambda b=b: mm(PSB[b][:, 0:NS], vn_s[0:NS, gi * 128:(gi + 1) * 128], WT[0:NS, gi, 0:NS], start=True, stop=True),
                      reads=[R_vn[NB], R_c["WT"]], writes=[PSR[b]])
                n, c0 = NS, 2048
                bs_ap = bsb[:, gi, 0:NS]
                pin = PSB[b][:, 0:NS]
            u = mixctr["n"] % 2
            mixctr["n"] += 1
            if ti < 4:
                t1v = t1[u][:, :].rearrange("p (a t) -> p a t", a=4)
            else:
                t1v = t1[u][:, 0:NS]
            DVE.op(lambda: v.tensor_tensor(out=t1v, in0=pin, in1=bs_ap, op=ALU.add), reads=[PSR[b], R_c["bsb"]], writes=[R_t1[u]])
            DVE.op(lambda: v.tensor_tensor(out=t1[u][:, 0:n], in0=t1[u][:, 0:n], in1=Usb[p][:, c0:c0 + n], op=ALU.mult),
                   reads=[R_t1[u], R_U[p]], writes=[R_t1[u]])
            DVE.op(lambda: v.tensor_tensor(out=oTB[:, gi, c0:c0 + n], in0=t1[u][:, 0:n], in1=GBsb[p][:, c0:c0 + n], op=ALU.mult),
                   reads=[R_t1[u], R_GB[p]], writes=[R_oTB[gi]] if ti == 0 else [], pwrites=[] if ti == 0 else [R_oTB[gi]])

    for ci in range(16, 32):
        if ci + 4 < len(allchunks):
            ring_load(ci + 4)
        kind, gi, col = allchunks[ci]
        if kind == "u":
            c_chunk(ci, Usb[gi % 2], R_U[gi % 2])
            if gi >= 1:
                mixing(gi - 1)
        else:
            c_chunk(ci, GBsb[gi % 2], R_GB[gi % 2])
    H2 = Bump(H_OFF + HB // 2, HB // 2)
    KTh0 = H2.take([NTOK_S], BF16)
    Vh0 = H2.take([33, 128], BF16)
    Qh0 = H2.take([TOWN], BF16)
    R_KTh0 = Res("KTh0", prev=R_H)
    R_Vh0 = Res("Vh0", prev=R_H)
    R_Qh0 = Res("Qh0", prev=R_H)

    def e_loads_big(h, KT_t, V_t, Q_t, rK, rV, rQ):
        load(SP, KT_t[:, :], KT_s[h, :, :], rK, extra_reads=[R_KTs])
        for q4 in range(4):
            SP.dma(V_t[:, q4 * 8:(q4 + 1) * 8, :],
                   V_s[q4 * 1024:(q4 + 1) * 1024, h * 128:(h + 1) * 128].rearrange("(b t) c -> t b c", t=128),
                   dsem_of(rV), reads=[R_Vs], writes=[rV] if q4 == 0 else [], pwrites=[] if q4 == 0 else [rV])
        SP.dma(V_t[0:NS, 32, :], V_s[4096:4096 + NS, h * 128:(h + 1) * 128], dsem_of(rV), pwrites=[rV])
        load(SP, Q_t[:, :], QT_s[h, :, :], rQ, extra_reads=[R_QTs])
    e_loads_big(0, KTh0, Vh0, Qh0, R_KTh0, R_Vh0, R_Qh0)
    mixing(7)

    if stop_after == "C2":
        drain(); return nc
    prevC = R_vn + R_stg + [R_gx, R_vt, R_junk, R_ln] + R_U + R_GB + R_t1
    prevH = R_H
    ME = Bump(M_OFF, MB)
    KTh = [KTh0, ME.take([NTOK_S], BF16)]
    Vh = [Vh0, ME.take([33, 128], BF16)]
    Qh = [Qh0, ME.take([TOWN], BF16)]
    GAh1 = H2.take([TOWN], BF16)
    GAh = [GAh1, GAh1]
    KTc = H2.take([PAST], BF16)
    kc = ME.take([16, 128], BF16)
    vc = ME.take([16, 128], BF16)
    Pb = [ME.take([512], BF16) for _ in range(5)]
    ptmp = H2.take([512], BF16)
    ptmp2 = H2.take([512], BF16)
    ptmp3 = H2.take([512], BF16)
    R_ptmp = Res("ptmp", prev=R_H)
    R_ptmp2 = Res("ptmp2", prev=R_H)
    R_ptmp3 = Res("ptmp3", prev=R_H)
    grp = {}
    fin = [ME.take([512], F32) for _ in range(2)]
    pv = [prevH, prevC]
    R_KTh = [R_KTh0, Res("KTh1", prev=prevC)]
    R_Vh = [R_Vh0, Res("Vh1", prev=prevC)]
    R_Qh = [R_Qh0, Res("Qh1", prev=prevC)]
    R_GAh1 = Res("GAh", prev=prevH)
    R_GAh = [R_GAh1, R_GAh1]
    R_KTc = Res("KTc", prev=prevH)
    R_kc, R_vc = Res("kc", prev=prevC), Res("vc", prev=prevC)
    A_all = ME.take([4096], BF16)
    Bh = [ME.take([2048], BF16) for _ in range(2)]
    As = ME.take([2176 + 128], BF16)
    R_Aall = Res("A_all", prev=prevC)
    R_Bh = [Res("Bh%d" % i, prev=prevC) for i in range(2)]
    R_Bh_ms = [Res("Bh_ms%d" % i) for i in range(2)]
    R_As_ms = Res("As_ms")
    R_As = Res("As", prev=prevC)
    R_Aall_ms = Res("A_all_ms", prev=prevC)
    DVE.op(lambda: v.memset(A_all[:, :], 1.0), writes=[R_Aall_ms, R_Aall])
    DVE.op(lambda: v.memset(As[:, :], 1.0), writes=[R_As])
    for hh in range(8):
        SP.dma(A_all[6 * hh + 3:6 * hh + 6, :], NCK_s[:, hh * 49:hh * 49 + 32, :].rearrange("s b t -> s (b t)"), dsem_of(R_Aall),
               reads=[R_NCKs, R_Aall_ms], pwrites=[R_Aall])
    ones_src3 = bass.AP(tensor=ones_b.tensor if hasattr(ones_b, "tensor") else ones_b, offset=0, ap=[[128, 3], [0, 16], [1, 128]])
    R_Pb = [Res("Pb%d" % i, prev=prevC) for i in range(5)]
    R_fin = [Res("fin%d" % i, prev=prevC) for i in range(2)]
    oTA = view(H_OFF, [8, TOWN], BF16)
    R_oTA = [Res("oTA%d" % i, prev=prevH) for i in range(8)]

    wo = [view(RING0 + k * 4096, [D], BF16) for k in range(8)]
    R_wo = [Res("wo%d" % k, prev=R_ring) for k in range(8)]
    for k in range(8):
        R_wo[k].dsem = R_ring[k].dsem
    wo2 = [view(H_OFF + HB // 2 + k * 4096, [D], BF16) for k in range(8)]
    R_wo2 = []
    ds_wo2 = [DSem(nc, "d_wo2_%d" % i) for i in range(2)]

    def wo_load(k):
        load(POOL, wo[k][:, :], w_out[k * 128:(k + 1) * 128, :], R_wo[k])

    def e_loads(h):
        p = h % 2
        if h > 0:
            e_loads_big(h, KTh[p], Vh[p], Qh[p], R_KTh[p], R_Vh[p], R_Qh[p])
        ACT.op(lambda: nc.scalar.memzero(Bh[p][:, :]), writes=[R_Bh_ms[p], R_Bh[p]])
        SP.dma(Bh[p][6 * h:6 * h + 3, :], CQ3_s[:, h * 17:h * 17 + 16, :].rearrange("s b t -> s (b t)"), dsem_of(R_Bh[p]),
               reads=[R_CQ3s, R_Bh_ms[p]], pwrites=[R_Bh[p]])
        SP.dma(Bh[p][6 * h + 3:6 * h + 6, :].rearrange("r (a t) -> r a t", t=128), ones_src3, dsem_of(R_Bh[p]),
               reads=[R_c["ones_b"], R_Bh_ms[p]], pwrites=[R_Bh[p]])

    ones_src1 = bass.AP(tensor=ones_b.tensor if hasattr(ones_b, "tensor") else ones_b, offset=0, ap=[[128, 3], [1, 128]])

    def e_loads2(h):
        load(SP, GAh1[:, :], GA_s[h, :, :], R_GAh1, extra_reads=[R_GAs])
        ACT.op(lambda: nc.scalar.memzero(As[:, 2176:2304]), writes=[R_As_ms, R_As])
        SP.dma(As[3:6, 0:2176], NCK_s[:, h * 49 + 32:h * 49 + 49, :].rearrange("s b t -> s (b t)"), dsem_of(R_As),
               reads=[R_NCKs, R_As_ms], pwrites=[R_As])
        SP.dma(As[0:3, 2176:2304], CQ3_s[:, h * 17 + 16, :], dsem_of(R_As), reads=[R_CQ3s, R_As_ms], pwrites=[R_As])
        SP.dma(As[3:6, 2176:2304], ones_src1, dsem_of(R_As), reads=[R_c["ones_b"], R_As_ms], pwrites=[R_As])
        load(POOL, kc[:, :, :], ck[:, h * 128:(h + 1) * 128].rearrange("(b t) c -> t b c", t=128), R_kc)
        load(POOL, vc[:, :, :], cv[:, h * 128:(h + 1) * 128].rearrange("(b t) c -> t b c", t=128), R_vc)

    units = []
    tiles_e = []
    for h in range(8):
        for T in range(5):
            tid = len(tiles_e)
            tiles_e.append((h, T))
            if T == 3:
                units.append(dict(h=h, T=T, tid=tid, kind="ktc", j=0, first=False, last=False, smp=False))
                units.append(dict(h=h, T=T, tid=tid, kind="ktc", j=1, first=False, last=False, smp=False))
            if T < 4:
                kbs = []
                for jp in range(4 * T + 4):
                    kbs.append(("own", jp))
                    kbs.append(("oth", jp))
            else:
                kbs = [("cacheall", 0), ("new", 16)]
            for i, (kk, jp) in enumerate(kbs):
                units.append(dict(h=h, T=T, tid=tid, kind=kk, j=jp, first=(i == 0), last=(i == len(kbs) - 1), smp=(T == 4)))

    NSB = 5
    NPB = len(Pb)

    def e_S(i):
        u = units[i]
        h, T, p = u["h"], u["T"], u["h"] % 2
        b = i % NSB
        u["sb"] = b
        if u["kind"] == "ktc":
            half = u["j"]
            for q4 in range(8):
                blk = half * 8 + q4
                PE.op(lambda blk=blk, q4=q4: nc.tensor.transpose(
                    PSB[b][:, :].bitcast(BF16)[:, q4 * 128:(q4 + 1) * 128], kc[:, blk, :], ident_b[:, :]),
                    reads=[R_kc, R_c["ident_b"]], writes=[PSR[b]] if q4 == 0 else [], pwrites=[] if q4 == 0 else [PSR[b]],
                    sig=(q4 == 7))
        elif not u["smp"]:
            jlo = max(u["j"], 4 * T)
            off = (jlo - 4 * T) * 128
            n = 512 - off
            kb = (u["j"] if u["kind"] == "own" else 16 + u["j"])
            kcol = kb * 128
            q0 = 4 * T * 128 + off
            u.update(off=off, n=n, kb=kb, rows=128)
            diag = u["j"] >= 4 * T
            PE.op(lambda: mm(PSB[b][:, off:off + n], KTh[p][:, kcol:kcol + 128], Qh[p][:, q0:q0 + n], start=True, stop=False),
                  reads=[R_KTh[p], R_Qh[p]], writes=[PSR[b]], sig=False)
            PE.op(lambda: mm(PSB[b][:, off:off + n], A_all[:, kcol:kcol + 128], Bh[p][:, q0:q0 + n], start=False, stop=not diag),
                  reads=[R_Aall, R_Bh[p]], pwrites=[PSR[b]], sig=not diag)
            if diag:
                mi = 0 if u["kind"] == "own" else 1 + u["j"] % 2
                PE.op(lambda: mm(PSB[b][:, off:off + 128], ident_b[:, :], mneg_b[:, mi, :], start=False, stop=True),
                      reads=[R_c["ident_b"], R_mnb], pwrites=[PSR[b]])
        elif u["kind"] == "cacheall":
            u.update(off=0, n=16 * NS, rows=128)
            for jj in range(16):
                PE.op(lambda jj=jj: mm(PSB[b][:, jj * NS:(jj + 1) * NS], KTc[:, jj * 128:(jj + 1) * 128], Qh[p][:, 2048:2048 + NS],
                                       start=(jj == 0), stop=False, skip_group_check=True),
                      reads=[R_KTc, R_Qh[p]], writes=[PSR[b]] if jj == 0 else [], pwrites=[] if jj == 0 else [PSR[b]], sig=False)
            for jj in range(16):
                kcol = (1 + jj) * 128
                PE.op(lambda jj=jj, kcol=kcol: mm(PSB[b][:, jj * NS:(jj + 1) * NS], As[:, kcol:kcol + 128], As[:, 2176:2176 + NS],
                                                 start=False, stop=(jj == 15), skip_group_check=True),
                      reads=[R_As], pwrites=[PSR[b]], sig=(jj == 15))
        else:
            u.update(off=0, n=NS, rows=NS)
            PE.op(lambda: mm(PSB[b][0:NS, 0:NS], KTh[p][:, 4096:4096 + NS], Qh[p][:, 2048:2048 + NS], start=True, stop=False),
                  reads=[R_KTh[p], R_Qh[p]], writes=[PSR[b]], sig=False)
            PE.op(lambda: mm(PSB[b][0:NS, 0:NS], As[:, 0:NS], As[:, 2176:2176 + NS], start=False, stop=False),
                  reads=[R_As], pwrites=[PSR[b]], sig=False)
            PE.op(lambda: mm(PSB[b][0:NS, 0:NS], ident_b[0:NS, 0:NS], mneg_b[0:NS, 0, 0:NS], start=False, stop=True),
                  reads=[R_c["ident_b"], R_mnb], pwrites=[PSR[b]])

    def e_soft(i):
        u = units[i]
        b = u["sb"]
        if u["kind"] == "ktc":
            half = u["j"]
            ACT.op(lambda: nc.scalar.activation(out=KTc[:, half * 1024:(half + 1) * 1024], in_=PSB[b][:, :].bitcast(BF16), func=AF.Copy),
                   reads=[PSR[b]], writes=[R_KTc] if half == 0 else [], pwrites=[] if half == 0 else [R_KTc])
            return
        off, n, rows = u["off"], u["n"], u["rows"]
        pi = i % NPB
        u["pi"] = pi
        tp = u["tid"] % 2
        ACT.op(lambda: nc.scalar.activation(out=Pb[pi][0:rows, off:off + n], in_=PSB[b][0:rows, off:off + n], func=AF.Exp, scale=SCALE),
               reads=[PSR[b]], writes=[R_Pb[pi]])
        if not u["smp"]:
            if u["kind"] == "oth":
                pa = units[i - 1]["pi"]
                m = u["j"]
                pair_in = dict(in0=Pb[pa][:, off:off + n], in1=Pb[pi][:, off:off + n], op=ALU.add)
                rd = [R_Pb[pa], R_Pb[pi]]
                q = m % 4
                if q == 0:
                    DVE.op(lambda: v.tensor_tensor(out=ptmp[:, off:off + n], **pair_in), reads=rd, writes=[R_ptmp])
                    grp["o0"], grp["n0"] = off, n
                elif q == 1:
                    DVE.op(lambda: v.tensor_tensor(out=ptmp2[:, off:off + n], **pair_in), reads=rd, writes=[R_ptmp2])
                    DVE.op(lambda: v.tensor_tensor(out=ptmp[:, off:off + n], in0=ptmp[:, off:off + n], in1=ptmp2[:, off:off + n], op=ALU.add),
                           reads=[R_ptmp2, R_ptmp], writes=[R_ptmp])
                elif q == 2:
                    DVE.op(lambda: v.tensor_tensor(out=ptmp3[:, off:off + n], **pair_in), reads=rd, writes=[R_ptmp3])
                    grp["o2"], grp["n2"] = off, n
                else:
                    o0, n0, o2, n2 = grp["o0"], grp["n0"], grp["o2"], grp["n2"]
                    DVE.op(lambda: v.tensor_tensor(out=ptmp2[:, off:off + n], **pair_in), reads=rd, writes=[R_ptmp2])
                    DVE.op(lambda: v.tensor_tensor(out=ptmp3[:, off:off + n], in0=ptmp3[:, off:off + n], in1=ptmp2[:, off:off + n], op=ALU.add),
                           reads=[R_ptmp2, R_ptmp3], writes=[R_ptmp3])
                    if m == 3 and o2 == 0:
                        DVE.op(lambda: v.tensor_tensor(out=fin[tp][:, :], in0=ptmp[:, :], in1=ptmp3[:, :], op=ALU.add),
                               reads=[R_ptmp, R_ptmp3], writes=[R_fin[tp]])
                    elif m == 3:
                        ACT.op(lambda: nc.scalar.activation(out=fin[tp][:, :], in_=ptmp[:, :], func=AF.Copy), reads=[R_ptmp], writes=[R_fin[tp]])
                        DVE.op(lambda: v.tensor_tensor(out=fin[tp][:, o2:o2 + n2], in0=fin[tp][:, o2:o2 + n2], in1=ptmp3[:, o2:o2 + n2], op=ALU.add),
                               reads=[R_ptmp3, R_fin[tp]], writes=[R_fin[tp]])
                    else:
                        DVE.op(lambda: v.tensor_tensor(out=ptmp[:, o2:o2 + n2], in0=ptmp[:, o2:o2 + n2], in1=ptmp3[:, o2:o2 + n2], op=ALU.add),
                               reads=[R_ptmp3, R_ptmp], writes=[R_ptmp])
                        DVE.op(lambda: v.tensor_tensor(out=fin[tp][:, o0:o0 + n0], in0=fin[tp][:, o0:o0 + n0], in1=ptmp[:, o0:o0 + n0], op=ALU.add),
                               reads=[R_ptmp, R_fin[tp]], writes=[R_fin[tp]])

    def e_PV(i):
        u = units[i]
        if u["kind"] == "ktc":
            return
        h, T, p = u["h"], u["T"], u["h"] % 2
        off, n, pi, rows = u["off"], u["n"], u["pi"], u["rows"]
        tp = u["tid"] % 2
        ab, db = 5 + tp, 7
        if not u["smp"]:
            PE.op(lambda: mm(PSB[ab][:, off:off + n], Vh[p][:, u["kb"], :], Pb[pi][:, off:off + n], start=u["first"], stop=u["last"]),
                  reads=[R_Vh[p], R_Pb[pi]], writes=[PSR[ab]] if u["first"] else [], pwrites=[] if u["first"] else [PSR[ab]], sig=True)
            if u["last"]:
                def _den(db=db, tp=tp):
                    PE.op(lambda: mm(PSB[db][:, :], ones_f[:, :], fin[tp][:, :], start=True, stop=True),
                          reads=[R_c["ones_f"], R_fin[tp]], writes=[PSR[db]])
                pending.append((i + 4, _den))
        elif u["kind"] == "cacheall":
            for jj in range(16):
                PE.op(lambda jj=jj: mm(PSB[ab][:, 0:NS], vc[:, jj, :], Pb[pi][:, jj * NS:(jj + 1) * NS], start=(jj == 0), stop=False),
                      reads=[R_vc, R_Pb[pi]], writes=[PSR[ab]] if jj == 0 else [], pwrites=[] if jj == 0 else [PSR[ab]], sig=False)
                PE.op(lambda jj=jj: mm(PSB[db][:, 0:NS], ones_b[:, :], Pb[pi][:, jj * NS:(jj + 1) * NS], start=(jj == 0), stop=False),
                      reads=[R_c["ones_b"], R_Pb[pi]], writes=[PSR[db]] if jj == 0 else [], pwrites=[] if jj == 0 else [PSR[db]],
                      sig=(jj == 15))
        else:
            PE.op(lambda: mm(PSB[ab][:, 0:NS], Vh[p][0:NS, 32, :], Pb[pi][0:NS, 0:NS], start=False, stop=True),
                  reads=[R_Vh[p], R_Pb[pi]], pwrites=[PSR[ab]], sig=False)
            PE.op(lambda: mm(PSB[db][:, 0:NS], ones_b[0:NS, :], Pb[pi][0:NS, 0:NS], start=False, stop=True),
                  reads=[R_c["ones_b"], R_Pb[pi]], pwrites=[PSR[db]], sig=True)
        if u["last"]:
            n_t = NS if u["smp"] else 512
            c0 = 2048 if u["smp"] else T * 512
            f = fin[tp]

            def _fin(db=db, ab=ab, tp=tp, n_t=n_t, c0=c0, f=f, h=h, T=T):
                DVE.op(lambda: v.reciprocal(out=f[:, 0:n_t], in_=PSB[db][:, 0:n_t]), reads=[PSR[db]], writes=[R_fin[tp]])
                DVE.op(lambda: v.tensor_tensor(out=f[:, 0:n_t], in0=PSB[ab][:, 0:n_t], in1=f[:, 0:n_t], op=ALU.mult),
                       reads=[PSR[ab], R_fin[tp]], writes=[R_fin[tp]])
                POOL.op(lambda: g.tensor_tensor(out=oTA[:, h, c0:c0 + n_t], in0=f[:, 0:n_t], in1=GAh1[:, c0:c0 + n_t], op=ALU.mult),
                        reads=[R_fin[tp], R_GAh1], writes=[R_oTA[h]] if T == 0 else [], pwrites=[] if T == 0 else [R_oTA[h]])
            if u["smp"]:
                _fin()
            else:
                pending.append((i + 4, _fin))

    pending = []

    def flush(upto):
        keep = []
        for (at, fn) in pending:
            if at <= upto:
                fn()
            else:
                keep.append((at, fn))
        pending[:] = keep

    e_loads(0)
    NU = len(units)
    LA = 4
    seen_heads = set()
    for i in range(NU + LA):
        if i < NU:
            e_S(i)
        jx = i - LA
        if jx >= 0:
            u = units[jx]
            if u["h"] not in seen_heads:
                flush(10 ** 9)
                seen_heads.add(u["h"])
                if u["h"] + 1 < 8:
                    e_loads(u["h"] + 1)
                e_loads2(u["h"])
                if u["h"] >= 1:
                    wo_load(u["h"] - 1)
                if u["h"] == 7:
                    wo_load(7)
                    for k in range(5):
                        R_wo2.append(Res("wo2_%d" % k, prev=[R_KTh[0], R_Vh[0], R_Qh[0]]))
                        R_wo2[k].dsem = ds_wo2[k // 4]
                        load(POOL, wo2[k][:, :], w_out[(8 + k) * 128:(9 + k) * 128, :], R_wo2[k])
            e_soft(jx)
            e_PV(jx)
            flush(jx)
    flush(10 ** 9)

    if stop_after == "E":
        drain(); return nc
    prevE2 = R_KTh[0:1] + R_Vh[0:1] + R_Qh[0:1] + [R_GAh1, R_KTc, R_ptmp, R_ptmp2, R_ptmp3]
    prevEM = R_KTh[1:2] + R_Vh[1:2] + R_Qh[1:2] + [R_kc, R_vc, R_Aall] + R_Bh + [R_mnb, R_As] + R_Pb + R_fin
    R_wo2.extend([Res("wo2_%d" % k, prev=prevE2) for k in range(5, 8)])
    for k in range(5, 8):
        R_wo2[k].dsem = ds_wo2[k // 4]
        load(POOL, wo2[k][:, :], w_out[(8 + k) * 128:(9 + k) * 128, :], R_wo2[k])
    regroup(R_wo2)
    wo_all = wo + wo2
    R_wo_all = R_wo + R_wo2
    MF = Bump(M_OFF, MB)
    xr = [MF.take([D], F32) for _ in range(2)]
    yb = [MF.take([D], F32) for _ in range(2)]
    fgb = MF.take([D], F32)
    junkF = MF.take([D], BF16)
    R_xr = [Res("xr%d" % i, prev=prevEM) for i in range(2)]
    R_yb = [Res("yb%d" % i, prev=prevEM) for i in range(2)]
    R_fg = Res("fg", prev=prevEM)
    R_junkF = Res("junkF", prev=prevEM)
    R_ssF = [Res("ssF%d" % i) for i in range(2)]
    load(SP, fgb[:, :], final_g.partition_broadcast(128), R_fg)

    def f_load(n):
        rows = 128 if n < NB else NS
        src = x_own[n * 128:(n + 1) * 128, :] if n < NB else x_smp
        load(SP, xr[n % 2][0:rows, :], src, R_xr[n % 2])

    def f_vars(n):
        return (128 if n < NB else NS), n * 128, n % 2

    def f_mm(n, ks):
        rows, c0, p = f_vars(n)
        for k in ks:
            lhs = oTA[:, k, c0:c0 + rows] if k < 8 else oTB[:, k - 8, c0:c0 + rows]
            rl = R_oTA[k] if k < 8 else R_oTB[k - 8]
            for cg in range(4):
                b = 4 * p + cg
                PE.op(lambda lhs=lhs, k=k, cg=cg, b=b: mm(PSB[b][0:rows, :], lhs, wo_all[k][:, cg * 512:(cg + 1) * 512],
                                                          start=(k == 0), stop=(k == KC - 1)),
                      reads=[rl, R_wo_all[k]], writes=[PSR[b]] if k == 0 else [], pwrites=[] if k == 0 else [PSR[b]],
                      sig=(k == KC - 1))

    def f_epi(n):
        rows, c0, p = f_vars(n)
        for cg in range(4):
            b = 4 * p + cg
            DVE.op(lambda cg=cg, b=b: v.tensor_tensor(out=yb[p][0:rows, cg * 512:(cg + 1) * 512], in0=PSB[b][0:rows, :],
                                                      in1=xr[p][0:rows, cg * 512:(cg + 1) * 512], op=ALU.add),
                   reads=[PSR[b], R_xr[p]], writes=[R_yb[p]] if cg == 0 else [], pwrites=[] if cg == 0 else [R_yb[p]])
        so = 40 + p * 4
        ss, sd, rs = [sml[0:rows, so + i:so + i + 1] for i in range(3)]
        ACT.op(lambda: nc.scalar.activation(out=junkF[0:rows, :], in_=yb[p][0:rows, :], func=AF.Square, accum_out=ss),
               reads=[R_yb[p]], writes=[R_junkF, R_ssF[p]])
        ACT.op(lambda: nc.scalar.activation(out=sd, in_=ss, func=AF.Sqrt, scale=1.0 / D, bias=RMS_EPS), reads=[R_ssF[p]], writes=[R_ssF[p]])
        DVE.op(lambda: v.reciprocal(out=rs, in_=sd), reads=[R_ssF[p]], writes=[R_ssF[p]])
        DVE.op(lambda: v.scalar_tensor_tensor(out=yb[p][0:rows, :], in0=yb[p][0:rows, :], scalar=rs, in1=fgb[0:rows, :],
                                              op0=ALU.mult, op1=ALU.mult),
                reads=[R_yb[p], R_ssF[p], R_fg], writes=[R_yb[p]])
        dst = y_own[n * 128:(n + 1) * 128, :] if n < NB else y_smp
        store(SP, dst, yb[p][0:rows, :], R_yb[p])


    f_load(0)
    f_load(1)
    f_mm(0, range(13)); f_mm(1, range(13))
    f_mm(0, range(13, KC)); f_epi(0)
    f_load(2)
    f_mm(1, range(13, KC)); f_epi(1)
    for n in range(2, NB + 1):
        if n + 1 < NB + 1:
            f_load(n + 1)
        f_mm(n, range(KC))
        f_epi(n)

    drain()
    return nc


_NC_CACHE = {}


def _own_blocks(p):
    gown = [2 * j + ((j + p) % 2) for j in range(NB)]
    goth = [2 * j + 1 - ((j + p) % 2) for j in range(NB)]
    return gown, goth


def make_in_maps(inputs):
    f32 = np.float32
    xp = np.ascontiguousarray(inputs["x_prompt"], dtype=f32)
    xs = np.ascontiguousarray(inputs["x_sample"], dtype=f32)
    cache_k = np.asarray(inputs["cache_k"], dtype=f32)
    cache_v = np.asarray(inputs["cache_v"], dtype=f32)
    cache_lf = np.asarray(inputs["cache_logf"], dtype=f32)
    shared = dict(
        w_in=np.ascontiguousarray(inputs["w_in"][0], dtype=f32), w_out=np.ascontiguousarray(inputs["w_out"][0], dtype=f32),
        norm_g=np.ascontiguousarray(inputs["norm_g"][0], dtype=f32), b_f=np.ascontiguousarray(inputs["b_f"][0], dtype=f32),
        ln_g=np.ascontiguousarray(inputs["ln_g"][0], dtype=f32), ln_b=np.ascontiguousarray(inputs["ln_b"][0], dtype=f32),
        w_s=np.ascontiguousarray(inputs["w_s"][0], dtype=f32), b_s=np.ascontiguousarray(inputs["b_s"][0], dtype=f32),
        final_g=np.ascontiguousarray(inputs["final_g"], dtype=f32))
    in_maps = []
    for c in range(8):
        b, p = c // 2, c % 2
        gown, goth = _own_blocks(p)
        xb = xp[b].reshape(32, 128, D)
        fo = np.zeros((128, 16, 8), f32)
        mno = np.zeros((2, 128, 128), f32)
        for j in range(NB):
            if (j + p) % 2 == 1:
                fo[:, j, :] = 1.0
        for r in range(2):
            mno[r] = 0.0 if (r + p) % 2 == 1 else NEG
        m = dict(shared)
        m.update(
            x_own=np.ascontiguousarray(xb[gown].reshape(NB * 128, D)),
            x_oth=np.ascontiguousarray(xb[goth].reshape(NB * 128, D)),
            x_smp=np.ascontiguousarray(xs[c]),
            ck=np.ascontiguousarray(cache_k[0, c].reshape(PAST, 1024)),
            cv=np.ascontiguousarray(cache_v[0, c].reshape(PAST, 1024)),
            clf=np.ascontiguousarray(cache_lf[0, c]),
            fo=fo.reshape(128, 128), fe=(1.0 - fo).reshape(128, 128), mno=mno)
        in_maps.append(m)
    return in_maps


def assemble(results):
    f32 = np.float32
    y_prompt = np.zeros((4, 32, 128, D), f32)
    k_prompt = np.zeros((1, 4, 32, 128, 8, 128), f32)
    v_prompt = np.zeros((1, 4, 32, 128, 8, 128), f32)
    lf_prompt = np.zeros((1, 4, 32, 128, 8), f32)
    y_sample = np.zeros((8, NS, D), f32)
    k_sample = np.zeros((1, 8, NS, 8, 128), f32)
    v_sample = np.zeros((1, 8, NS, 8, 128), f32)
    lf_sample = np.zeros((1, 8, NS, 8), f32)
    gv_sample = np.zeros((1, 8, NS, 1024), f32)
    for c in range(8):
        r = results[c]
        b, p = c // 2, c % 2
        gown, _ = _own_blocks(p)
        y_prompt[b, gown] = np.asarray(r["y_own"]).reshape(NB, 128, D)
        k_prompt[0, b, gown] = np.asarray(r["k_own"]).reshape(NB, 128, 8, 128)
        v_prompt[0, b, gown] = np.asarray(r["v_own"]).reshape(NB, 128, 8, 128)
        lf_prompt[0, b, gown] = np.asarray(r["lf_own"]).reshape(NB, 128, 8)
        y_sample[c] = np.asarray(r["y_smp"])
        k_sample[0, c] = np.asarray(r["k_smp"]).reshape(NS, 8, 128)
        v_sample[0, c] = np.asarray(r["v_smp"]).reshape(NS, 8, 128)
        lf_sample[0, c] = np.asarray(r["lf_smp"])
        gv_sample[0, c] = np.asarray(r["gv_smp"])
    return (y_prompt.reshape(4, 4096, D), y_sample, k_prompt.reshape(1, 4, 4096, 8, 128),
            v_prompt.reshape(1, 4, 4096, 8, 128), lf_prompt.reshape(1, 4, 4096, 8),
            k_sample, v_sample, lf_sample, gv_sample)


def kernel(**inputs):
    if "nc" not in _NC_CACHE:
        _NC_CACHE["nc"] = build_program()
    nc = _NC_CACHE["nc"]
    in_maps = make_in_maps(inputs)
    res = run_bass_kernel_spmd(nc, in_maps, core_ids=list(range(8)))
    return assemble(res.results)
```

```python
import numpy as np
import concourse.bass as bass
import concourse.mybir as mybir
from concourse.bass_utils import run_bass_kernel_spmd

F32 = mybir.dt.float32
BF16 = mybir.dt.bfloat16
AF = mybir.ActivationFunctionType
ALU = mybir.AluOpType

D = 2048
KC = 16
NB = 16
NS = 16
TOWN = NB * 128 + NS
PAST = 2048
DIN = 7176
OFF_Q, OFF_K, OFF_V, OFF_F, OFF_GA, OFF_U, OFF_VB, OFF_GB = 0, 1024, 2048, 3072, 3080, 4104, 5128, 6152
SCALE = 128 ** -0.5
RMS_EPS = 1e-6
LN_EPS = 1e-5
NEG = -1.0e6
NTOK_S = 2 * NB * 128 + NS


class Res:
    def __init__(self, name, prev=(), excl=False):
        self.name = name
        self.excl = excl
        self.w = {}
        self.r = {}
        self.dsem = None
        for p in prev:
            for d in (p.w, p.r):
                for k, ev in d.items():
                    if k not in self.r or self.r[k][1] < ev[1]:
                        self.r[k] = ev


def _merge(d, ev):
    k = id(ev[0])
    if k not in d or d[k][1] < ev[1]:
        d[k] = ev


class DSem:
    def __init__(self, nc, name):
        self.sem = nc.alloc_semaphore(name)
        self.cnt = 0


class Eng:
    def __init__(self, nc, eng, name, is_pe=False, compute=True):
        self.nc = nc
        self.eng = eng
        self.name = name
        self.is_pe = is_pe
        self.sem = nc.alloc_semaphore("sem_" + name) if compute else None
        self.cnt = 0
        self.seen = {}
        self.last_unsig = False

    def _wait(self, ev, raw=True):
        sem, val = ev
        if sem is self.sem and self.is_pe:
            return
        if self.seen.get(id(sem), 0) >= val:
            return
        self.eng.wait_ge(sem, val)
        self.seen[id(sem)] = val

    def _deps(self, reads, writes, pwrites):
        for r in reads:
            for ev in r.w.values():
                self._wait(ev, raw=True)
            if r.excl:
                for ev in r.r.values():
                    self._wait(ev, raw=False)
        for w in writes:
            for ev in w.w.values():
                self._wait(ev, raw=False)
            for ev in w.r.values():
                self._wait(ev, raw=False)
        for w in pwrites:
            for ev in w.r.values():
                self._wait(ev, raw=False)

    def _update(self, ev, reads, writes, pwrites):
        for w in writes:
            w.w = {id(ev[0]): ev}
            w.r = {}
        for w in pwrites:
            _merge(w.w, ev)
        for r in reads:
            _merge(r.r, ev)

    def op(self, fn, reads=(), writes=(), pwrites=(), sig=True):
        self._deps(reads, writes, pwrites)
        ins = fn()
        if sig:
            ins.then_inc(self.sem, 1)
            self.cnt += 1
            ev = (self.sem, self.cnt)
            self.last_unsig = False
        else:
            ev = (self.sem, self.cnt + 1)
            self.last_unsig = True
        self._update(ev, reads, writes, pwrites)
        return ev

    def dma(self, out, in_, dsem, reads=(), writes=(), pwrites=()):
        self._deps(reads, writes, pwrites)
        self.eng.dma_start(out=out, in_=in_).then_inc(dsem.sem, 16)
        dsem.cnt += 16
        ev = (dsem.sem, dsem.cnt)
        self._update(ev, reads, writes, pwrites)
        return ev


def build_program(debug=False, stop_after=None):
    nc = bass.Bass("TRN2", target_bir_lowering=False)
    all_dsems = []

    def din(name, shape):
        return nc.dram_tensor(name, list(shape), F32, kind="ExternalInput").ap()

    def dout(name, shape):
        return nc.dram_tensor(name, list(shape), F32, kind="ExternalOutput").ap()

    skind = "ExternalOutput" if debug else "Internal"

    def dscr(name, shape, dt=BF16):
        return nc.dram_tensor(name, list(shape), dt, kind=skind).ap()

    x_own = din("x_own", [NB * 128, D])
    x_oth = din("x_oth", [NB * 128, D])
    x_smp = din("x_smp", [NS, D])
    ck = din("ck", [PAST, 1024])
    cv = din("cv", [PAST, 1024])
    clf = din("clf", [PAST, 8])
    w_in = din("w_in", [D, DIN])
    w_out = din("w_out", [D, D])
    norm_g = din("norm_g", [D])
    b_f = din("b_f", [8])
    ln_g = din("ln_g", [1024])
    ln_b = din("ln_b", [1024])
    w_s = din("w_s", [8, 128, 128])
    b_s = din("b_s", [8, 128])
    final_g = din("final_g", [D])
    fo_in = din("fo", [128, 128])
    fe_in = din("fe", [128, 128])
    mno_in = din("mno", [2, 128, 128])

    y_own = dout("y_own", [NB * 128, D])
    y_smp = dout("y_smp", [NS, D])
    k_own = dout("k_own", [NB * 128, 1024])
    v_own = dout("v_own", [NB * 128, 1024])
    lf_own = dout("lf_own", [NB * 128, 8])
    k_smp = dout("k_smp", [NS, 1024])
    v_smp = dout("v_smp", [NS, 1024])
    lf_smp = dout("lf_smp", [NS, 8])
    gv_smp = dout("gv_smp", [NS, 1024])

    KT_s = dscr("KT_s", [8, 128, NTOK_S])
    V_s = dscr("V_s", [NTOK_S, 1024])
    QT_s = dscr("QT_s", [8, 128, TOWN])
    GA_s = dscr("GA_s", [8, 128, TOWN])

    PE = Eng(nc, nc.tensor, "pe", is_pe=True)
    ACT = Eng(nc, nc.scalar, "act")
    DVE = Eng(nc, nc.vector, "dve")
    POOL = Eng(nc, nc.gpsimd, "pool")
    SP = Eng(nc, nc.sync, "sp", compute=False)
    store_events = []

    def dsem_of(res):
        if res.dsem is None:
            res.dsem = DSem(nc, "d_" + res.name)
        if res.dsem not in all_dsems:
            all_dsems.append(res.dsem)
        return res.dsem

    def drain():
        for ds in all_dsems:
            if ds.cnt > 0:
                SP._wait((ds.sem, ds.cnt))
        for e in (PE, ACT, DVE, POOL):
            if e.cnt > 0:
                SP._wait((e.sem, e.cnt))

    def regroup(res_list):
        for r in res_list:
            ds = r.dsem
            r.w = {id(ds.sem): (ds.sem, ds.cnt)}

    def load(q, out, in_, res, extra_reads=()):
        return q.dma(out, in_, dsem_of(res), reads=list(extra_reads), writes=[res])

    def store(q, out, in_, res, dram_res=None, final=True):
        ev = q.dma(out, in_, dsem_of(res), reads=[res], pwrites=[dram_res] if dram_res is not None else [])
        if final:
            store_events.append(ev)
        return ev

    PSB = [nc.alloc_psum_tensor("psb%d" % i, [128, 512], F32) for i in range(8)]
    PSR = [Res("psb%d" % i, excl=True) for i in range(8)]

    def sb(name, shape, dt=F32):
        return nc.alloc_sbuf_tensor(name, list(shape), dt)

    ident_f = sb("ident_f", [128, 128]); ident_b = sb("ident_b", [128, 128], BF16)
    utri_f = sb("utri_f", [128, 128])
    ones_f = sb("ones_f", [128, 128]); ones_b = sb("ones_b", [128, 128], BF16)
    mneg_d = sb("mneg_d", [128, 128])
    mneg_o = sb("mneg_o", [128, 2, 128])
    mneg_b = sb("mneg_b", [128, 3, 128], BF16)
    fo_t = sb("fo_t", [128, 128]); fe_t = sb("fe_t", [128, 128])
    ng16 = sb("ng16", [16, 128]); gcol = sb("gcol", [128, 16])
    bfb = sb("bfb", [128, 8])
    bsb = sb("bsb", [128, 8, 128])
    WT = sb("WT", [128, 8, 128], BF16)
    wf = sb("wf", [128, 16, 8], BF16)
    tz = sb("tz", [128, 33, 8])
    lfc = sb("lfc", [128, 16, 8])
    cum = sb("cum", [128, 49, 8])
    ncum = sb("ncum", [128, 49, 8])
    tmpA = sb("tmpA", [128, 128]); tmpB = sb("tmpB", [128, 128]); tmpC = sb("tmpC", [128, 128])
    tmpD = sb("tmpD", [128, 128]); tmpE = sb("tmpE", [128, 128])
    ones16 = sb("ones16", [128, 16])
    sml = sb("sml", [128, 64])
    R_consts = Res("consts")
    R_tz = [Res("tz%d" % i) for i in range(33)]
    R_cum = Res("cum")

    rem = nc.sbuf_bytes_remaining
    HB = KC * TOWN * 2
    WB = HB
    MB = (rem - HB - WB - 64) // 64 * 64
    assert MB >= 59600, MB
    arena = nc.alloc_sbuf_tensor("arena", [128, (HB + WB + MB) // 4], F32)
    H_OFF, W_OFF, M_OFF = 0, HB, HB + WB

    def view(off, shape, dt):
        es = 4 if dt == F32 else 2
        n = int(np.prod(shape))
        nbytes = n * es
        assert off % 4 == 0 and nbytes % 4 == 0, (off, shape)
        ap = arena[:, off // 4:(off + nbytes) // 4]
        if dt != F32:
            ap = ap.bitcast(dt)
        if len(shape) == 2:
            ap = ap.rearrange("p (a b) -> p a b", a=shape[0])
        elif len(shape) == 3:
            ap = ap.rearrange("p (a b c) -> p a b c", a=shape[0], b=shape[1])
        return ap

    class Bump:
        def __init__(self, base, size):
            self.base, self.size, self.off = base, size, 0

        def take(self, shape, dt):
            es = 4 if dt == F32 else 2
            nbytes = (int(np.prod(shape)) * es + 31) // 32 * 32
            assert self.off + nbytes <= self.size, ("arena overflow", self.off, nbytes, self.size)
            v = view(self.base + self.off, shape, dt)
            self.off += nbytes
            return v

    H = view(H_OFF, [KC, TOWN], BF16)
    R_H = [Res("H%d" % j) for j in range(NB + 1)]

    wsf = view(M_OFF, [8, 128], F32)
    g = nc.gpsimd

    def pool(fn, reads=(), writes=()):
        return POOL.op(fn, reads=reads, writes=writes)

    R_c = {n: Res(n) for n in ["ident_f", "ident_b", "utri", "ones_f", "ones_b", "mneg_d",
                               "ones16", "mneg_o", "fo", "fe", "ng16", "gcol", "bfb", "bsb", "wsf", "WT", "wf", "lfc"]}
    pool(lambda: g.memset(ident_f[:], 1.0), writes=[R_c["ident_f"]])
    pool(lambda: g.affine_select(out=ident_f[:], in_=ident_f[:], pattern=[[-1, 128]], compare_op=ALU.is_equal,
                                 fill=0.0, base=0, channel_multiplier=1),
         reads=[R_c["ident_f"]], writes=[R_c["ident_f"]])
    pool(lambda: g.tensor_copy(out=ident_b[:], in_=ident_f[:]), reads=[R_c["ident_f"]], writes=[R_c["ident_b"]])
    pool(lambda: g.memset(utri_f[:], 1.0), writes=[R_c["utri"]])
    pool(lambda: g.affine_select(out=utri_f[:], in_=utri_f[:], pattern=[[1, 128]], compare_op=ALU.is_ge,
                                 fill=0.0, base=0, channel_multiplier=-1),
         reads=[R_c["utri"]], writes=[R_c["utri"]])
    pool(lambda: g.memset(ones_f[:], 1.0), writes=[R_c["ones_f"]])
    pool(lambda: g.memset(ones_b[:], 1.0), writes=[R_c["ones_b"]])
    pool(lambda: g.memset(ones16[:], 1.0), writes=[R_c["ones16"]])
    pool(lambda: g.memset(cum[:, :, :].rearrange("p j h -> p (j h)"), 0.0), writes=[R_cum])
    pool(lambda: g.memset(mneg_d[:], 0.0), writes=[R_c["mneg_d"]])
    pool(lambda: g.affine_select(out=mneg_d[:], in_=mneg_d[:], pattern=[[1, 128]], compare_op=ALU.is_ge,
                                 fill=NEG, base=0, channel_multiplier=-1),
         reads=[R_c["mneg_d"]], writes=[R_c["mneg_d"]])

    ds_setup = DSem(nc, "d_setup")
    for nm in ["mneg_o", "fo", "fe", "ng16", "bfb", "bsb", "lfc", "wsf"]:
        R_c[nm].dsem = ds_setup
    load(SP, mneg_o[:], mno_in.rearrange("r k q -> k r q"), R_c["mneg_o"])
    load(SP, fo_t[:], fo_in, R_c["fo"])
    load(SP, fe_t[:], fe_in, R_c["fe"])
    load(SP, ng16[:], norm_g.rearrange("(k p) -> k p", p=128), R_c["ng16"])
    load(SP, bfb[:], b_f.partition_broadcast(128), R_c["bfb"])
    load(SP, bsb[:].rearrange("p g t -> p (g t)"), b_s.rearrange("g t -> (g t)").partition_broadcast(128), R_c["bsb"])
    load(SP, lfc[:], clf.rearrange("(j p) h -> p j h", p=128), R_c["lfc"])
    load(SP, wsf[:, :, :], w_s.rearrange("g t s -> t g s"), R_c["wsf"])
    regroup([R_c[nm] for nm in ["mneg_o", "fo", "fe", "ng16", "bfb", "bsb", "lfc", "wsf"]])

    R_mnb = Res("mneg_b")
    pool(lambda: g.tensor_copy(out=mneg_b[:, 0, :], in_=mneg_d[:, :]), reads=[R_c["mneg_d"]], writes=[R_mnb])
    pool(lambda: g.tensor_copy(out=mneg_b[:, 1:3, :], in_=mneg_o[:, :, :]), reads=[R_c["mneg_o"], R_mnb], writes=[R_mnb])
    WK = view(W_OFF, [KC, 1024], BF16)
    WV = view(W_OFF + 32768, [KC, 1024], BF16)
    R_WK = [Res("WK%d" % k) for k in range(KC)]
    R_WV = [Res("WV%d" % k) for k in range(KC)]
    ds_wk = [DSem(nc, "d_wk%d" % i) for i in range(4)]
    ds_wv = [DSem(nc, "d_wv%d" % i) for i in range(4)]
    for k in range(KC):
        R_WK[k].dsem = ds_wk[k // 4]
        R_WV[k].dsem = ds_wv[k // 4]
    for k in range(KC):
        load(POOL, WK[:, k, :], w_in[k * 128:(k + 1) * 128, OFF_K:OFF_K + 1024], R_WK[k])
    regroup(R_WK)
    load(POOL, wf[:], w_in.rearrange("(k p) c -> p k c", p=128)[:, :, OFF_F:OFF_F + 8], R_c["wf"])
    for k in range(KC):
        load(POOL, WV[:, k, :], w_in[k * 128:(k + 1) * 128, OFF_V:OFF_V + 1024], R_WV[k])
    regroup(R_WV)

    PE.op(lambda: nc.tensor.transpose(PSB[7][:, 0:16], ng16[:, :], ident_f[0:16, 0:16]),
          reads=[R_c["ng16"], R_c["ident_f"]], writes=[PSR[7]])
    DVE.op(lambda: nc.vector.tensor_copy(out=gcol[:], in_=PSB[7][:, 0:16]), reads=[PSR[7]], writes=[R_c["gcol"]])
    for half in range(2):
        for gg in range(4):
            gi = half * 4 + gg
            PE.op(lambda gi=gi, gg=gg, half=half: nc.tensor.transpose(
                PSB[5 + half][:, gg * 128:(gg + 1) * 128], wsf[:, gi, :], ident_f[:, :]),
                reads=[R_c["wsf"], R_c["ident_f"]], pwrites=[PSR[5 + half]], sig=(gg == 3))
        DVE.op(lambda half=half: nc.vector.tensor_tensor(
            out=WT[:, half * 4:(half + 1) * 4, :],
            in0=PSB[5 + half][:, :].rearrange("p (g t) -> p g t", g=4),
            in1=utri_f[:, :].unsqueeze(1).broadcast_to([128, 4, 128]), op=ALU.mult),
            reads=[PSR[5 + half], R_c["utri"]], pwrites=[R_c["WT"]])

    if stop_after == "setup":
        drain(); return nc
    MA = Bump(M_OFF, MB)
    xbuf = [MA.take([D], F32) for _ in range(2)]
    xn = [MA.take([D], BF16) for _ in range(2)]
    hTb = [MA.take([KC, 128], BF16) for _ in range(2)]
    Kf32 = MA.take([1024], F32)
    Vf32 = MA.take([1024], F32)
    Kb = [MA.take([1024], BF16) for _ in range(2)]
    Vb = [MA.take([1024], BF16) for _ in range(2)]
    KTsb = [MA.take([8, 128], BF16) for _ in range(2)]
    R_xbuf = [Res("xbuf%d" % i, prev=[R_c["wsf"]]) for i in range(2)]
    R_xn = [Res("xn%d" % i) for i in range(2)]
    R_hTb = [Res("hTb%d" % i) for i in range(2)]
    R_Kf32, R_Vf32 = Res("Kf32"), Res("Vf32")
    R_Kb = [Res("Kb%d" % i) for i in range(2)]
    R_Vb = [Res("Vb%d" % i) for i in range(2)]
    R_KTsb = [Res("KTsb%d" % i) for i in range(2)]
    R_ss = [Res("ss%d" % i) for i in range(2)]
    R_KTs = Res("KT_s"); R_Vs = Res("V_s"); R_QTs = Res("QT_s"); R_GAs = Res("GA_s")

    blocks = [("own", j) for j in range(NB)] + [("smp", 0)] + [("oth", j) for j in range(NB)]
    NBLK = len(blocks)
    psT = [PSB[0][:, :].bitcast(BF16).rearrange("p (k t) -> p k t", k=8),
           PSB[1][:, :].bitcast(BF16).rearrange("p (k t) -> p k t", k=8)]
    psKT = PSB[6][:, :].bitcast(BF16).rearrange("p (h t) -> p h t", h=8)

    def binfo(n):
        kind, j = blocks[n]
        rows = NS if kind == "smp" else 128
        if kind == "own":
            xsrc = x_own[j * 128:(j + 1) * 128, :]
            tcol = j * 128
            tzi = j
        elif kind == "oth":
            xsrc = x_oth[j * 128:(j + 1) * 128, :]
            tcol = 2048 + j * 128
            tzi = 16 + j
        else:
            xsrc = x_smp
            tcol = 4096
            tzi = 32
        return kind, j, rows, xsrc, tcol, tzi

    def hT_of(n):
        kind, j, rows, _, _, _ = binfo(n)
        if kind == "own":
            return H[:, :, j * 128:(j + 1) * 128], R_H[j]
        if kind == "smp":
            return H[:, :, 2048:2048 + NS], R_H[NB]
        return hTb[n % 2][:, :, :], R_hTb[n % 2]

    def a_load(n):
        kind, j, rows, xsrc, _, _ = binfo(n)
        load(SP, xbuf[n % 2][0:rows, :], xsrc, R_xbuf[n % 2])

    def a_norm(n):
        kind, j, rows, _, _, _ = binfo(n)
        p = n % 2
        ss = sml[0:rows, p * 4:p * 4 + 1]
        sd = sml[0:rows, p * 4 + 1:p * 4 + 2]
        rs = sml[0:rows, p * 4 + 2:p * 4 + 3]
        ACT.op(lambda: nc.scalar.activation(out=xn[p][0:rows, :], in_=xbuf[p][0:rows, :], func=AF.Square,
                                            accum_out=ss), reads=[R_xbuf[p]], writes=[R_xn[p], R_ss[p]])
        ACT.op(lambda: nc.scalar.activation(out=sd, in_=ss, func=AF.Sqrt, scale=1.0 / D, bias=RMS_EPS),
               reads=[R_ss[p]], writes=[R_ss[p]])
        DVE.op(lambda: nc.vector.reciprocal(out=rs, in_=sd), reads=[R_ss[p]], writes=[R_ss[p]])
        ACT.op(lambda: nc.scalar.activation(out=xn[p][0:rows, :], in_=xbuf[p][0:rows, :], func=AF.Copy, scale=rs),
               reads=[R_xbuf[p], R_ss[p]], writes=[R_xn[p]])

    def a_transp(n):
        kind, j, rows, _, _, _ = binfo(n)
        p = n % 2
        for k in range(KC):
            PE.op(lambda k=k: nc.tensor.transpose(psT[k // 8][:, k % 8, 0:rows], xn[p][0:rows, k * 128:(k + 1) * 128],
                                                  ident_b[0:rows, 0:rows]),
                  reads=[R_xn[p], R_c["ident_b"]], pwrites=[PSR[0], PSR[1]] if k == 0 else [], sig=(k == KC - 1))
        dst, rdst = hT_of(n)
        for half in range(2):
            DVE.op(lambda half=half: nc.vector.tensor_tensor(
                out=dst[:, half * 8:(half + 1) * 8, :], in0=psT[half][:, :, 0:rows],
                in1=gcol[:, half * 8:(half + 1) * 8].unsqueeze(2).broadcast_to([128, 8, rows]), op=ALU.mult),
                reads=[PSR[half], R_c["gcol"]], pwrites=[rdst])

    def a_Kmm(n):
        kind, j, rows, _, _, _ = binfo(n)
        hT, rh = hT_of(n)
        for k in range(KC):
            for c in range(2):
                PE.op(lambda k=k, c=c: nc.tensor.matmul(PSB[2 + c][0:rows, :], hT[:, k, :], WK[:, k, c * 512:(c + 1) * 512],
                                                        start=(k == 0), stop=(k == KC - 1)),
                      reads=[rh, R_WK[k]], writes=[PSR[2 + c]] if k == 0 else [], pwrites=[] if k == 0 else [PSR[2 + c]],
                      sig=(k == KC - 1))

    def a_Vmm(n):
        kind, j, rows, _, _, _ = binfo(n)
        hT, rh = hT_of(n)
        for k in range(KC):
            for c in range(2):
                PE.op(lambda k=k, c=c: nc.tensor.matmul(PSB[4 + c][0:rows, :], hT[:, k, :], WV[:, k, c * 512:(c + 1) * 512],
                                                        start=(k == 0), stop=(k == KC - 1)),
                      reads=[rh, R_WV[k]], writes=[PSR[4 + c]] if k == 0 else [], pwrites=[] if k == 0 else [PSR[4 + c]],
                      sig=(k == KC - 1))
            PE.op(lambda k=k: nc.tensor.matmul(PSB[7][0:rows, 0:8], hT[:, k, :], wf[:, k, :],
                                               start=(k == 0), stop=(k == KC - 1)),
                  reads=[rh, R_c["wf"]], writes=[PSR[7]] if k == 0 else [], pwrites=[] if k == 0 else [PSR[7]],
                  sig=(k == KC - 1))

    def a_Kepi(n):
        kind, j, rows, _, _, _ = binfo(n)
        p = n % 2
        if kind != "oth":
            for c in range(2):
                ACT.op(lambda c=c: nc.scalar.activation(out=Kf32[0:rows, c * 512:(c + 1) * 512], in_=PSB[2 + c][0:rows, :],
                                                        func=AF.Copy),
                       reads=[PSR[2 + c]], writes=[R_Kf32] if c == 0 else [], pwrites=[] if c == 0 else [R_Kf32])
            POOL.op(lambda: g.tensor_copy(out=Kb[p][0:rows, :], in_=Kf32[0:rows, :]), reads=[R_Kf32], writes=[R_Kb[p]])
        else:
            for c in range(2):
                DVE.op(lambda c=c: nc.vector.tensor_copy(out=Kb[p][0:rows, c * 512:(c + 1) * 512], in_=PSB[2 + c][0:rows, :]),
                       reads=[PSR[2 + c]], writes=[R_Kb[p]] if c == 0 else [], pwrites=[] if c == 0 else [R_Kb[p]])

    def a_KTtr(n):
        kind, j, rows, _, _, _ = binfo(n)
        p = n % 2
        for h in range(8):
            PE.op(lambda h=h: nc.tensor.transpose(psKT[:, h, 0:rows], Kb[p][0:rows, h * 128:(h + 1) * 128],
                                                  ident_b[0:rows, 0:rows]),
                  reads=[R_Kb[p], R_c["ident_b"]], writes=[PSR[6]] if h == 0 else [], pwrites=[] if h == 0 else [PSR[6]],
                  sig=(h == 7))

    def a_Vepi(n):
        kind, j, rows, _, _, tzi = binfo(n)
        p = n % 2
        if kind != "oth":
            for c in range(2):
                ACT.op(lambda c=c: nc.scalar.activation(out=Vf32[0:rows, c * 512:(c + 1) * 512], in_=PSB[4 + c][0:rows, :],
                                                        func=AF.Copy),
                       reads=[PSR[4 + c]], writes=[R_Vf32] if c == 0 else [], pwrites=[] if c == 0 else [R_Vf32])
            POOL.op(lambda: g.tensor_copy(out=Vb[p][0:rows, :], in_=Vf32[0:rows, :]), reads=[R_Vf32], writes=[R_Vb[p]])
        else:
            for c in range(2):
                DVE.op(lambda c=c: nc.vector.tensor_copy(out=Vb[p][0:rows, c * 512:(c + 1) * 512], in_=PSB[4 + c][0:rows, :]),
                       reads=[PSR[4 + c]], writes=[R_Vb[p]] if c == 0 else [], pwrites=[] if c == 0 else [R_Vb[p]])
        DVE.op(lambda: nc.vector.tensor_tensor(out=tz[0:rows, tzi, :], in0=PSB[7][0:rows, 0:8], in1=bfb[0:rows, :], op=ALU.add),
               reads=[PSR[7], R_c["bfb"]], writes=[R_tz[tzi]])
        DVE.op(lambda: nc.vector.tensor_copy(out=KTsb[p][:, :, 0:rows], in_=psKT[:, :, 0:rows]),
               reads=[PSR[6]], writes=[R_KTsb[p]])

    def a_stores(n):
        kind, j, rows, _, tcol, _ = binfo(n)
        p = n % 2
        if kind == "own":
            store(SP, k_own[j * 128:(j + 1) * 128, :], Kf32[:, :], R_Kf32)
            store(SP, v_own[j * 128:(j + 1) * 128, :], Vf32[:, :], R_Vf32)
        elif kind == "smp":
            store(SP, k_smp, Kf32[0:rows, :], R_Kf32)
            store(SP, v_smp, Vf32[0:rows, :], R_Vf32)
        store(SP, KT_s.rearrange("h d t -> d h t")[:, :, tcol:tcol + rows], KTsb[p][:, :, 0:rows], R_KTsb[p], R_KTs, final=False)
        store(SP, V_s[tcol:tcol + rows, :], Vb[p][0:rows, :], R_Vb[p], R_Vs, final=False)

    a_load(0); a_load(1)
    a_norm(0); a_transp(0)
    for n in range(NBLK):
        if n + 2 < NBLK:
            a_load(n + 2)
        if n + 1 < NBLK:
            a_norm(n + 1)
        a_Kmm(n)
        if n + 1 < NBLK:
            a_transp(n + 1)
        a_Kepi(n)
        a_Vmm(n)
        a_KTtr(n)
        a_Vepi(n)
        a_stores(n)

    if stop_after == "A":
        drain(); return nc
    v = nc.vector
    tzf = tz[:, :, :].rearrange("p j h -> p (j h)")
    ACT.op(lambda: nc.scalar.activation(out=tzf[:, 0:256], in_=tzf[:, 0:256], func=AF.Exp, scale=-1.0),
           reads=R_tz[0:16] + R_tz[17:33], writes=[R_cum])
    ACT.op(lambda: nc.scalar.activation(out=tzf[0:NS, 256:264], in_=tzf[0:NS, 256:264], func=AF.Exp, scale=-1.0),
           reads=[R_tz[32]], pwrites=[R_cum])
    ACT.op(lambda: nc.scalar.activation(out=tzf[:, 0:256], in_=tzf[:, 0:256], func=AF.Ln, bias=1.0),
           reads=[R_cum], pwrites=[R_cum])
    ACT.op(lambda: nc.scalar.activation(out=tzf[0:NS, 256:264], in_=tzf[0:NS, 256:264], func=AF.Ln, bias=1.0),
           reads=[R_cum], pwrites=[R_cum])
    DVE.op(lambda: v.tensor_scalar(out=tzf[:, 0:256], in0=tzf[:, 0:256], scalar1=-1.0, scalar2=None, op0=ALU.mult),
           reads=[R_cum], pwrites=[R_cum])
    DVE.op(lambda: v.tensor_scalar(out=tzf[0:NS, 256:264], in0=tzf[0:NS, 256:264], scalar1=-1.0, scalar2=None, op0=ALU.mult),
           reads=[R_cum], pwrites=[R_cum])
    R_lf = Res("lf")
    store(SP, lf_own.rearrange("(j p) h -> p j h", p=128), tz[:, 0:16, :], R_cum)
    store(SP, lf_smp, tz[0:NS, 32, :], R_cum)
    cumv = {}
    prevA_ext = []
    prevA = R_xbuf + R_xn + R_hTb + [R_Kf32, R_Vf32] + R_Kb + R_Vb + R_KTsb
    mm = nc.tensor.matmul
    def cum_stage2():
        LO = tzf[:, 0:128]
        LT = tzf[:, 128:256]
        LS = tzf[0:NS, 256:264]
        lfcf = lfc[:, :, :].rearrange("p j h -> p (j h)")
        cumf = cum[:, :, :].rearrange("p j h -> p (j h)")
        mm = nc.tensor.matmul
        PE.op(lambda: mm(PSB[7][:, 0:128], utri_f[:, :], LO, start=True, stop=True), reads=[R_cum, R_c["utri"]], writes=[PSR[7]], sig=False)
        PE.op(lambda: mm(PSB[7][:, 128:256], utri_f[:, :], LT, start=True, stop=True), pwrites=[PSR[7]], sig=False)
        PE.op(lambda: mm(PSB[7][:, 256:384], ones_f[:, :], LO, start=True, stop=True), reads=[R_c["ones_f"]], pwrites=[PSR[7]], sig=False)
        PE.op(lambda: mm(PSB[7][:, 384:512], ones_f[:, :], LT, start=True, stop=True), pwrites=[PSR[7]])
        PE.op(lambda: mm(PSB[6][:, 0:128], utri_f[:, :], lfcf, start=True, stop=True), reads=[R_c["lfc"]], writes=[PSR[6]], sig=False)
        PE.op(lambda: mm(PSB[6][:, 128:256], ones_f[:, :], lfcf, start=True, stop=True), pwrites=[PSR[6]], sig=False)
        PE.op(lambda: mm(PSB[6][0:NS, 256:264], utri_f[0:NS, 0:NS], LS, start=True, stop=True), pwrites=[PSR[6]])
        cumv.update(LO=LO, LT=LT, LS=LS, lfcf=lfcf, cumf=cumf)

    def cum_stage3():
        cumf = cumv['cumf']
        R_t = Res("cumtmp")
        DVE.op(lambda: v.tensor_copy(out=tmpC[:, :], in_=PSB[7][:, 384:512]), reads=[PSR[7]], writes=[R_t])
        DVE.op(lambda: v.tensor_tensor(out=tmpA[:, :], in0=PSB[7][:, 256:384], in1=tmpC[:, :], op=ALU.add), reads=[R_t], pwrites=[R_t])
        tA = tmpA[:, :].rearrange("p (j h) -> p h j", h=8)
        tB = tmpB[:, :].rearrange("p (j h) -> p h j", h=8)
        for h in range(8):
            DVE.op(lambda h=h: v.tensor_tensor_scan(out=tB[:, h, :], data0=ones16[:, :], data1=tA[:, h, :], initial=0.0,
                                                    op0=ALU.mult, op1=ALU.add), reads=[R_t, R_c["ones16"]], pwrites=[R_t])
        DVE.op(lambda: v.tensor_tensor(out=tmpB[:, :], in0=tmpB[:, :], in1=tmpA[:, :], op=ALU.subtract), reads=[R_t], pwrites=[R_t])
        DVE.op(lambda: v.tensor_tensor(out=tmpD[:, :], in0=tmpC[:, :], in1=fo_t[:, :], op=ALU.mult), reads=[R_t, R_c["fo"]], pwrites=[R_t])
        DVE.op(lambda: v.tensor_tensor(out=tmpD[:, :], in0=tmpD[:, :], in1=tmpB[:, :], op=ALU.add), reads=[R_t], pwrites=[R_t])
        DVE.op(lambda: v.tensor_tensor(out=cumf[:, 0:128], in0=PSB[7][:, 0:128], in1=tmpD[:, :], op=ALU.add), reads=[R_t], pwrites=[R_t])
        DVE.op(lambda: v.tensor_tensor(out=tmpE[:, :], in0=PSB[7][:, 256:384], in1=fe_t[:, :], op=ALU.mult), reads=[R_c["fe"]], pwrites=[R_t])
        DVE.op(lambda: v.tensor_tensor(out=tmpE[:, :], in0=tmpE[:, :], in1=tmpB[:, :], op=ALU.add), reads=[R_t], pwrites=[R_t])
        DVE.op(lambda: v.tensor_tensor(out=cumf[:, 128:256], in0=PSB[7][:, 128:256], in1=tmpE[:, :], op=ALU.add), reads=[R_t], pwrites=[R_t])
        DVE.op(lambda: v.tensor_copy(out=tmpA[:, :], in_=PSB[6][:, 128:256]), reads=[PSR[6], R_t], pwrites=[R_t])
        for h in range(8):
            DVE.op(lambda h=h: v.tensor_tensor_scan(out=tB[:, h, :], data0=ones16[:, :], data1=tA[:, h, :], initial=0.0,
                                                    op0=ALU.mult, op1=ALU.add), reads=[R_t], pwrites=[R_t])
        DVE.op(lambda: v.tensor_tensor(out=tmpC[:, :], in0=tmpB[:, :], in1=tmpA[:, :], op=ALU.subtract), reads=[R_t], pwrites=[R_t])
        DVE.op(lambda: v.tensor_tensor(out=cumf[:, 33 * 8:49 * 8], in0=PSB[6][:, 0:128], in1=tmpC[:, :], op=ALU.add), reads=[R_t], pwrites=[R_t])
        DVE.op(lambda: v.tensor_tensor(out=cumf[0:NS, 256:264], in0=PSB[6][0:NS, 256:264], in1=tmpB[0:NS, 120:128], op=ALU.add),
               reads=[R_t], pwrites=[R_t])
        R_cum2 = Res("cum2")
        DVE.op(lambda: v.tensor_scalar(out=ncum[:, :, :].rearrange("p j h -> p (j h)"), in0=cumf[:, :], scalar1=-1.0 / SCALE,
                                       scalar2=None, op0=ALU.mult), reads=[R_t], writes=[R_cum2])
        MCUM = Bump(M_OFF + MB - 13312, 13312)
        xk = MCUM.take([392], F32); xq = MCUM.take([136], F32)
        xr1 = MCUM.take([392], F32); xf = MCUM.take([392], F32)
        spl = [MCUM.take([528], BF16) for i in range(3)]
        rows6 = MCUM.take([3, 6, 128], BF16)
        R_spl = Res("spl", prev=prevA)
        DVE.op(lambda: v.tensor_copy(out=xk[:, :].rearrange("p (h b) -> p b h", h=8), in_=ncum[:, :, :]), reads=[R_cum2], writes=[R_spl])
        DVE.op(lambda: v.memset(xq[:, :], 0.0), pwrites=[R_spl])
        xq3 = xq[:, :].rearrange("p (h b) -> p b h", h=8)
        DVE.op(lambda: v.tensor_scalar(out=xq3[:, 0:16, :], in0=cum[:, 0:16, :], scalar1=1.0 / SCALE, scalar2=None, op0=ALU.mult),
               reads=[R_t, R_spl], pwrites=[R_spl])
        DVE.op(lambda: v.tensor_scalar(out=xq3[0:NS, 16, :], in0=cum[0:NS, 32, :], scalar1=1.0 / SCALE, scalar2=None, op0=ALU.mult),
               reads=[R_t, R_spl], pwrites=[R_spl])
        for (src, c0, n) in ((xk, 0, 392), (xq, 392, 136)):
            cur = src
            for si in range(3):
                DVE.op(lambda cur=cur, si=si: v.tensor_copy(out=spl[si][:, c0:c0 + n], in_=cur[:, 0:n]), reads=[R_spl], pwrites=[R_spl])
                if si < 2:
                    DVE.op(lambda si=si: v.tensor_copy(out=xf[:, 0:n], in_=spl[si][:, c0:c0 + n]), reads=[R_spl], pwrites=[R_spl])
                    DVE.op(lambda cur=cur: v.tensor_tensor(out=xr1[:, 0:n], in0=cur[:, 0:n], in1=xf[:, 0:n], op=ALU.subtract),
                           reads=[R_spl], pwrites=[R_spl])
                    cur = xr1
        cumv.update(R_spl=R_spl, spl=spl, rows6=rows6, R_cum2=R_cum2)
        prevA_ext.extend([R_spl])

    def cum_stage4():
        R_spl, spl, rows6 = cumv['R_spl'], cumv['spl'], cumv['rows6']
        R_rows = Res("rows", prev=prevA)
        chunks_t = [(0, 128), (128, 128), (256, 128), (384, 8), (392, 128), (520, 8)]
        NCK_s = dscr("NCK_s", [3, 392, 128]); CQ3_s = dscr("CQ3_s", [3, 136, 128])
        R_NCKs, R_CQ3s = Res("NCK_s"), Res("CQ3_s")
        for si in range(3):
            pbank = PSB[5 + si][:, :].bitcast(BF16)
            for ci, (c0, n) in enumerate(chunks_t):
                PE.op(lambda si=si, ci=ci, c0=c0, n=n, pbank=pbank: nc.tensor.transpose(pbank[0:n, ci * 128:(ci + 1) * 128], spl[si][:, c0:c0 + n], ident_b[:, :]),
                      reads=[R_spl, R_c["ident_b"]], writes=[PSR[5 + si]] if ci == 0 else [], pwrites=[] if ci == 0 else [PSR[5 + si]],
                      sig=(ci == len(chunks_t) - 1))
            for ci, (c0, n) in enumerate(chunks_t):
                DVE.op(lambda si=si, ci=ci, n=n, pbank=pbank: v.tensor_copy(out=rows6[0:n, si, ci, :], in_=pbank[0:n, ci * 128:(ci + 1) * 128]),
                       reads=[PSR[5 + si]], pwrites=[R_rows])
        first = True
        for si in range(3):
            for ci, (c0, n) in enumerate(chunks_t):
                if c0 < 392:
                    dst, rd = NCK_s[si, c0:c0 + n, :], R_NCKs
                else:
                    dst, rd = CQ3_s[si, c0 - 392:c0 - 392 + n, :], R_CQ3s
                SP.dma(dst, rows6[0:n, si, ci, :], dsem_of(R_rows), reads=[R_rows], pwrites=[rd])

        cumv.update(R_NCKs=R_NCKs, R_CQ3s=R_CQ3s, NCK_s=NCK_s, CQ3_s=CQ3_s)
        prevA_ext.extend([R_rows])

    RING0 = W_OFF
    ring = [view(RING0 + s * 4096, [KC, 128], BF16) for s in range(8)]
    R_ring = [Res("ring%d" % s, prev=R_WK) for s in range(8)]
    WVB = view(W_OFF + 32768, [KC, 1024], BF16)
    R_WVB = [Res("WVB%d" % k, prev=R_WV) for k in range(KC)]
    for k in range(KC):
        R_WVB[k].dsem = ds_wv[k // 4]
    w_in_r = w_in.rearrange("(k p) c -> p k c", p=128)

    chunksC1 = [("q", h, OFF_Q + h * 128) for h in range(8)] + [("ga", h, OFF_GA + h * 128) for h in range(8)]
    chunksC2 = []
    for gi in range(8):
        chunksC2.append(("u", gi, OFF_U + gi * 128))
        chunksC2.append(("gb", gi, OFF_GB + gi * 128))
    allchunks = chunksC1 + chunksC2
    ring_state = {"next": 0}

    def ring_load(ci):
        kind, idx, col = allchunks[ci]
        s = ci % 8
        load(POOL, ring[s][:, :, :], w_in_r[:, :, col:col + 128], R_ring[s])

    tiles = []
    for (c0_, n_) in [(0, 416), (416, 416), (832, 416), (1248, 416), (1664, 400)]:
        blks = sorted(set(min(c // 128, NB) for c in range(c0_, c0_ + n_, 16)))
        tiles.append((c0_, n_, [R_H[b_] for b_ in blks]))
    bank_ctr = {"n": 0}

    def next_bank(lo=0, hi=6):
        b = lo + bank_ctr["n"] % (hi - lo)
        bank_ctr["n"] += 1
        return b

    MC = Bump(M_OFF, MB)
    vn = MC.take([NB, 1024], BF16)
    vn_s = MC.take([1024], BF16)
    R_vn = [Res("vn%d" % i, prev=prevA) for i in range(NB + 1)]
    stg = [MC.take([512], BF16) for _ in range(4)]
    R_stg = [Res("stg%d" % i, prev=prevA) for i in range(4)]
    mark_c = MC.off

    def c_chunk(ci, dst_sb=None, r_dst=None):
        kind, idx, col = allchunks[ci]
        s = ci % 8
        for ti, (c0, n, rh) in enumerate(tiles):
            b = next_bank()
            for k in range(KC):
                PE.op(lambda k=k, b=b, c0=c0, n=n: mm(PSB[b][:, 0:n], ring[s][:, k, :], H[:, k, c0:c0 + n],
                                                       start=(k == 0), stop=(k == KC - 1)),
                      reads=rh + [R_ring[s]], writes=[PSR[b]] if k == 0 else [], pwrites=[] if k == 0 else [PSR[b]],
                      sig=(k == KC - 1))
            func = {"q": AF.Copy, "ga": AF.Silu, "u": AF.Gelu_apprx_tanh, "gb": AF.Silu}[kind]
            if kind in ("q", "ga"):
                u = c_chunk.ctr % 4
                c_chunk.ctr += 1
                ACT.op(lambda b=b, n=n, u=u: nc.scalar.activation(out=stg[u][:, 0:n], in_=PSB[b][:, 0:n], func=func),
                       reads=[PSR[b]], writes=[R_stg[u]])
                dst = (QT_s if kind == "q" else GA_s)[idx, :, c0:c0 + n]
                store(SP, dst, stg[u][:, 0:n], R_stg[u], R_QTs if kind == "q" else R_GAs, final=False)
            else:
                ACT.op(lambda b=b, n=n, c0=c0: nc.scalar.activation(out=dst_sb[:, c0:c0 + n], in_=PSB[b][:, 0:n], func=func),
                       reads=[PSR[b]], writes=[r_dst] if ti == 0 else [], pwrites=[] if ti == 0 else [r_dst])
    c_chunk.ctr = 0

    for ci in range(4):
        ring_load(ci)
    for ci in range(16):
        if ci + 4 < len(allchunks):
            ring_load(ci + 4)
        load(POOL, WVB[:, ci, :], w_in[ci * 128:(ci + 1) * 128, OFF_VB:OFF_VB + 1024], R_WVB[ci])
        c_chunk(ci)
        if ci == 1:
            cum_stage2()
            cum_stage3()
        if ci == 3:
            cum_stage4()
    R_cum2, R_NCKs, R_CQ3s, NCK_s, CQ3_s = cumv["R_cum2"], cumv["R_NCKs"], cumv["R_CQ3s"], cumv["NCK_s"], cumv["CQ3_s"]

    if stop_after == "C1":
        drain(); return nc
    regroup(R_WVB)
    gxs = [MC.take([1024], F32) for _ in range(2)]
    vt = MC.take([1024], F32)
    lngb = MC.take([1024], F32)
    lnbb = MC.take([1024], F32)
    prevA = prevA + prevA_ext
    R_gxs = [Res("gx%d" % i, prev=prevA) for i in range(2)]
    R_vt = Res("vt", prev=prevA)
    R_gx, R_junk = R_gxs[0], R_gxs[1]
    R_ln = Res("ln", prev=prevA)
    R_st = [Res("st%d" % i) for i in range(2)]
    load(SP, lngb[:, :], ln_g.partition_broadcast(128), R_ln)
    ev = SP.dma(lnbb[:, :], ln_b.partition_broadcast(128), dsem_of(R_ln), pwrites=[R_ln])
    junkP = [PSB[4], PSB[5]]

    def b_vars(n):
        rows = 128 if n < NB else NS
        so = 16 + (n % 2) * 8
        return rows, n * 128, 2 * (n % 2), [sml[0:rows, so + i:so + i + 1] for i in range(8)], R_st[n % 2], gxs[n % 2], R_gxs[n % 2]

    def b_front(n):
        rows, c0, pb, (s1a, s1b, s2, msum, mean, msq, var, rstd), rst, gx, rgx = b_vars(n)
        for k in range(KC):
            for c in range(2):
                PE.op(lambda k=k, c=c: mm(PSB[pb + c][0:rows, :], H[:, k, c0:c0 + rows], WVB[:, k, c * 512:(c + 1) * 512],
                                          start=(k == 0), stop=(k == KC - 1)),
                      reads=[R_H[n], R_WVB[k]], writes=[PSR[pb + c]] if k == 0 else [], pwrites=[] if k == 0 else [PSR[pb + c]],
                      sig=(k == KC - 1))
        ACT.op(lambda: nc.scalar.activation(out=gx[0:rows, 0:512], in_=PSB[pb][0:rows, :], func=AF.Gelu_apprx_tanh, accum_out=s1a),
               reads=[PSR[pb]], writes=[rgx, rst])
        ACT.op(lambda: nc.scalar.activation(out=gx[0:rows, 512:1024], in_=PSB[pb + 1][0:rows, :], func=AF.Gelu_apprx_tanh, accum_out=s1b),
               reads=[PSR[pb + 1]], pwrites=[rgx, rst])
        for c in range(2):
            sq = s2 if c == 0 else msq
            ACT.op(lambda c=c, sq=sq: nc.scalar.activation(out=junkP[c][0:rows, :], in_=gx[0:rows, c * 512:(c + 1) * 512], func=AF.Square,
                                                           accum_out=sq),
                   reads=[rgx], writes=[PSR[4 + c]], pwrites=[rst])
        DVE.op(lambda: v.tensor_tensor(out=msum, in0=s1a, in1=s1b, op=ALU.add), reads=[rst], pwrites=[rst])
        DVE.op(lambda: v.tensor_scalar(out=mean, in0=msum, scalar1=1.0 / 1024, scalar2=None, op0=ALU.mult), reads=[rst], pwrites=[rst])
        DVE.op(lambda: v.tensor_tensor(out=s2, in0=s2, in1=msq, op=ALU.add), reads=[rst], pwrites=[rst])
        DVE.op(lambda: v.tensor_tensor(out=msq, in0=mean, in1=mean, op=ALU.mult), reads=[rst], pwrites=[rst])
        DVE.op(lambda: v.scalar_tensor_tensor(out=var, in0=s2, scalar=1.0 / 1024, in1=msq, op0=ALU.mult, op1=ALU.subtract),
               reads=[rst], pwrites=[rst])

    def b_sqrt(n):
        rows, c0, pb, (s1a, s1b, s2, msum, mean, msq, var, rstd), rst, gx, rgx = b_vars(n)
        ACT.op(lambda: nc.scalar.activation(out=var, in_=var, func=AF.Sqrt, bias=LN_EPS), reads=[rst], pwrites=[rst])

    def b_back(n):
        rows, c0, pb, (s1a, s1b, s2, msum, mean, msq, var, rstd), rst, gx, rgx = b_vars(n)
        DVE.op(lambda: v.reciprocal(out=rstd, in_=var), reads=[rst], pwrites=[rst])
        DVE.op(lambda: v.tensor_scalar(out=vt[0:rows, :], in0=gx[0:rows, :], scalar1=mean, scalar2=rstd, op0=ALU.subtract, op1=ALU.mult),
               reads=[rgx, rst], writes=[R_vt])
        POOL.op(lambda: g.tensor_tensor(out=vt[0:rows, :], in0=vt[0:rows, :], in1=lngb[0:rows, :], op=ALU.mult),
                reads=[R_vt, R_ln], writes=[R_vt])
        if n < NB:
            DVE.op(lambda: v.tensor_tensor(out=vn[:, n, :], in0=vt[:, :], in1=lnbb[:, :], op=ALU.add),
                   reads=[R_vt, R_ln], writes=[R_vn[n]])
        else:
            DVE.op(lambda: v.tensor_tensor(out=vt[0:rows, :], in0=vt[0:rows, :], in1=lnbb[0:rows, :], op=ALU.add),
                   reads=[R_vt, R_ln], writes=[R_vt])
            store(SP, gv_smp, vt[0:rows, :], R_vt)
            DVE.op(lambda: v.tensor_copy(out=vn_s[0:rows, :], in_=vt[0:rows, :]), reads=[R_vt], writes=[R_vn[NB]])

    for n0 in range(0, NB + 1, 2):
        grp_b = [n for n in (n0, n0 + 1) if n < NB + 1]
        for n in grp_b:
            b_front(n)
        for n in grp_b:
            b_sqrt(n)
        for n in grp_b:
            b_back(n)

    if stop_after == "B":
        drain(); return nc
    prevB = [R_gx, R_vt, R_junk, R_ln] + prevA
    MC2 = Bump(M_OFF + mark_c, MB - mark_c)
    Usb = [MC2.take([TOWN], BF16) for _ in range(2)]
    GBsb = [MC2.take([TOWN], BF16) for _ in range(2)]
    t1 = [MC2.take([512], F32) for _ in range(2)]
    R_U = [Res("U%d" % i, prev=prevB) for i in range(2)]
    R_GB = [Res("GB%d" % i, prev=prevB) for i in range(2)]
    R_t1 = [Res("t1_%d" % i, prev=prevB) for i in range(2)]
    oTB = view(W_OFF + HB // 2, [8, TOWN], BF16)
    R_oTB = [Res("oTB%d" % i, prev=R_WVB + R_WV + R_WK[KC - 1:KC]) for i in range(8)]
    mixctr = {"n": 0}

    def mixing(gi):
        p = gi % 2
        for ti in range(5):
            b = next_bank()
            if ti < 4:
                for i in range(4):
                    blk = ti * 4 + i
                    PE.op(lambda blk=blk, i=i, b=b: mm(PSB[b][:, i * 128:(i + 1) * 128], vn[:, blk, gi * 128:(gi + 1) * 128], WT[:, gi, :],
                                                        start=True, stop=True),
                          reads=[R_vn[blk], R_c["WT"]], writes=[PSR[b]] if i == 0 else [], pwrites=[] if i == 0 else [PSR[b]],
                          sig=(i == 3))
                n, c0 = 512, ti * 512
                nb_ = 4
                bs_ap = bsb[:, gi, :].unsqueeze(1).broadcast_to([128, 4, 128])
                pin = PSB[b][:, :].rearrange("p (a t) -> p a t", a=4)
            else:
                PE.op(lambda b=b: mm(PSB[b][:, 0:NS], vn_s[0:NS, gi * 128:(gi + 1) * 128], WT[0:NS, gi, 0:NS], start=True, stop=True),
                      reads=[R_vn[NB], R_c["WT"]], writes=[PSR[b]])
                n, c0 = NS, 2048
                bs_ap = bsb[:, gi, 0:NS]
                pin = PSB[b][:, 0:NS]
            u = mixctr["n"] % 2
            mixctr["n"] += 1
            if ti < 4:
                t1v = t1[u][:, :].rearrange("p (a t) -> p a t", a=4)
            else:
                t1v = t1[u][:, 0:NS]
            DVE.op(lambda: v.tensor_tensor(out=t1v, in0=pin, in1=bs_ap, op=ALU.add), reads=[PSR[b], R_c["bsb"]], writes=[R_t1[u]])
            DVE.op(lambda: v.tensor_tensor(out=t1[u][:, 0:n], in0=t1[u][:, 0:n], in1=Usb[p][:, c0:c0 + n], op=ALU.mult),
                   reads=[R_t1[u], R_U[p]], writes=[R_t1[u]])
            DVE.op(lambda: v.tensor_tensor(out=oTB[:, gi, c0:c0 + n], in0=t1[u][:, 0:n], in1=GBsb[p][:, c0:c0 + n], op=ALU.mult),
                   reads=[R_t1[u], R_GB[p]], writes=[R_oTB[gi]] if ti == 0 else [], pwrites=[] if ti == 0 else [R_oTB[gi]])

    for ci in range(16, 32):
        if ci + 4 < len(allchunks):
            ring_load(ci + 4)
        kind, gi, col = allchunks[ci]
        if kind == "u":
            c_chunk(ci, Usb[gi % 2], R_U[gi % 2])
            if gi >= 1:
                mixing(gi - 1)
        else:
            c_chunk(ci, GBsb[gi % 2], R_GB[gi % 2])
    H2 = Bump(H_OFF + HB // 2, HB // 2)
    KTh0 = H2.take([NTOK_S], BF16)
    Vh0 = H2.take([33, 128], BF16)
    Qh0 = H2.take([TOWN], BF16)
    R_KTh0 = Res("KTh0", prev=R_H)
    R_Vh0 = Res("Vh0", prev=R_H)
    R_Qh0 = Res("Qh0", prev=R_H)

    def e_loads_big(h, KT_t, V_t, Q_t, rK, rV, rQ):
        load(SP, KT_t[:, :], KT_s[h, :, :], rK, extra_reads=[R_KTs])
        for q4 in range(4):
            SP.dma(V_t[:, q4 * 8:(q4 + 1) * 8, :],
                   V_s[q4 * 1024:(q4 + 1) * 1024, h * 128:(h + 1) * 128].rearrange("(b t) c -> t b c", t=128),
                   dsem_of(rV), reads=[R_Vs], writes=[rV] if q4 == 0 else [], pwrites=[] if q4 == 0 else [rV])
        SP.dma(V_t[0:NS, 32, :], V_s[4096:4096 + NS, h * 128:(h + 1) * 128], dsem_of(rV), pwrites=[rV])
        load(SP, Q_t[:, :], QT_s[h, :, :], rQ, extra_reads=[R_QTs])
    e_loads_big(0, KTh0, Vh0, Qh0, R_KTh0, R_Vh0, R_Qh0)
    mixing(7)

    if stop_after == "C2":
        drain(); return nc
    prevC = R_vn + R_stg + [R_gx, R_vt, R_junk, R_ln] + R_U + R_GB + R_t1
    prevH = R_H
    ME = Bump(M_OFF, MB)
    KTh = [KTh0, ME.take([NTOK_S], BF16)]
    Vh = [Vh0, ME.take([33, 128], BF16)]
    Qh = [Qh0, ME.take([TOWN], BF16)]
    GAh1 = H2.take([TOWN], BF16)
    GAh = [GAh1, GAh1]
    KTc = H2.take([PAST], BF16)
    kc = ME.take([16, 128], BF16)
    vc = ME.take([16, 128], BF16)
    Pb = [ME.take([512], BF16) for _ in range(5)]
    ptmp = H2.take([512], BF16)
    ptmp2 = H2.take([512], BF16)
    Pb.append(H2.take([512], BF16))
    R_ptmp = Res("ptmp", prev=R_H)
    R_ptmp2 = Res("ptmp2", prev=R_H)
    grp = {}
    fin = [ME.take([512], F32) for _ in range(2)]
    pv = [prevH, prevC]
    R_KTh = [R_KTh0, Res("KTh1", prev=prevC)]
    R_Vh = [R_Vh0, Res("Vh1", prev=prevC)]
    R_Qh = [R_Qh0, Res("Qh1", prev=prevC)]
    R_GAh1 = Res("GAh", prev=prevH)
    R_GAh = [R_GAh1, R_GAh1]
    R_KTc = Res("KTc", prev=prevH)
    R_kc, R_vc = Res("kc", prev=prevC), Res("vc", prev=prevC)
    A_all = ME.take([4096], BF16)
    Bh = [ME.take([2048], BF16) for _ in range(2)]
    As = ME.take([2176 + 128], BF16)
    R_Aall = Res("A_all", prev=prevC)
    R_Bh = [Res("Bh%d" % i, prev=prevC) for i in range(2)]
    R_Bh_ms = [Res("Bh_ms%d" % i) for i in range(2)]
    R_As_ms = Res("As_ms")
    R_As = Res("As", prev=prevC)
    R_Aall_ms = Res("A_all_ms", prev=prevC)
    DVE.op(lambda: v.memset(A_all[:, :], 1.0), writes=[R_Aall_ms, R_Aall])
    DVE.op(lambda: v.memset(As[:, :], 1.0), writes=[R_As])
    for hh in range(8):
        SP.dma(A_all[6 * hh + 3:6 * hh + 6, :], NCK_s[:, hh * 49:hh * 49 + 32, :].rearrange("s b t -> s (b t)"), dsem_of(R_Aall),
               reads=[R_NCKs, R_Aall_ms], pwrites=[R_Aall])
    ones_src3 = bass.AP(tensor=ones_b.tensor if hasattr(ones_b, "tensor") else ones_b, offset=0, ap=[[128, 3], [0, 16], [1, 128]])
    R_Pb = [Res("Pb%d" % i, prev=prevC) for i in range(5)] + [Res("Pb5", prev=R_H)]
    R_fin = [Res("fin%d" % i, prev=prevC) for i in range(2)]
    oTA = view(H_OFF, [8, TOWN], BF16)
    R_oTA = [Res("oTA%d" % i, prev=prevH) for i in range(8)]

    wo = [view(RING0 + k * 4096, [D], BF16) for k in range(8)]
    R_wo = [Res("wo%d" % k, prev=R_ring) for k in range(8)]
    for k in range(8):
        R_wo[k].dsem = R_ring[k].dsem
    wo2 = [view(H_OFF + HB // 2 + k * 4096, [D], BF16) for k in range(8)]
    R_wo2 = []
    ds_wo2 = [DSem(nc, "d_wo2_%d" % i) for i in range(2)]

    def wo_load(k):
        load(POOL, wo[k][:, :], w_out[k * 128:(k + 1) * 128, :], R_wo[k])

    def e_loads(h):
        p = h % 2
        if h > 0:
            e_loads_big(h, KTh[p], Vh[p], Qh[p], R_KTh[p], R_Vh[p], R_Qh[p])
        ACT.op(lambda: nc.scalar.memzero(Bh[p][:, :]), writes=[R_Bh_ms[p], R_Bh[p]])
        SP.dma(Bh[p][6 * h:6 * h + 3, :], CQ3_s[:, h * 17:h * 17 + 16, :].rearrange("s b t -> s (b t)"), dsem_of(R_Bh[p]),
               reads=[R_CQ3s, R_Bh_ms[p]], pwrites=[R_Bh[p]])
        SP.dma(Bh[p][6 * h + 3:6 * h + 6, :].rearrange("r (a t) -> r a t", t=128), ones_src3, dsem_of(R_Bh[p]),
               reads=[R_c["ones_b"], R_Bh_ms[p]], pwrites=[R_Bh[p]])

    ones_src1 = bass.AP(tensor=ones_b.tensor if hasattr(ones_b, "tensor") else ones_b, offset=0, ap=[[128, 3], [1, 128]])

    def e_loads2(h):
        load(SP, GAh1[:, :], GA_s[h, :, :], R_GAh1, extra_reads=[R_GAs])
        ACT.op(lambda: nc.scalar.memzero(As[:, 2176:2304]), writes=[R_As_ms, R_As])
        SP.dma(As[3:6, 0:2176], NCK_s[:, h * 49 + 32:h * 49 + 49, :].rearrange("s b t -> s (b t)"), dsem_of(R_As),
               reads=[R_NCKs, R_As_ms], pwrites=[R_As])
        SP.dma(As[0:3, 2176:2304], CQ3_s[:, h * 17 + 16, :], dsem_of(R_As), reads=[R_CQ3s, R_As_ms], pwrites=[R_As])
        SP.dma(As[3:6, 2176:2304], ones_src1, dsem_of(R_As), reads=[R_c["ones_b"], R_As_ms], pwrites=[R_As])
        load(POOL, kc[:, :, :], ck[:, h * 128:(h + 1) * 128].rearrange("(b t) c -> t b c", t=128), R_kc)
        load(POOL, vc[:, :, :], cv[:, h * 128:(h + 1) * 128].rearrange("(b t) c -> t b c", t=128), R_vc)

    units = []
    tiles_e = []
    for h in range(8):
        for T in range(5):
            tid = len(tiles_e)
            tiles_e.append((h, T))
            if T == 3:
                units.append(dict(h=h, T=T, tid=tid, kind="ktc", j=0, first=False, last=False, smp=False))
                units.append(dict(h=h, T=T, tid=tid, kind="ktc", j=1, first=False, last=False, smp=False))
            if T < 4:
                kbs = []
                for jp in range(4 * T + 4):
                    kbs.append(("own", jp))
                    kbs.append(("oth", jp))
            else:
                kbs = [("cacheall", 0), ("new", 16)]
            for i, (kk, jp) in enumerate(kbs):
                units.append(dict(h=h, T=T, tid=tid, kind=kk, j=jp, first=(i == 0), last=(i == len(kbs) - 1), smp=(T == 4)))

    NSB = 5
    NPB = len(Pb)

    def e_S(i):
        u = units[i]
        h, T, p = u["h"], u["T"], u["h"] % 2
        b = i % NSB
        u["sb"] = b
        if u["kind"] == "ktc":
            half = u["j"]
            for q4 in range(8):
                blk = half * 8 + q4
                PE.op(lambda blk=blk, q4=q4: nc.tensor.transpose(
                    PSB[b][:, :].bitcast(BF16)[:, q4 * 128:(q4 + 1) * 128], kc[:, blk, :], ident_b[:, :]),
                    reads=[R_kc, R_c["ident_b"]], writes=[PSR[b]] if q4 == 0 else [], pwrites=[] if q4 == 0 else [PSR[b]],
                    sig=(q4 == 7))
        elif not u["smp"]:
            jlo = max(u["j"], 4 * T)
            off = (jlo - 4 * T) * 128
            n = 512 - off
            kb = (u["j"] if u["kind"] == "own" else 16 + u["j"])
            kcol = kb * 128
            q0 = 4 * T * 128 + off
            u.update(off=off, n=n, kb=kb, rows=128)
            diag = u["j"] >= 4 * T
            PE.op(lambda: mm(PSB[b][:, off:off + n], KTh[p][:, kcol:kcol + 128], Qh[p][:, q0:q0 + n], start=True, stop=False),
                  reads=[R_KTh[p], R_Qh[p]], writes=[PSR[b]], sig=False)
            PE.op(lambda: mm(PSB[b][:, off:off + n], A_all[:, kcol:kcol + 128], Bh[p][:, q0:q0 + n], start=False, stop=not diag),
                  reads=[R_Aall, R_Bh[p]], pwrites=[PSR[b]], sig=not diag)
            if diag:
                mi = 0 if u["kind"] == "own" else 1 + u["j"] % 2
                PE.op(lambda: mm(PSB[b][:, off:off + 128], ident_b[:, :], mneg_b[:, mi, :], start=False, stop=True),
                      reads=[R_c["ident_b"], R_mnb], pwrites=[PSR[b]])
        elif u["kind"] == "cacheall":
            u.update(off=0, n=16 * NS, rows=128)
            for jj in range(16):
                PE.op(lambda jj=jj: mm(PSB[b][:, jj * NS:(jj + 1) * NS], KTc[:, jj * 128:(jj + 1) * 128], Qh[p][:, 2048:2048 + NS],
                                       start=(jj == 0), stop=False, skip_group_check=True),
                      reads=[R_KTc, R_Qh[p]], writes=[PSR[b]] if jj == 0 else [], pwrites=[] if jj == 0 else [PSR[b]], sig=False)
            for jj in range(16):
                kcol = (1 + jj) * 128
                PE.op(lambda jj=jj, kcol=kcol: mm(PSB[b][:, jj * NS:(jj + 1) * NS], As[:, kcol:kcol + 128], As[:, 2176:2176 + NS],
                                                 start=False, stop=(jj == 15), skip_group_check=True),
                      reads=[R_As], pwrites=[PSR[b]], sig=(jj == 15))
        else:
            u.update(off=0, n=NS, rows=NS)
            PE.op(lambda: mm(PSB[b][0:NS, 0:NS], KTh[p][:, 4096:4096 + NS], Qh[p][:, 2048:2048 + NS], start=True, stop=False),
                  reads=[R_KTh[p], R_Qh[p]], writes=[PSR[b]], sig=False)
            PE.op(lambda: mm(PSB[b][0:NS, 0:NS], As[:, 0:NS], As[:, 2176:2176 + NS], start=False, stop=False),
                  reads=[R_As], pwrites=[PSR[b]], sig=False)
            PE.op(lambda: mm(PSB[b][0:NS, 0:NS], ident_b[0:NS, 0:NS], mneg_b[0:NS, 0, 0:NS], start=False, stop=True),
                  reads=[R_c["ident_b"], R_mnb], pwrites=[PSR[b]])

    def e_soft(i):
        u = units[i]
        b = u["sb"]
        if u["kind"] == "ktc":
            half = u["j"]
            ACT.op(lambda: nc.scalar.activation(out=KTc[:, half * 1024:(half + 1) * 1024], in_=PSB[b][:, :].bitcast(BF16), func=AF.Copy),
                   reads=[PSR[b]], writes=[R_KTc] if half == 0 else [], pwrites=[] if half == 0 else [R_KTc])
            return
        off, n, rows = u["off"], u["n"], u["rows"]
        pi = i % NPB
        u["pi"] = pi
        tp = u["tid"] % 2
        ACT.op(lambda: nc.scalar.activation(out=Pb[pi][0:rows, off:off + n], in_=PSB[b][0:rows, off:off + n], func=AF.Exp, scale=SCALE),
               reads=[PSR[b]], writes=[R_Pb[pi]])
        if not u["smp"]:
            if u["kind"] == "oth":
                pa = units[i - 1]["pi"]
                m = u["j"]
                if m % 2 == 0:
                    DVE.op(lambda: v.tensor_tensor(out=ptmp[:, off:off + n], in0=Pb[pa][:, off:off + n], in1=Pb[pi][:, off:off + n], op=ALU.add),
                           reads=[R_Pb[pa], R_Pb[pi]], writes=[R_ptmp])
                    grp["off"], grp["n"] = off, n
                else:
                    oa, na = grp["off"], grp["n"]
                    DVE.op(lambda: v.tensor_tensor(out=ptmp2[:, off:off + n], in0=Pb[pa][:, off:off + n], in1=Pb[pi][:, off:off + n], op=ALU.add),
                           reads=[R_Pb[pa], R_Pb[pi]], writes=[R_ptmp2])
                    if m == 1 and off == 0:
                        DVE.op(lambda: v.tensor_tensor(out=fin[tp][:, :], in0=ptmp[:, :], in1=ptmp2[:, :], op=ALU.add),
                               reads=[R_ptmp, R_ptmp2], writes=[R_fin[tp]])
                    elif m == 1:
                        ACT.op(lambda: nc.scalar.activation(out=fin[tp][:, :], in_=ptmp[:, :], func=AF.Copy), reads=[R_ptmp], writes=[R_fin[tp]])
                        DVE.op(lambda: v.tensor_tensor(out=fin[tp][:, off:off + n], in0=fin[tp][:, off:off + n], in1=ptmp2[:, off:off + n], op=ALU.add),
                               reads=[R_ptmp2, R_fin[tp]], writes=[R_fin[tp]])
                    else:
                        DVE.op(lambda: v.tensor_tensor(out=ptmp[:, off:off + n], in0=ptmp[:, off:off + n], in1=ptmp2[:, off:off + n], op=ALU.add),
                               reads=[R_ptmp2, R_ptmp], writes=[R_ptmp])
                        DVE.op(lambda: v.tensor_tensor(out=fin[tp][:, oa:oa + na], in0=fin[tp][:, oa:oa + na], in1=ptmp[:, oa:oa + na], op=ALU.add),
                               reads=[R_ptmp, R_fin[tp]], writes=[R_fin[tp]])

    def e_PV(i):
        u = units[i]
        if u["kind"] == "ktc":
            return
        h, T, p = u["h"], u["T"], u["h"] % 2
        off, n, pi, rows = u["off"], u["n"], u["pi"], u["rows"]
        tp = u["tid"] % 2
        ab, db = 5 + tp, 7
        if not u["smp"]:
            PE.op(lambda: mm(PSB[ab][:, off:off + n], Vh[p][:, u["kb"], :], Pb[pi][:, off:off + n], start=u["first"], stop=u["last"]),
                  reads=[R_Vh[p], R_Pb[pi]], writes=[PSR[ab]] if u["first"] else [], pwrites=[] if u["first"] else [PSR[ab]], sig=True)
            if u["last"]:
                def _den(db=db, tp=tp):
                    PE.op(lambda: mm(PSB[db][:, :], ones_f[:, :], fin[tp][:, :], start=True, stop=True),
                          reads=[R_c["ones_f"], R_fin[tp]], writes=[PSR[db]])
                pending.append((i + 4, _den))
        elif u["kind"] == "cacheall":
            for jj in range(16):
                PE.op(lambda jj=jj: mm(PSB[ab][:, 0:NS], vc[:, jj, :], Pb[pi][:, jj * NS:(jj + 1) * NS], start=(jj == 0), stop=False),
                      reads=[R_vc, R_Pb[pi]], writes=[PSR[ab]] if jj == 0 else [], pwrites=[] if jj == 0 else [PSR[ab]], sig=False)
                PE.op(lambda jj=jj: mm(PSB[db][:, 0:NS], ones_b[:, :], Pb[pi][:, jj * NS:(jj + 1) * NS], start=(jj == 0), stop=False),
                      reads=[R_c["ones_b"], R_Pb[pi]], writes=[PSR[db]] if jj == 0 else [], pwrites=[] if jj == 0 else [PSR[db]],
                      sig=(jj == 15))
        else:
            PE.op(lambda: mm(PSB[ab][:, 0:NS], Vh[p][0:NS, 32, :], Pb[pi][0:NS, 0:NS], start=False, stop=True),
                  reads=[R_Vh[p], R_Pb[pi]], pwrites=[PSR[ab]], sig=False)
            PE.op(lambda: mm(PSB[db][:, 0:NS], ones_b[0:NS, :], Pb[pi][0:NS, 0:NS], start=False, stop=True),
                  reads=[R_c["ones_b"], R_Pb[pi]], pwrites=[PSR[db]], sig=True)
        if u["last"]:
            n_t = NS if u["smp"] else 512
            c0 = 2048 if u["smp"] else T * 512
            f = fin[tp]

            def _fin(db=db, ab=ab, tp=tp, n_t=n_t, c0=c0, f=f, h=h, T=T):
                DVE.op(lambda: v.reciprocal(out=f[:, 0:n_t], in_=PSB[db][:, 0:n_t]), reads=[PSR[db]], writes=[R_fin[tp]])
                DVE.op(lambda: v.tensor_tensor(out=f[:, 0:n_t], in0=PSB[ab][:, 0:n_t], in1=f[:, 0:n_t], op=ALU.mult),
                       reads=[PSR[ab], R_fin[tp]], writes=[R_fin[tp]])
                POOL.op(lambda: g.tensor_tensor(out=oTA[:, h, c0:c0 + n_t], in0=f[:, 0:n_t], in1=GAh1[:, c0:c0 + n_t], op=ALU.mult),
                        reads=[R_fin[tp], R_GAh1], writes=[R_oTA[h]] if T == 0 else [], pwrites=[] if T == 0 else [R_oTA[h]])
            if u["smp"]:
                _fin()
            else:
                pending.append((i + 4, _fin))

    pending = []

    def flush(upto):
        keep = []
        for (at, fn) in pending:
            if at <= upto:
                fn()
            else:
                keep.append((at, fn))
        pending[:] = keep

    e_loads(0)
    NU = len(units)
    LA = 4
    seen_heads = set()
    for i in range(NU + LA):
        if i < NU:
            e_S(i)
        jx = i - LA
        if jx >= 0:
            u = units[jx]
            if u["h"] not in seen_heads:
                flush(10 ** 9)
                seen_heads.add(u["h"])
                if u["h"] + 1 < 8:
                    e_loads(u["h"] + 1)
                e_loads2(u["h"])
                if u["h"] >= 1:
                    wo_load(u["h"] - 1)
                if u["h"] == 7:
                    wo_load(7)
                    for k in range(5):
                        R_wo2.append(Res("wo2_%d" % k, prev=[R_KTh[0], R_Vh[0], R_Qh[0]]))
                        R_wo2[k].dsem = ds_wo2[k // 4]
                        load(POOL, wo2[k][:, :], w_out[(8 + k) * 128:(9 + k) * 128, :], R_wo2[k])
            e_soft(jx)
            e_PV(jx)
            flush(jx)
    flush(10 ** 9)

    if stop_after == "E":
        drain(); return nc
    prevE2 = R_KTh[0:1] + R_Vh[0:1] + R_Qh[0:1] + [R_GAh1, R_KTc, R_ptmp, R_ptmp2, R_Pb[5]]
    prevEM = R_KTh[1:2] + R_Vh[1:2] + R_Qh[1:2] + [R_kc, R_vc, R_Aall] + R_Bh + [R_mnb, R_As] + R_Pb + R_fin
    R_wo2.extend([Res("wo2_%d" % k, prev=prevE2) for k in range(5, 8)])
    for k in range(5, 8):
        R_wo2[k].dsem = ds_wo2[k // 4]
        load(POOL, wo2[k][:, :], w_out[(8 + k) * 128:(9 + k) * 128, :], R_wo2[k])
    regroup(R_wo2)
    wo_all = wo + wo2
    R_wo_all = R_wo + R_wo2
    MF = Bump(M_OFF, MB)
    xr = [MF.take([D], F32) for _ in range(2)]
    yb = [MF.take([D], F32) for _ in range(2)]
    fgb = MF.take([D], F32)
    junkF = MF.take([D], BF16)
    R_xr = [Res("xr%d" % i, prev=prevEM) for i in range(2)]
    R_yb = [Res("yb%d" % i, prev=prevEM) for i in range(2)]
    R_fg = Res("fg", prev=prevEM)
    R_junkF = Res("junkF", prev=prevEM)
    R_ssF = [Res("ssF%d" % i) for i in range(2)]
    load(SP, fgb[:, :], final_g.partition_broadcast(128), R_fg)

    def f_load(n):
        rows = 128 if n < NB else NS
        src = x_own[n * 128:(n + 1) * 128, :] if n < NB else x_smp
        load(SP, xr[n % 2][0:rows, :], src, R_xr[n % 2])

    f_load(0)
    for n in range(NB + 1):
        rows = 128 if n < NB else NS
        c0 = n * 128
        p = n % 2
        if n + 1 < NB + 1:
            f_load(n + 1)
        for k in range(KC):
            lhs = oTA[:, k, c0:c0 + rows] if k < 8 else oTB[:, k - 8, c0:c0 + rows]
            rl = R_oTA[k] if k < 8 else R_oTB[k - 8]
            for cg in range(4):
                b = 4 * p + cg
                PE.op(lambda lhs=lhs, k=k, cg=cg, b=b: mm(PSB[b][0:rows, :], lhs, wo_all[k][:, cg * 512:(cg + 1) * 512],
                                                          start=(k == 0), stop=(k == KC - 1)),
                      reads=[rl, R_wo_all[k]], writes=[PSR[b]] if k == 0 else [], pwrites=[] if k == 0 else [PSR[b]],
                      sig=(k == KC - 1))
        for cg in range(4):
            b = 4 * p + cg
            DVE.op(lambda cg=cg, b=b: v.tensor_tensor(out=yb[p][0:rows, cg * 512:(cg + 1) * 512], in0=PSB[b][0:rows, :],
                                                      in1=xr[p][0:rows, cg * 512:(cg + 1) * 512], op=ALU.add),
                   reads=[PSR[b], R_xr[p]], writes=[R_yb[p]] if cg == 0 else [], pwrites=[] if cg == 0 else [R_yb[p]])
        so = 40 + p * 4
        ss, sd, rs = [sml[0:rows, so + i:so + i + 1] for i in range(3)]
        ACT.op(lambda: nc.scalar.activation(out=junkF[0:rows, :], in_=yb[p][0:rows, :], func=AF.Square, accum_out=ss),
               reads=[R_yb[p]], writes=[R_junkF, R_ssF[p]])
        ACT.op(lambda: nc.scalar.activation(out=sd, in_=ss, func=AF.Sqrt, scale=1.0 / D, bias=RMS_EPS), reads=[R_ssF[p]], writes=[R_ssF[p]])
        DVE.op(lambda: v.reciprocal(out=rs, in_=sd), reads=[R_ssF[p]], writes=[R_ssF[p]])
        DVE.op(lambda: v.scalar_tensor_tensor(out=yb[p][0:rows, :], in0=yb[p][0:rows, :], scalar=rs, in1=fgb[0:rows, :],
                                              op0=ALU.mult, op1=ALU.mult),
                reads=[R_yb[p], R_ssF[p], R_fg], writes=[R_yb[p]])
        dst = y_own[n * 128:(n + 1) * 128, :] if n < NB else y_smp
        store(SP, dst, yb[p][0:rows, :], R_yb[p])

    drain()
    return nc


_NC_CACHE = {}


def _own_blocks(p):
    gown = [2 * j + ((j + p) % 2) for j in range(NB)]
    goth = [2 * j + 1 - ((j + p) % 2) for j in range(NB)]
    return gown, goth


def make_in_maps(inputs):
    f32 = np.float32
    xp = np.ascontiguousarray(inputs["x_prompt"], dtype=f32)
    xs = np.ascontiguousarray(inputs["x_sample"], dtype=f32)
    cache_k = np.asarray(inputs["cache_k"], dtype=f32)
    cache_v = np.asarray(inputs["cache_v"], dtype=f32)
    cache_lf = np.asarray(inputs["cache_logf"], dtype=f32)
    shared = dict(
        w_in=np.ascontiguousarray(inputs["w_in"][0], dtype=f32), w_out=np.ascontiguousarray(inputs["w_out"][0], dtype=f32),
        norm_g=np.ascontiguousarray(inputs["norm_g"][0], dtype=f32), b_f=np.ascontiguousarray(inputs["b_f"][0], dtype=f32),
        ln_g=np.ascontiguousarray(inputs["ln_g"][0], dtype=f32), ln_b=np.ascontiguousarray(inputs["ln_b"][0], dtype=f32),
        w_s=np.ascontiguousarray(inputs["w_s"][0], dtype=f32), b_s=np.ascontiguousarray(inputs["b_s"][0], dtype=f32),
        final_g=np.ascontiguousarray(inputs["final_g"], dtype=f32))
    in_maps = []
    for c in range(8):
        b, p = c // 2, c % 2
        gown, goth = _own_blocks(p)
        xb = xp[b].reshape(32, 128, D)
        fo = np.zeros((128, 16, 8), f32)
        mno = np.zeros((2, 128, 128), f32)
        for j in range(NB):
            if (j + p) % 2 == 1:
                fo[:, j, :] = 1.0
        for r in range(2):
            mno[r] = 0.0 if (r + p) % 2 == 1 else NEG
        m = dict(shared)
        m.update(
            x_own=np.ascontiguousarray(xb[gown].reshape(NB * 128, D)),
            x_oth=np.ascontiguousarray(xb[goth].reshape(NB * 128, D)),
            x_smp=np.ascontiguousarray(xs[c]),
            ck=np.ascontiguousarray(cache_k[0, c].reshape(PAST, 1024)),
            cv=np.ascontiguousarray(cache_v[0, c].reshape(PAST, 1024)),
            clf=np.ascontiguousarray(cache_lf[0, c]),
            fo=fo.reshape(128, 128), fe=(1.0 - fo).reshape(128, 128), mno=mno)
        in_maps.append(m)
    return in_maps


def assemble(results):
    f32 = np.float32
    y_prompt = np.zeros((4, 32, 128, D), f32)
    k_prompt = np.zeros((1, 4, 32, 128, 8, 128), f32)
    v_prompt = np.zeros((1, 4, 32, 128, 8, 128), f32)
    lf_prompt = np.zeros((1, 4, 32, 128, 8), f32)
    y_sample = np.zeros((8, NS, D), f32)
    k_sample = np.zeros((1, 8, NS, 8, 128), f32)
    v_sample = np.zeros((1, 8, NS, 8, 128), f32)
    lf_sample = np.zeros((1, 8, NS, 8), f32)
    gv_sample = np.zeros((1, 8, NS, 1024), f32)
    for c in range(8):
        r = results[c]
        b, p = c // 2, c % 2
        gown, _ = _own_blocks(p)
        y_prompt[b, gown] = np.asarray(r["y_own"]).reshape(NB, 128, D)
        k_prompt[0, b, gown] = np.asarray(r["k_own"]).reshape(NB, 128, 8, 128)
        v_prompt[0, b, gown] = np.asarray(r["v_own"]).reshape(NB, 128, 8, 128)
        lf_prompt[0, b, gown] = np.asarray(r["lf_own"]).reshape(NB, 128, 8)
        y_sample[c] = np.asarray(r["y_smp"])
        k_sample[0, c] = np.asarray(r["k_smp"]).reshape(NS, 8, 128)
        v_sample[0, c] = np.asarray(r["v_smp"]).reshape(NS, 8, 128)
        lf_sample[0, c] = np.asarray(r["lf_smp"])
        gv_sample[0, c] = np.asarray(r["gv_smp"])
    return (y_prompt.reshape(4, 4096, D), y_sample, k_prompt.reshape(1, 4, 4096, 8, 128),
            v_prompt.reshape(1, 4, 4096, 8, 128), lf_prompt.reshape(1, 4, 4096, 8),
            k_sample, v_sample, lf_sample, gv_sample)


def kernel(**inputs):
    if "nc" not in _NC_CACHE:
        _NC_CACHE["nc"] = build_program()
    nc = _NC_CACHE["nc"]
    in_maps = make_in_maps(inputs)
    res = run_bass_kernel_spmd(nc, in_maps, core_ids=list(range(8)))
    return assemble(res.results)
```

```python
import numpy as np
import concourse.bass as bass
import concourse.mybir as mybir
from concourse.bass_utils import run_bass_kernel_spmd

F32 = mybir.dt.float32
BF16 = mybir.dt.bfloat16
AF = mybir.ActivationFunctionType
ALU = mybir.AluOpType

D = 2048
KC = 16
NB = 16
NS = 16
TOWN = NB * 128 + NS
PAST = 2048
DIN = 7176
OFF_Q, OFF_K, OFF_V, OFF_F, OFF_GA, OFF_U, OFF_VB, OFF_GB = 0, 1024, 2048, 3072, 3080, 4104, 5128, 6152
SCALE = 128 ** -0.5
RMS_EPS = 1e-6
LN_EPS = 1e-5
NEG = -1.0e6
NTOK_S = 2 * NB * 128 + NS


class Res:
    def __init__(self, name, prev=(), excl=False):
        self.name = name
        self.excl = excl
        self.w = {}
        self.r = {}
        self.dsem = None
        for p in prev:
            for d in (p.w, p.r):
                for k, ev in d.items():
                    if k not in self.r or self.r[k][1] < ev[1]:
                        self.r[k] = ev


def _merge(d, ev):
    k = id(ev[0])
    if k not in d or d[k][1] < ev[1]:
        d[k] = ev


class DSem:
    def __init__(self, nc, name):
        self.sem = nc.alloc_semaphore(name)
        self.cnt = 0


class Eng:
    def __init__(self, nc, eng, name, is_pe=False, compute=True):
        self.nc = nc
        self.eng = eng
        self.name = name
        self.is_pe = is_pe
        self.sem = nc.alloc_semaphore("sem_" + name) if compute else None
        self.cnt = 0
        self.seen = {}
        self.last_unsig = False

    def _wait(self, ev, raw=True):
        sem, val = ev
        if sem is self.sem and self.is_pe:
            return
        if self.seen.get(id(sem), 0) >= val:
            return
        self.eng.wait_ge(sem, val)
        self.seen[id(sem)] = val

    def _deps(self, reads, writes, pwrites):
        for r in reads:
            for ev in r.w.values():
                self._wait(ev, raw=True)
            if r.excl:
                for ev in r.r.values():
                    self._wait(ev, raw=False)
        for w in writes:
            for ev in w.w.values():
                self._wait(ev, raw=False)
            for ev in w.r.values():
                self._wait(ev, raw=False)
        for w in pwrites:
            for ev in w.r.values():
                self._wait(ev, raw=False)

    def _update(self, ev, reads, writes, pwrites):
        for w in writes:
            w.w = {id(ev[0]): ev}
            w.r = {}
        for w in pwrites:
            _merge(w.w, ev)
        for r in reads:
            _merge(r.r, ev)

    def op(self, fn, reads=(), writes=(), pwrites=(), sig=True):
        self._deps(reads, writes, pwrites)
        ins = fn()
        if sig:
            ins.then_inc(self.sem, 1)
            self.cnt += 1
            ev = (self.sem, self.cnt)
            self.last_unsig = False
        else:
            ev = (self.sem, self.cnt + 1)
            self.last_unsig = True
        self._update(ev, reads, writes, pwrites)
        return ev

    def dma(self, out, in_, dsem, reads=(), writes=(), pwrites=()):
        self._deps(reads, writes, pwrites)
        self.eng.dma_start(out=out, in_=in_).then_inc(dsem.sem, 16)
        dsem.cnt += 16
        ev = (dsem.sem, dsem.cnt)
        self._update(ev, reads, writes, pwrites)
        return ev


def build_program(debug=False, stop_after=None):
    nc = bass.Bass("TRN2", target_bir_lowering=False)
    all_dsems = []

    def din(name, shape):
        return nc.dram_tensor(name, list(shape), F32, kind="ExternalInput").ap()

    def dout(name, shape):
        return nc.dram_tensor(name, list(shape), F32, kind="ExternalOutput").ap()

    skind = "ExternalOutput" if debug else "Internal"

    def dscr(name, shape, dt=BF16):
        return nc.dram_tensor(name, list(shape), dt, kind=skind).ap()

    x_own = din("x_own", [NB * 128, D])
    x_oth = din("x_oth", [NB * 128, D])
    x_smp = din("x_smp", [NS, D])
    ck = din("ck", [PAST, 1024])
    cv = din("cv", [PAST, 1024])
    clf = din("clf", [PAST, 8])
    w_in = din("w_in", [D, DIN])
    w_out = din("w_out", [D, D])
    norm_g = din("norm_g", [D])
    b_f = din("b_f", [8])
    ln_g = din("ln_g", [1024])
    ln_b = din("ln_b", [1024])
    w_s = din("w_s", [8, 128, 128])
    b_s = din("b_s", [8, 128])
    final_g = din("final_g", [D])
    fo_in = din("fo", [128, 128])
    fe_in = din("fe", [128, 128])
    mno_in = din("mno", [2, 128, 128])

    y_own = dout("y_own", [NB * 128, D])
    y_smp = dout("y_smp", [NS, D])
    k_own = dout("k_own", [NB * 128, 1024])
    v_own = dout("v_own", [NB * 128, 1024])
    lf_own = dout("lf_own", [NB * 128, 8])
    k_smp = dout("k_smp", [NS, 1024])
    v_smp = dout("v_smp", [NS, 1024])
    lf_smp = dout("lf_smp", [NS, 8])
    gv_smp = dout("gv_smp", [NS, 1024])

    KT_s = dscr("KT_s", [8, 128, NTOK_S])
    V_s = dscr("V_s", [NTOK_S, 1024])
    QT_s = dscr("QT_s", [8, 128, TOWN])
    GA_s = dscr("GA_s", [8, 128, TOWN])

    PE = Eng(nc, nc.tensor, "pe", is_pe=True)
    ACT = Eng(nc, nc.scalar, "act")
    DVE = Eng(nc, nc.vector, "dve")
    POOL = Eng(nc, nc.gpsimd, "pool")
    SP = Eng(nc, nc.sync, "sp", compute=False)
    store_events = []

    def dsem_of(res):
        if res.dsem is None:
            res.dsem = DSem(nc, "d_" + res.name)
        if res.dsem not in all_dsems:
            all_dsems.append(res.dsem)
        return res.dsem

    def drain():
        for ds in all_dsems:
            if ds.cnt > 0:
                SP._wait((ds.sem, ds.cnt))
        for e in (PE, ACT, DVE, POOL):
            if e.cnt > 0:
                SP._wait((e.sem, e.cnt))

    def regroup(res_list):
        for r in res_list:
            ds = r.dsem
            r.w = {id(ds.sem): (ds.sem, ds.cnt)}

    def load(q, out, in_, res, extra_reads=()):
        return q.dma(out, in_, dsem_of(res), reads=list(extra_reads), writes=[res])

    def store(q, out, in_, res, dram_res=None, final=True):
        ev = q.dma(out, in_, dsem_of(res), reads=[res], pwrites=[dram_res] if dram_res is not None else [])
        if final:
            store_events.append(ev)
        return ev

    PSB = [nc.alloc_psum_tensor("psb%d" % i, [128, 512], F32) for i in range(8)]
    PSR = [Res("psb%d" % i, excl=True) for i in range(8)]

    def sb(name, shape, dt=F32):
        return nc.alloc_sbuf_tensor(name, list(shape), dt)

    ident_f = sb("ident_f", [128, 128]); ident_b = sb("ident_b", [128, 128], BF16)
    utri_f = sb("utri_f", [128, 128])
    ones_f = sb("ones_f", [128, 128]); ones_b = sb("ones_b", [128, 128], BF16)
    mneg_d = sb("mneg_d", [128, 128])
    mneg_o = sb("mneg_o", [128, 2, 128])
    mneg_b = sb("mneg_b", [128, 3, 128], BF16)
    fo_t = sb("fo_t", [128, 128]); fe_t = sb("fe_t", [128, 128])
    ng16 = sb("ng16", [16, 128]); gcol = sb("gcol", [128, 16])
    bfb = sb("bfb", [128, 8])
    bsb = sb("bsb", [128, 8, 128])
    WT = sb("WT", [128, 8, 128], BF16)
    wf = sb("wf", [128, 16, 8], BF16)
    tz = sb("tz", [128, 33, 8])
    lfc = sb("lfc", [128, 16, 8])
    cum = sb("cum", [128, 49, 8])
    ncum = sb("ncum", [128, 49, 8])
    tmpA = sb("tmpA", [128, 128]); tmpB = sb("tmpB", [128, 128]); tmpC = sb("tmpC", [128, 128])
    tmpD = sb("tmpD", [128, 128]); tmpE = sb("tmpE", [128, 128])
    ones16 = sb("ones16", [128, 16])
    sml = sb("sml", [128, 64])
    R_consts = Res("consts")
    R_tz = [Res("tz%d" % i) for i in range(33)]
    R_cum = Res("cum")

    rem = nc.sbuf_bytes_remaining
    HB = KC * TOWN * 2
    WB = HB
    MB = (rem - HB - WB - 64) // 64 * 64
    assert MB >= 59600, MB
    arena = nc.alloc_sbuf_tensor("arena", [128, (HB + WB + MB) // 4], F32)
    H_OFF, W_OFF, M_OFF = 0, HB, HB + WB

    def view(off, shape, dt):
        es = 4 if dt == F32 else 2
        n = int(np.prod(shape))
        nbytes = n * es
        assert off % 4 == 0 and nbytes % 4 == 0, (off, shape)
        ap = arena[:, off // 4:(off + nbytes) // 4]
        if dt != F32:
            ap = ap.bitcast(dt)
        if len(shape) == 2:
            ap = ap.rearrange("p (a b) -> p a b", a=shape[0])
        elif len(shape) == 3:
            ap = ap.rearrange("p (a b c) -> p a b c", a=shape[0], b=shape[1])
        return ap

    class Bump:
        def __init__(self, base, size):
            self.base, self.size, self.off = base, size, 0

        def take(self, shape, dt):
            es = 4 if dt == F32 else 2
            nbytes = (int(np.prod(shape)) * es + 31) // 32 * 32
            assert self.off + nbytes <= self.size, ("arena overflow", self.off, nbytes, self.size)
            v = view(self.base + self.off, shape, dt)
            self.off += nbytes
            return v

    H = view(H_OFF, [KC, TOWN], BF16)
    R_H = [Res("H%d" % j) for j in range(NB + 1)]

    wsf = view(M_OFF, [8, 128], F32)
    g = nc.gpsimd

    def pool(fn, reads=(), writes=()):
        return POOL.op(fn, reads=reads, writes=writes)

    R_c = {n: Res(n) for n in ["ident_f", "ident_b", "utri", "ones_f", "ones_b", "mneg_d",
                               "ones16", "mneg_o", "fo", "fe", "ng16", "gcol", "bfb", "bsb", "wsf", "WT", "wf", "lfc"]}
    pool(lambda: g.memset(ident_f[:], 1.0), writes=[R_c["ident_f"]])
    pool(lambda: g.affine_select(out=ident_f[:], in_=ident_f[:], pattern=[[-1, 128]], compare_op=ALU.is_equal,
                                 fill=0.0, base=0, channel_multiplier=1),
         reads=[R_c["ident_f"]], writes=[R_c["ident_f"]])
    pool(lambda: g.tensor_copy(out=ident_b[:], in_=ident_f[:]), reads=[R_c["ident_f"]], writes=[R_c["ident_b"]])
    pool(lambda: g.memset(utri_f[:], 1.0), writes=[R_c["utri"]])
    pool(lambda: g.affine_select(out=utri_f[:], in_=utri_f[:], pattern=[[1, 128]], compare_op=ALU.is_ge,
                                 fill=0.0, base=0, channel_multiplier=-1),
         reads=[R_c["utri"]], writes=[R_c["utri"]])
    pool(lambda: g.memset(ones_f[:], 1.0), writes=[R_c["ones_f"]])
    pool(lambda: g.memset(ones_b[:], 1.0), writes=[R_c["ones_b"]])
    pool(lambda: g.memset(ones16[:], 1.0), writes=[R_c["ones16"]])
    pool(lambda: g.memset(cum[:, :, :].rearrange("p j h -> p (j h)"), 0.0), writes=[R_cum])
    pool(lambda: g.memset(mneg_d[:], 0.0), writes=[R_c["mneg_d"]])
    pool(lambda: g.affine_select(out=mneg_d[:], in_=mneg_d[:], pattern=[[1, 128]], compare_op=ALU.is_ge,
                                 fill=NEG, base=0, channel_multiplier=-1),
         reads=[R_c["mneg_d"]], writes=[R_c["mneg_d"]])

    ds_setup = DSem(nc, "d_setup")
    for nm in ["mneg_o", "fo", "fe", "ng16", "bfb", "bsb", "lfc", "wsf"]:
        R_c[nm].dsem = ds_setup
    load(SP, mneg_o[:], mno_in.rearrange("r k q -> k r q"), R_c["mneg_o"])
    load(SP, fo_t[:], fo_in, R_c["fo"])
    load(SP, fe_t[:], fe_in, R_c["fe"])
    load(SP, ng16[:], norm_g.rearrange("(k p) -> k p", p=128), R_c["ng16"])
    load(SP, bfb[:], b_f.partition_broadcast(128), R_c["bfb"])
    load(SP, bsb[:].rearrange("p g t -> p (g t)"), b_s.rearrange("g t -> (g t)").partition_broadcast(128), R_c["bsb"])
    load(SP, lfc[:], clf.rearrange("(j p) h -> p j h", p=128), R_c["lfc"])
    load(SP, wsf[:, :, :], w_s.rearrange("g t s -> t g s"), R_c["wsf"])
    regroup([R_c[nm] for nm in ["mneg_o", "fo", "fe", "ng16", "bfb", "bsb", "lfc", "wsf"]])

    R_mnb = Res("mneg_b")
    pool(lambda: g.tensor_copy(out=mneg_b[:, 0, :], in_=mneg_d[:, :]), reads=[R_c["mneg_d"]], writes=[R_mnb])
    pool(lambda: g.tensor_copy(out=mneg_b[:, 1:3, :], in_=mneg_o[:, :, :]), reads=[R_c["mneg_o"], R_mnb], writes=[R_mnb])
    WK = view(W_OFF, [KC, 1024], BF16)
    WV = view(W_OFF + 32768, [KC, 1024], BF16)
    R_WK = [Res("WK%d" % k) for k in range(KC)]
    R_WV = [Res("WV%d" % k) for k in range(KC)]
    ds_wk = [DSem(nc, "d_wk%d" % i) for i in range(4)]
    ds_wv = [DSem(nc, "d_wv%d" % i) for i in range(4)]
    for k in range(KC):
        R_WK[k].dsem = ds_wk[k // 4]
        R_WV[k].dsem = ds_wv[k // 4]
    for k in range(KC):
        load(POOL, WK[:, k, :], w_in[k * 128:(k + 1) * 128, OFF_K:OFF_K + 1024], R_WK[k])
    regroup(R_WK)
    load(POOL, wf[:], w_in.rearrange("(k p) c -> p k c", p=128)[:, :, OFF_F:OFF_F + 8], R_c["wf"])
    for k in range(KC):
        load(POOL, WV[:, k, :], w_in[k * 128:(k + 1) * 128, OFF_V:OFF_V + 1024], R_WV[k])
    regroup(R_WV)

    PE.op(lambda: nc.tensor.transpose(PSB[7][:, 0:16], ng16[:, :], ident_f[0:16, 0:16]),
          reads=[R_c["ng16"], R_c["ident_f"]], writes=[PSR[7]])
    DVE.op(lambda: nc.vector.tensor_copy(out=gcol[:], in_=PSB[7][:, 0:16]), reads=[PSR[7]], writes=[R_c["gcol"]])
    for half in range(2):
        for gg in range(4):
            gi = half * 4 + gg
            PE.op(lambda gi=gi, gg=gg, half=half: nc.tensor.transpose(
                PSB[5 + half][:, gg * 128:(gg + 1) * 128], wsf[:, gi, :], ident_f[:, :]),
                reads=[R_c["wsf"], R_c["ident_f"]], pwrites=[PSR[5 + half]], sig=(gg == 3))
        DVE.op(lambda half=half: nc.vector.tensor_tensor(
            out=WT[:, half * 4:(half + 1) * 4, :],
            in0=PSB[5 + half][:, :].rearrange("p (g t) -> p g t", g=4),
            in1=utri_f[:, :].unsqueeze(1).broadcast_to([128, 4, 128]), op=ALU.mult),
            reads=[PSR[5 + half], R_c["utri"]], pwrites=[R_c["WT"]])

    if stop_after == "setup":
        drain(); return nc
    MA = Bump(M_OFF, MB)
    xbuf = [MA.take([D], F32) for _ in range(2)]
    xn = [MA.take([D], BF16) for _ in range(2)]
    hTb = [MA.take([KC, 128], BF16) for _ in range(2)]
    Kf32 = MA.take([1024], F32)
    Vf32 = MA.take([1024], F32)
    Kb = [MA.take([1024], BF16) for _ in range(2)]
    Vb = [MA.take([1024], BF16) for _ in range(2)]
    KTsb = [MA.take([8, 128], BF16) for _ in range(2)]
    R_xbuf = [Res("xbuf%d" % i, prev=[R_c["wsf"]]) for i in range(2)]
    R_xn = [Res("xn%d" % i) for i in range(2)]
    R_hTb = [Res("hTb%d" % i) for i in range(2)]
    R_Kf32, R_Vf32 = Res("Kf32"), Res("Vf32")
    R_Kb = [Res("Kb%d" % i) for i in range(2)]
    R_Vb = [Res("Vb%d" % i) for i in range(2)]
    R_KTsb = [Res("KTsb%d" % i) for i in range(2)]
    R_ss = [Res("ss%d" % i) for i in range(2)]
    R_KTs = Res("KT_s"); R_Vs = Res("V_s"); R_QTs = Res("QT_s"); R_GAs = Res("GA_s")

    blocks = [("own", j) for j in range(NB)] + [("smp", 0)] + [("oth", j) for j in range(NB)]
    NBLK = len(blocks)
    psT = [PSB[0][:, :].bitcast(BF16).rearrange("p (k t) -> p k t", k=8),
           PSB[1][:, :].bitcast(BF16).rearrange("p (k t) -> p k t", k=8)]
    psKT = PSB[6][:, :].bitcast(BF16).rearrange("p (h t) -> p h t", h=8)

    def binfo(n):
        kind, j = blocks[n]
        rows = NS if kind == "smp" else 128
        if kind == "own":
            xsrc = x_own[j * 128:(j + 1) * 128, :]
            tcol = j * 128
            tzi = j
        elif kind == "oth":
            xsrc = x_oth[j * 128:(j + 1) * 128, :]
            tcol = 2048 + j * 128
            tzi = 16 + j
        else:
            xsrc = x_smp
            tcol = 4096
            tzi = 32
        return kind, j, rows, xsrc, tcol, tzi

    def hT_of(n):
        kind, j, rows, _, _, _ = binfo(n)
        if kind == "own":
            return H[:, :, j * 128:(j + 1) * 128], R_H[j]
        if kind == "smp":
            return H[:, :, 2048:2048 + NS], R_H[NB]
        return hTb[n % 2][:, :, :], R_hTb[n % 2]

    def a_load(n):
        kind, j, rows, xsrc, _, _ = binfo(n)
        load(SP, xbuf[n % 2][0:rows, :], xsrc, R_xbuf[n % 2])

    def a_norm(n):
        kind, j, rows, _, _, _ = binfo(n)
        p = n % 2
        ss = sml[0:rows, p * 4:p * 4 + 1]
        sd = sml[0:rows, p * 4 + 1:p * 4 + 2]
        rs = sml[0:rows, p * 4 + 2:p * 4 + 3]
        ACT.op(lambda: nc.scalar.activation(out=xn[p][0:rows, :], in_=xbuf[p][0:rows, :], func=AF.Square,
                                            accum_out=ss), reads=[R_xbuf[p]], writes=[R_xn[p], R_ss[p]])
        ACT.op(lambda: nc.scalar.activation(out=sd, in_=ss, func=AF.Sqrt, scale=1.0 / D, bias=RMS_EPS),
               reads=[R_ss[p]], writes=[R_ss[p]])
        DVE.op(lambda: nc.vector.reciprocal(out=rs, in_=sd), reads=[R_ss[p]], writes=[R_ss[p]])
        ACT.op(lambda: nc.scalar.activation(out=xn[p][0:rows, :], in_=xbuf[p][0:rows, :], func=AF.Copy, scale=rs),
               reads=[R_xbuf[p], R_ss[p]], writes=[R_xn[p]])

    def a_transp(n):
        kind, j, rows, _, _, _ = binfo(n)
        p = n % 2
        for k in range(KC):
            PE.op(lambda k=k: nc.tensor.transpose(psT[k // 8][:, k % 8, 0:rows], xn[p][0:rows, k * 128:(k + 1) * 128],
                                                  ident_b[0:rows, 0:rows]),
                  reads=[R_xn[p], R_c["ident_b"]], pwrites=[PSR[0], PSR[1]] if k == 0 else [], sig=(k == KC - 1))
        dst, rdst = hT_of(n)
        for half in range(2):
            DVE.op(lambda half=half: nc.vector.tensor_tensor(
                out=dst[:, half * 8:(half + 1) * 8, :], in0=psT[half][:, :, 0:rows],
                in1=gcol[:, half * 8:(half + 1) * 8].unsqueeze(2).broadcast_to([128, 8, rows]), op=ALU.mult),
                reads=[PSR[half], R_c["gcol"]], pwrites=[rdst])

    def a_Kmm(n):
        kind, j, rows, _, _, _ = binfo(n)
        hT, rh = hT_of(n)
        for k in range(KC):
            for c in range(2):
                PE.op(lambda k=k, c=c: nc.tensor.matmul(PSB[2 + c][0:rows, :], hT[:, k, :], WK[:, k, c * 512:(c + 1) * 512],
                                                        start=(k == 0), stop=(k == KC - 1)),
                      reads=[rh, R_WK[k]], writes=[PSR[2 + c]] if k == 0 else [], pwrites=[] if k == 0 else [PSR[2 + c]],
                      sig=(k == KC - 1))

    def a_Vmm(n):
        kind, j, rows, _, _, _ = binfo(n)
        hT, rh = hT_of(n)
        for k in range(KC):
            for c in range(2):
                PE.op(lambda k=k, c=c: nc.tensor.matmul(PSB[4 + c][0:rows, :], hT[:, k, :], WV[:, k, c * 512:(c + 1) * 512],
                                                        start=(k == 0), stop=(k == KC - 1)),
                      reads=[rh, R_WV[k]], writes=[PSR[4 + c]] if k == 0 else [], pwrites=[] if k == 0 else [PSR[4 + c]],
                      sig=(k == KC - 1))
            PE.op(lambda k=k: nc.tensor.matmul(PSB[7][0:rows, 0:8], hT[:, k, :], wf[:, k, :],
                                               start=(k == 0), stop=(k == KC - 1)),
                  reads=[rh, R_c["wf"]], writes=[PSR[7]] if k == 0 else [], pwrites=[] if k == 0 else [PSR[7]],
                  sig=(k == KC - 1))

    def a_Kepi(n):
        kind, j, rows, _, _, _ = binfo(n)
        p = n % 2
        if kind != "oth":
            for c in range(2):
                ACT.op(lambda c=c: nc.scalar.activation(out=Kf32[0:rows, c * 512:(c + 1) * 512], in_=PSB[2 + c][0:rows, :],
                                                        func=AF.Copy),
                       reads=[PSR[2 + c]], writes=[R_Kf32] if c == 0 else [], pwrites=[] if c == 0 else [R_Kf32])
            POOL.op(lambda: g.tensor_copy(out=Kb[p][0:rows, :], in_=Kf32[0:rows, :]), reads=[R_Kf32], writes=[R_Kb[p]])
        else:
            for c in range(2):
                DVE.op(lambda c=c: nc.vector.tensor_copy(out=Kb[p][0:rows, c * 512:(c + 1) * 512], in_=PSB[2 + c][0:rows, :]),
                       reads=[PSR[2 + c]], writes=[R_Kb[p]] if c == 0 else [], pwrites=[] if c == 0 else [R_Kb[p]])

    def a_KTtr(n):
        kind, j, rows, _, _, _ = binfo(n)
        p = n % 2
        for h in range(8):
            PE.op(lambda h=h: nc.tensor.transpose(psKT[:, h, 0:rows], Kb[p][0:rows, h * 128:(h + 1) * 128],
                                                  ident_b[0:rows, 0:rows]),
                  reads=[R_Kb[p], R_c["ident_b"]], writes=[PSR[6]] if h == 0 else [], pwrites=[] if h == 0 else [PSR[6]],
                  sig=(h == 7))

    def a_Vepi(n):
        kind, j, rows, _, _, tzi = binfo(n)
        p = n % 2
        if kind != "oth":
            for c in range(2):
                ACT.op(lambda c=c: nc.scalar.activation(out=Vf32[0:rows, c * 512:(c + 1) * 512], in_=PSB[4 + c][0:rows, :],
                                                        func=AF.Copy),
                       reads=[PSR[4 + c]], writes=[R_Vf32] if c == 0 else [], pwrites=[] if c == 0 else [R_Vf32])
            POOL.op(lambda: g.tensor_copy(out=Vb[p][0:rows, :], in_=Vf32[0:rows, :]), reads=[R_Vf32], writes=[R_Vb[p]])
        else:
            for c in range(2):
                DVE.op(lambda c=c: nc.vector.tensor_copy(out=Vb[p][0:rows, c * 512:(c + 1) * 512], in_=PSB[4 + c][0:rows, :]),
                       reads=[PSR[4 + c]], writes=[R_Vb[p]] if c == 0 else [], pwrites=[] if c == 0 else [R_Vb[p]])
        DVE.op(lambda: nc.vector.tensor_tensor(out=tz[0:rows, tzi, :], in0=PSB[7][0:rows, 0:8], in1=bfb[0:rows, :], op=ALU.add),
               reads=[PSR[7], R_c["bfb"]], writes=[R_tz[tzi]])
        DVE.op(lambda: nc.vector.tensor_copy(out=KTsb[p][:, :, 0:rows], in_=psKT[:, :, 0:rows]),
               reads=[PSR[6]], writes=[R_KTsb[p]])

    def a_stores(n):
        kind, j, rows, _, tcol, _ = binfo(n)
        p = n % 2
        if kind == "own":
            store(SP, k_own[j * 128:(j + 1) * 128, :], Kf32[:, :], R_Kf32)
            store(SP, v_own[j * 128:(j + 1) * 128, :], Vf32[:, :], R_Vf32)
        elif kind == "smp":
            store(SP, k_smp, Kf32[0:rows, :], R_Kf32)
            store(SP, v_smp, Vf32[0:rows, :], R_Vf32)
        store(SP, KT_s.rearrange("h d t -> d h t")[:, :, tcol:tcol + rows], KTsb[p][:, :, 0:rows], R_KTsb[p], R_KTs, final=False)
        store(SP, V_s[tcol:tcol + rows, :], Vb[p][0:rows, :], R_Vb[p], R_Vs, final=False)

    a_load(0); a_load(1)
    a_norm(0); a_transp(0)
    for n in range(NBLK):
        if n + 2 < NBLK:
            a_load(n + 2)
        if n + 1 < NBLK:
            a_norm(n + 1)
        a_Kmm(n)
        if n + 1 < NBLK:
            a_transp(n + 1)
        a_Kepi(n)
        a_Vmm(n)
        a_KTtr(n)
        a_Vepi(n)
        a_stores(n)

    if stop_after == "A":
        drain(); return nc
    v = nc.vector
    tzf = tz[:, :, :].rearrange("p j h -> p (j h)")
    ACT.op(lambda: nc.scalar.activation(out=tzf[:, 0:256], in_=tzf[:, 0:256], func=AF.Exp, scale=-1.0),
           reads=R_tz[0:16] + R_tz[17:33], writes=[R_cum])
    ACT.op(lambda: nc.scalar.activation(out=tzf[0:NS, 256:264], in_=tzf[0:NS, 256:264], func=AF.Exp, scale=-1.0),
           reads=[R_tz[32]], pwrites=[R_cum])
    ACT.op(lambda: nc.scalar.activation(out=tzf[:, 0:256], in_=tzf[:, 0:256], func=AF.Ln, bias=1.0),
           reads=[R_cum], pwrites=[R_cum])
    ACT.op(lambda: nc.scalar.activation(out=tzf[0:NS, 256:264], in_=tzf[0:NS, 256:264], func=AF.Ln, bias=1.0),
           reads=[R_cum], pwrites=[R_cum])
    DVE.op(lambda: v.tensor_scalar(out=tzf[:, 0:256], in0=tzf[:, 0:256], scalar1=-1.0, scalar2=None, op0=ALU.mult),
           reads=[R_cum], pwrites=[R_cum])
    DVE.op(lambda: v.tensor_scalar(out=tzf[0:NS, 256:264], in0=tzf[0:NS, 256:264], scalar1=-1.0, scalar2=None, op0=ALU.mult),
           reads=[R_cum], pwrites=[R_cum])
    R_lf = Res("lf")
    store(SP, lf_own.rearrange("(j p) h -> p j h", p=128), tz[:, 0:16, :], R_cum)
    store(SP, lf_smp, tz[0:NS, 32, :], R_cum)
    cumv = {}
    prevA_ext = []
    prevA = R_xbuf + R_xn + R_hTb + [R_Kf32, R_Vf32] + R_Kb + R_Vb + R_KTsb
    mm = nc.tensor.matmul
    def cum_stage2():
        LO = tzf[:, 0:128]
        LT = tzf[:, 128:256]
        LS = tzf[0:NS, 256:264]
        lfcf = lfc[:, :, :].rearrange("p j h -> p (j h)")
        cumf = cum[:, :, :].rearrange("p j h -> p (j h)")
        mm = nc.tensor.matmul
        PE.op(lambda: mm(PSB[7][:, 0:128], utri_f[:, :], LO, start=True, stop=True), reads=[R_cum, R_c["utri"]], writes=[PSR[7]], sig=False)
        PE.op(lambda: mm(PSB[7][:, 128:256], utri_f[:, :], LT, start=True, stop=True), pwrites=[PSR[7]], sig=False)
        PE.op(lambda: mm(PSB[7][:, 256:384], ones_f[:, :], LO, start=True, stop=True), reads=[R_c["ones_f"]], pwrites=[PSR[7]], sig=False)
        PE.op(lambda: mm(PSB[7][:, 384:512], ones_f[:, :], LT, start=True, stop=True), pwrites=[PSR[7]])
        PE.op(lambda: mm(PSB[6][:, 0:128], utri_f[:, :], lfcf, start=True, stop=True), reads=[R_c["lfc"]], writes=[PSR[6]], sig=False)
        PE.op(lambda: mm(PSB[6][:, 128:256], ones_f[:, :], lfcf, start=True, stop=True), pwrites=[PSR[6]], sig=False)
        PE.op(lambda: mm(PSB[6][0:NS, 256:264], utri_f[0:NS, 0:NS], LS, start=True, stop=True), pwrites=[PSR[6]])
        cumv.update(LO=LO, LT=LT, LS=LS, lfcf=lfcf, cumf=cumf)

    def cum_stage3():
        cumf = cumv['cumf']
        R_t = Res("cumtmp")
        DVE.op(lambda: v.tensor_copy(out=tmpC[:, :], in_=PSB[7][:, 384:512]), reads=[PSR[7]], writes=[R_t])
        DVE.op(lambda: v.tensor_tensor(out=tmpA[:, :], in0=PSB[7][:, 256:384], in1=tmpC[:, :], op=ALU.add), reads=[R_t], pwrites=[R_t])
        tA = tmpA[:, :].rearrange("p (j h) -> p h j", h=8)
        tB = tmpB[:, :].rearrange("p (j h) -> p h j", h=8)
        for h in range(8):
            DVE.op(lambda h=h: v.tensor_tensor_scan(out=tB[:, h, :], data0=ones16[:, :], data1=tA[:, h, :], initial=0.0,
                                                    op0=ALU.mult, op1=ALU.add), reads=[R_t, R_c["ones16"]], pwrites=[R_t])
        DVE.op(lambda: v.tensor_tensor(out=tmpB[:, :], in0=tmpB[:, :], in1=tmpA[:, :], op=ALU.subtract), reads=[R_t], pwrites=[R_t])
        DVE.op(lambda: v.tensor_tensor(out=tmpD[:, :], in0=tmpC[:, :], in1=fo_t[:, :], op=ALU.mult), reads=[R_t, R_c["fo"]], pwrites=[R_t])
        DVE.op(lambda: v.tensor_tensor(out=tmpD[:, :], in0=tmpD[:, :], in1=tmpB[:, :], op=ALU.add), reads=[R_t], pwrites=[R_t])
        DVE.op(lambda: v.tensor_tensor(out=cumf[:, 0:128], in0=PSB[7][:, 0:128], in1=tmpD[:, :], op=ALU.add), reads=[R_t], pwrites=[R_t])
        DVE.op(lambda: v.tensor_tensor(out=tmpE[:, :], in0=PSB[7][:, 256:384], in1=fe_t[:, :], op=ALU.mult), reads=[R_c["fe"]], pwrites=[R_t])
        DVE.op(lambda: v.tensor_tensor(out=tmpE[:, :], in0=tmpE[:, :], in1=tmpB[:, :], op=ALU.add), reads=[R_t], pwrites=[R_t])
        DVE.op(lambda: v.tensor_tensor(out=cumf[:, 128:256], in0=PSB[7][:, 128:256], in1=tmpE[:, :], op=ALU.add), reads=[R_t], pwrites=[R_t])
        DVE.op(lambda: v.tensor_copy(out=tmpA[:, :], in_=PSB[6][:, 128:256]), reads=[PSR[6], R_t], pwrites=[R_t])
        for h in range(8):
            DVE.op(lambda h=h: v.tensor_tensor_scan(out=tB[:, h, :], data0=ones16[:, :], data1=tA[:, h, :], initial=0.0,
                                                    op0=ALU.mult, op1=ALU.add), reads=[R_t], pwrites=[R_t])
        DVE.op(lambda: v.tensor_tensor(out=tmpC[:, :], in0=tmpB[:, :], in1=tmpA[:, :], op=ALU.subtract), reads=[R_t], pwrites=[R_t])
        DVE.op(lambda: v.tensor_tensor(out=cumf[:, 33 * 8:49 * 8], in0=PSB[6][:, 0:128], in1=tmpC[:, :], op=ALU.add), reads=[R_t], pwrites=[R_t])
        DVE.op(lambda: v.tensor_tensor(out=cumf[0:NS, 256:264], in0=PSB[6][0:NS, 256:264], in1=tmpB[0:NS, 120:128], op=ALU.add),
               reads=[R_t], pwrites=[R_t])
        R_cum2 = Res("cum2")
        DVE.op(lambda: v.tensor_scalar(out=ncum[:, :, :].rearrange("p j h -> p (j h)"), in0=cumf[:, :], scalar1=-1.0 / SCALE,
                                       scalar2=None, op0=ALU.mult), reads=[R_t], writes=[R_cum2])
        MCUM = Bump(M_OFF + MB - 13312, 13312)
        xk = MCUM.take([392], F32); xq = MCUM.take([136], F32)
        xr1 = MCUM.take([392], F32); xf = MCUM.take([392], F32)
        spl = [MCUM.take([528], BF16) for i in range(3)]
        rows6 = MCUM.take([3, 6, 128], BF16)
        R_spl = Res("spl", prev=prevA)
        DVE.op(lambda: v.tensor_copy(out=xk[:, :].rearrange("p (h b) -> p b h", h=8), in_=ncum[:, :, :]), reads=[R_cum2], writes=[R_spl])
        DVE.op(lambda: v.memset(xq[:, :], 0.0), pwrites=[R_spl])
        xq3 = xq[:, :].rearrange("p (h b) -> p b h", h=8)
        DVE.op(lambda: v.tensor_scalar(out=xq3[:, 0:16, :], in0=cum[:, 0:16, :], scalar1=1.0 / SCALE, scalar2=None, op0=ALU.mult),
               reads=[R_t, R_spl], pwrites=[R_spl])
        DVE.op(lambda: v.tensor_scalar(out=xq3[0:NS, 16, :], in0=cum[0:NS, 32, :], scalar1=1.0 / SCALE, scalar2=None, op0=ALU.mult),
               reads=[R_t, R_spl], pwrites=[R_spl])
        for (src, c0, n) in ((xk, 0, 392), (xq, 392, 136)):
            cur = src
            for si in range(3):
                DVE.op(lambda cur=cur, si=si: v.tensor_copy(out=spl[si][:, c0:c0 + n], in_=cur[:, 0:n]), reads=[R_spl], pwrites=[R_spl])
                if si < 2:
                    DVE.op(lambda si=si: v.tensor_copy(out=xf[:, 0:n], in_=spl[si][:, c0:c0 + n]), reads=[R_spl], pwrites=[R_spl])
                    DVE.op(lambda cur=cur: v.tensor_tensor(out=xr1[:, 0:n], in0=cur[:, 0:n], in1=xf[:, 0:n], op=ALU.subtract),
                           reads=[R_spl], pwrites=[R_spl])
                    cur = xr1
        cumv.update(R_spl=R_spl, spl=spl, rows6=rows6, R_cum2=R_cum2)
        prevA_ext.extend([R_spl])

    def cum_stage4():
        R_spl, spl, rows6 = cumv['R_spl'], cumv['spl'], cumv['rows6']
        R_rows = Res("rows", prev=prevA)
        chunks_t = [(0, 128), (128, 128), (256, 128), (384, 8), (392, 128), (520, 8)]
        NCK_s = dscr("NCK_s", [3, 392, 128]); CQ3_s = dscr("CQ3_s", [3, 136, 128])
        R_NCKs, R_CQ3s = Res("NCK_s"), Res("CQ3_s")
        for si in range(3):
            pbank = PSB[5 + si][:, :].bitcast(BF16)
            for ci, (c0, n) in enumerate(chunks_t):
                PE.op(lambda si=si, ci=ci, c0=c0, n=n, pbank=pbank: nc.tensor.transpose(pbank[0:n, ci * 128:(ci + 1) * 128], spl[si][:, c0:c0 + n], ident_b[:, :]),
                      reads=[R_spl, R_c["ident_b"]], writes=[PSR[5 + si]] if ci == 0 else [], pwrites=[] if ci == 0 else [PSR[5 + si]],
                      sig=(ci == len(chunks_t) - 1))
            for ci, (c0, n) in enumerate(chunks_t):
                DVE.op(lambda si=si, ci=ci, n=n, pbank=pbank: v.tensor_copy(out=rows6[0:n, si, ci, :], in_=pbank[0:n, ci * 128:(ci + 1) * 128]),
                       reads=[PSR[5 + si]], pwrites=[R_rows])
        first = True
        for si in range(3):
            for ci, (c0, n) in enumerate(chunks_t):
                if c0 < 392:
                    dst, rd = NCK_s[si, c0:c0 + n, :], R_NCKs
                else:
                    dst, rd = CQ3_s[si, c0 - 392:c0 - 392 + n, :], R_CQ3s
                SP.dma(dst, rows6[0:n, si, ci, :], dsem_of(R_rows), reads=[R_rows], pwrites=[rd])

        cumv.update(R_NCKs=R_NCKs, R_CQ3s=R_CQ3s, NCK_s=NCK_s, CQ3_s=CQ3_s)
        prevA_ext.extend([R_rows])

    RING0 = W_OFF
    ring = [view(RING0 + s * 4096, [KC, 128], BF16) for s in range(8)]
    R_ring = [Res("ring%d" % s, prev=R_WK) for s in range(8)]
    WVB = view(W_OFF + 32768, [KC, 1024], BF16)
    R_WVB = [Res("WVB%d" % k, prev=R_WV) for k in range(KC)]
    for k in range(KC):
        R_WVB[k].dsem = ds_wv[k // 4]
    w_in_r = w_in.rearrange("(k p) c -> p k c", p=128)

    chunksC1 = [("q", h, OFF_Q + h * 128) for h in range(8)] + [("ga", h, OFF_GA + h * 128) for h in range(8)]
    chunksC2 = []
    for gi in range(8):
        chunksC2.append(("u", gi, OFF_U + gi * 128))
        chunksC2.append(("gb", gi, OFF_GB + gi * 128))
    allchunks = chunksC1 + chunksC2
    ring_state = {"next": 0}

    def ring_load(ci):
        kind, idx, col = allchunks[ci]
        s = ci % 8
        load(POOL, ring[s][:, :, :], w_in_r[:, :, col:col + 128], R_ring[s])

    tiles = []
    for (c0_, n_) in [(0, 416), (416, 416), (832, 416), (1248, 416), (1664, 400)]:
        blks = sorted(set(min(c // 128, NB) for c in range(c0_, c0_ + n_, 16)))
        tiles.append((c0_, n_, [R_H[b_] for b_ in blks]))
    bank_ctr = {"n": 0}

    def next_bank(lo=0, hi=6):
        b = lo + bank_ctr["n"] % (hi - lo)
        bank_ctr["n"] += 1
        return b

    MC = Bump(M_OFF, MB)
    vn = MC.take([NB, 1024], BF16)
    vn_s = MC.take([1024], BF16)
    R_vn = [Res("vn%d" % i, prev=prevA) for i in range(NB + 1)]
    stg = [MC.take([512], BF16) for _ in range(4)]
    R_stg = [Res("stg%d" % i, prev=prevA) for i in range(4)]
    mark_c = MC.off

    def c_chunk(ci, dst_sb=None, r_dst=None):
        kind, idx, col = allchunks[ci]
        s = ci % 8
        for ti, (c0, n, rh) in enumerate(tiles):
            b = next_bank()
            for k in range(KC):
                PE.op(lambda k=k, b=b, c0=c0, n=n: mm(PSB[b][:, 0:n], ring[s][:, k, :], H[:, k, c0:c0 + n],
                                                       start=(k == 0), stop=(k == KC - 1)),
                      reads=rh + [R_ring[s]], writes=[PSR[b]] if k == 0 else [], pwrites=[] if k == 0 else [PSR[b]],
                      sig=(k == KC - 1))
            func = {"q": AF.Copy, "ga": AF.Silu, "u": AF.Gelu_apprx_tanh, "gb": AF.Silu}[kind]
            if kind in ("q", "ga"):
                u = c_chunk.ctr % 4
                c_chunk.ctr += 1
                ACT.op(lambda b=b, n=n, u=u: nc.scalar.activation(out=stg[u][:, 0:n], in_=PSB[b][:, 0:n], func=func),
                       reads=[PSR[b]], writes=[R_stg[u]])
                dst = (QT_s if kind == "q" else GA_s)[idx, :, c0:c0 + n]
                store(SP, dst, stg[u][:, 0:n], R_stg[u], R_QTs if kind == "q" else R_GAs, final=False)
            else:
                ACT.op(lambda b=b, n=n, c0=c0: nc.scalar.activation(out=dst_sb[:, c0:c0 + n], in_=PSB[b][:, 0:n], func=func),
                       reads=[PSR[b]], writes=[r_dst] if ti == 0 else [], pwrites=[] if ti == 0 else [r_dst])
    c_chunk.ctr = 0

    for ci in range(6):
        ring_load(ci)
    for ci in range(16):
        if ci + 6 < len(allchunks):
            ring_load(ci + 6)
        load(POOL, WVB[:, ci, :], w_in[ci * 128:(ci + 1) * 128, OFF_VB:OFF_VB + 1024], R_WVB[ci])
        c_chunk(ci)
        if ci == 1:
            cum_stage2()
            cum_stage3()
        if ci == 3:
            cum_stage4()
    R_cum2, R_NCKs, R_CQ3s, NCK_s, CQ3_s = cumv["R_cum2"], cumv["R_NCKs"], cumv["R_CQ3s"], cumv["NCK_s"], cumv["CQ3_s"]

    if stop_after == "C1":
        drain(); return nc
    regroup(R_WVB)
    gxs = [MC.take([1024], F32) for _ in range(2)]
    vt = MC.take([1024], F32)
    lngb = MC.take([1024], F32)
    lnbb = MC.take([1024], F32)
    prevA = prevA + prevA_ext
    R_gxs = [Res("gx%d" % i, prev=prevA) for i in range(2)]
    R_vt = Res("vt", prev=prevA)
    R_gx, R_junk = R_gxs[0], R_gxs[1]
    R_ln = Res("ln", prev=prevA)
    R_st = [Res("st%d" % i) for i in range(2)]
    load(SP, lngb[:, :], ln_g.partition_broadcast(128), R_ln)
    ev = SP.dma(lnbb[:, :], ln_b.partition_broadcast(128), dsem_of(R_ln), pwrites=[R_ln])
    junkP = [PSB[4], PSB[5]]

    def b_vars(n):
        rows = 128 if n < NB else NS
        so = 16 + (n % 2) * 8
        return rows, n * 128, 2 * (n % 2), [sml[0:rows, so + i:so + i + 1] for i in range(8)], R_st[n % 2], gxs[n % 2], R_gxs[n % 2]

    def b_front(n):
        rows, c0, pb, (s1a, s1b, s2, msum, mean, msq, var, rstd), rst, gx, rgx = b_vars(n)
        for k in range(KC):
            for c in range(2):
                PE.op(lambda k=k, c=c: mm(PSB[pb + c][0:rows, :], H[:, k, c0:c0 + rows], WVB[:, k, c * 512:(c + 1) * 512],
                                          start=(k == 0), stop=(k == KC - 1)),
                      reads=[R_H[n], R_WVB[k]], writes=[PSR[pb + c]] if k == 0 else [], pwrites=[] if k == 0 else [PSR[pb + c]],
                      sig=(k == KC - 1))
        ACT.op(lambda: nc.scalar.activation(out=gx[0:rows, 0:512], in_=PSB[pb][0:rows, :], func=AF.Gelu_apprx_tanh, accum_out=s1a),
               reads=[PSR[pb]], writes=[rgx, rst])
        ACT.op(lambda: nc.scalar.activation(out=gx[0:rows, 512:1024], in_=PSB[pb + 1][0:rows, :], func=AF.Gelu_apprx_tanh, accum_out=s1b),
               reads=[PSR[pb + 1]], pwrites=[rgx, rst])
        for c in range(2):
            sq = s2 if c == 0 else msq
            ACT.op(lambda c=c, sq=sq: nc.scalar.activation(out=junkP[c][0:rows, :], in_=gx[0:rows, c * 512:(c + 1) * 512], func=AF.Square,
                                                           accum_out=sq),
                   reads=[rgx], writes=[PSR[4 + c]], pwrites=[rst])
        DVE.op(lambda: v.tensor_tensor(out=msum, in0=s1a, in1=s1b, op=ALU.add), reads=[rst], pwrites=[rst])
        DVE.op(lambda: v.tensor_scalar(out=mean, in0=msum, scalar1=1.0 / 1024, scalar2=None, op0=ALU.mult), reads=[rst], pwrites=[rst])
        DVE.op(lambda: v.tensor_tensor(out=s2, in0=s2, in1=msq, op=ALU.add), reads=[rst], pwrites=[rst])
        DVE.op(lambda: v.tensor_tensor(out=msq, in0=mean, in1=mean, op=ALU.mult), reads=[rst], pwrites=[rst])
        DVE.op(lambda: v.scalar_tensor_tensor(out=var, in0=s2, scalar=1.0 / 1024, in1=msq, op0=ALU.mult, op1=ALU.subtract),
               reads=[rst], pwrites=[rst])

    def b_sqrt(n):
        rows, c0, pb, (s1a, s1b, s2, msum, mean, msq, var, rstd), rst, gx, rgx = b_vars(n)
        ACT.op(lambda: nc.scalar.activation(out=var, in_=var, func=AF.Sqrt, bias=LN_EPS), reads=[rst], pwrites=[rst])

    def b_back(n):
        rows, c0, pb, (s1a, s1b, s2, msum, mean, msq, var, rstd), rst, gx, rgx = b_vars(n)
        DVE.op(lambda: v.reciprocal(out=rstd, in_=var), reads=[rst], pwrites=[rst])
        DVE.op(lambda: v.tensor_scalar(out=vt[0:rows, :], in0=gx[0:rows, :], scalar1=mean, scalar2=rstd, op0=ALU.subtract, op1=ALU.mult),
               reads=[rgx, rst], writes=[R_vt])
        POOL.op(lambda: g.tensor_tensor(out=vt[0:rows, :], in0=vt[0:rows, :], in1=lngb[0:rows, :], op=ALU.mult),
                reads=[R_vt, R_ln], writes=[R_vt])
        if n < NB:
            DVE.op(lambda: v.tensor_tensor(out=vn[:, n, :], in0=vt[:, :], in1=lnbb[:, :], op=ALU.add),
                   reads=[R_vt, R_ln], writes=[R_vn[n]])
        else:
            DVE.op(lambda: v.tensor_tensor(out=vt[0:rows, :], in0=vt[0:rows, :], in1=lnbb[0:rows, :], op=ALU.add),
                   reads=[R_vt, R_ln], writes=[R_vt])
            store(SP, gv_smp, vt[0:rows, :], R_vt)
            DVE.op(lambda: v.tensor_copy(out=vn_s[0:rows, :], in_=vt[0:rows, :]), reads=[R_vt], writes=[R_vn[NB]])

    for n0 in range(0, NB + 1, 2):
        grp_b = [n for n in (n0, n0 + 1) if n < NB + 1]
        for n in grp_b:
            b_front(n)
        for n in grp_b:
            b_sqrt(n)
        for n in grp_b:
            b_back(n)

    if stop_after == "B":
        drain(); return nc
    prevB = [R_gx, R_vt, R_junk, R_ln] + prevA
    MC2 = Bump(M_OFF + mark_c, MB - mark_c)
    Usb = [MC2.take([TOWN], BF16) for _ in range(2)]
    GBsb = [MC2.take([TOWN], BF16) for _ in range(2)]
    t1 = [MC2.take([512], F32) for _ in range(2)]
    R_U = [Res("U%d" % i, prev=prevB) for i in range(2)]
    R_GB = [Res("GB%d" % i, prev=prevB) for i in range(2)]
    R_t1 = [Res("t1_%d" % i, prev=prevB) for i in range(2)]
    oTB = view(W_OFF + HB // 2, [8, TOWN], BF16)
    R_oTB = [Res("oTB%d" % i, prev=R_WVB + R_WV + R_WK[KC - 1:KC]) for i in range(8)]
    mixctr = {"n": 0}

    def mixing(gi):
        p = gi % 2
        for ti in range(5):
            b = next_bank()
            if ti < 4:
                for i in range(4):
                    blk = ti * 4 + i
                    PE.op(lambda blk=blk, i=i, b=b: mm(PSB[b][:, i * 128:(i + 1) * 128], vn[:, blk, gi * 128:(gi + 1) * 128], WT[:, gi, :],
                                                        start=True, stop=True),
                          reads=[R_vn[blk], R_c["WT"]], writes=[PSR[b]] if i == 0 else [], pwrites=[] if i == 0 else [PSR[b]],
                          sig=(i == 3))
                n, c0 = 512, ti * 512
                nb_ = 4
                bs_ap = bsb[:, gi, :].unsqueeze(1).broadcast_to([128, 4, 128])
                pin = PSB[b][:, :].rearrange("p (a t) -> p a t", a=4)
            else:
                PE.op(lambda b=b: mm(PSB[b][:, 0:NS], vn_s[0:NS, gi * 128:(gi + 1) * 128], WT[0:NS, gi, 0:NS], start=True, stop=True),
                      reads=[R_vn[NB], R_c["WT"]], writes=[PSR[b]])
                n, c0 = NS, 2048
                bs_ap = bsb[:, gi, 0:NS]
                pin = PSB[b][:, 0:NS]
            u = mixctr["n"] % 2
            mixctr["n"] += 1
            if ti < 4:
                t1v = t1[u][:, :].rearrange("p (a t) -> p a t", a=4)
            else:
                t1v = t1[u][:, 0:NS]
            DVE.op(lambda: v.tensor_tensor(out=t1v, in0=pin, in1=bs_ap, op=ALU.add), reads=[PSR[b], R_c["bsb"]], writes=[R_t1[u]])
            DVE.op(lambda: v.tensor_tensor(out=t1[u][:, 0:n], in0=t1[u][:, 0:n], in1=Usb[p][:, c0:c0 + n], op=ALU.mult),
                   reads=[R_t1[u], R_U[p]], writes=[R_t1[u]])
            DVE.op(lambda: v.tensor_tensor(out=oTB[:, gi, c0:c0 + n], in0=t1[u][:, 0:n], in1=GBsb[p][:, c0:c0 + n], op=ALU.mult),
                   reads=[R_t1[u], R_GB[p]], writes=[R_oTB[gi]] if ti == 0 else [], pwrites=[] if ti == 0 else [R_oTB[gi]])

    for ci in range(16, 32):
        if ci + 6 < len(allchunks):
            ring_load(ci + 6)
        kind, gi, col = allchunks[ci]
        if kind == "u":
            c_chunk(ci, Usb[gi % 2], R_U[gi % 2])
            if gi >= 1:
                mixing(gi - 1)
        else:
            c_chunk(ci, GBsb[gi % 2], R_GB[gi % 2])
    H2 = Bump(H_OFF + HB // 2, HB // 2)
    KTh0 = H2.take([NTOK_S], BF16)
    Vh0 = H2.take([33, 128], BF16)
    Qh0 = H2.take([TOWN], BF16)
    R_KTh0 = Res("KTh0", prev=R_H)
    R_Vh0 = Res("Vh0", prev=R_H)
    R_Qh0 = Res("Qh0", prev=R_H)

    def e_loads_big(h, KT_t, V_t, Q_t, rK, rV, rQ):
        load(SP, KT_t[:, :], KT_s[h, :, :], rK, extra_reads=[R_KTs])
        for q4 in range(4):
            SP.dma(V_t[:, q4 * 8:(q4 + 1) * 8, :],
                   V_s[q4 * 1024:(q4 + 1) * 1024, h * 128:(h + 1) * 128].rearrange("(b t) c -> t b c", t=128),
                   dsem_of(rV), reads=[R_Vs], writes=[rV] if q4 == 0 else [], pwrites=[] if q4 == 0 else [rV])
        SP.dma(V_t[0:NS, 32, :], V_s[4096:4096 + NS, h * 128:(h + 1) * 128], dsem_of(rV), pwrites=[rV])
        load(SP, Q_t[:, :], QT_s[h, :, :], rQ, extra_reads=[R_QTs])
    e_loads_big(0, KTh0, Vh0, Qh0, R_KTh0, R_Vh0, R_Qh0)
    mixing(7)

    if stop_after == "C2":
        drain(); return nc
    prevC = R_vn + R_stg + [R_gx, R_vt, R_junk, R_ln] + R_U + R_GB + R_t1
    prevH = R_H
    ME = Bump(M_OFF, MB)
    KTh = [KTh0, ME.take([NTOK_S], BF16)]
    Vh = [Vh0, ME.take([33, 128], BF16)]
    Qh = [Qh0, ME.take([TOWN], BF16)]
    GAh1 = H2.take([TOWN], BF16)
    GAh = [GAh1, GAh1]
    KTc = H2.take([PAST], BF16)
    kc = ME.take([16, 128], BF16)
    vc = ME.take([16, 128], BF16)
    Pb = [ME.take([512], BF16) for _ in range(5)]
    ptmp = H2.take([512], BF16)
    ptmp2 = H2.take([512], BF16)
    Pb.append(H2.take([512], BF16))
    R_ptmp = Res("ptmp", prev=R_H)
    R_ptmp2 = Res("ptmp2", prev=R_H)
    grp = {}
    fin = [ME.take([512], F32) for _ in range(2)]
    pv = [prevH, prevC]
    R_KTh = [R_KTh0, Res("KTh1", prev=prevC)]
    R_Vh = [R_Vh0, Res("Vh1", prev=prevC)]
    R_Qh = [R_Qh0, Res("Qh1", prev=prevC)]
    R_GAh1 = Res("GAh", prev=prevH)
    R_GAh = [R_GAh1, R_GAh1]
    R_KTc = Res("KTc", prev=prevH)
    R_kc, R_vc = Res("kc", prev=prevC), Res("vc", prev=prevC)
    A_all = ME.take([4096], BF16)
    Bh = [ME.take([2048], BF16) for _ in range(2)]
    As = ME.take([2176 + 128], BF16)
    R_Aall = Res("A_all", prev=prevC)
    R_Bh = [Res("Bh%d" % i, prev=prevC) for i in range(2)]
    R_Bh_ms = [Res("Bh_ms%d" % i) for i in range(2)]
    R_As_ms = Res("As_ms")
    R_As = Res("As", prev=prevC)
    R_Aall_ms = Res("A_all_ms", prev=prevC)
    DVE.op(lambda: v.memset(A_all[:, :], 1.0), writes=[R_Aall_ms, R_Aall])
    DVE.op(lambda: v.memset(As[:, :], 1.0), writes=[R_As])
    for hh in range(8):
        SP.dma(A_all[6 * hh + 3:6 * hh + 6, :], NCK_s[:, hh * 49:hh * 49 + 32, :].rearrange("s b t -> s (b t)"), dsem_of(R_Aall),
               reads=[R_NCKs, R_Aall_ms], pwrites=[R_Aall])
    ones_src3 = bass.AP(tensor=ones_b.tensor if hasattr(ones_b, "tensor") else ones_b, offset=0, ap=[[128, 3], [0, 16], [1, 128]])
    R_Pb = [Res("Pb%d" % i, prev=prevC) for i in range(5)] + [Res("Pb5", prev=R_H)]
    R_fin = [Res("fin%d" % i, prev=prevC) for i in range(2)]
    oTA = view(H_OFF, [8, TOWN], BF16)
    R_oTA = [Res("oTA%d" % i, prev=prevH) for i in range(8)]

    wo = [view(RING0 + k * 4096, [D], BF16) for k in range(8)]
    R_wo = [Res("wo%d" % k, prev=R_ring) for k in range(8)]
    for k in range(8):
        R_wo[k].dsem = R_ring[k].dsem
    wo2 = [view(H_OFF + HB // 2 + k * 4096, [D], BF16) for k in range(8)]
    R_wo2 = []
    ds_wo2 = [DSem(nc, "d_wo2_%d" % i) for i in range(2)]

    def wo_load(k):
        load(POOL, wo[k][:, :], w_out[k * 128:(k + 1) * 128, :], R_wo[k])

    def e_loads(h):
        p = h % 2
        if h > 0:
            e_loads_big(h, KTh[p], Vh[p], Qh[p], R_KTh[p], R_Vh[p], R_Qh[p])
        ACT.op(lambda: nc.scalar.memzero(Bh[p][:, :]), writes=[R_Bh_ms[p], R_Bh[p]])
        SP.dma(Bh[p][6 * h:6 * h + 3, :], CQ3_s[:, h * 17:h * 17 + 16, :].rearrange("s b t -> s (b t)"), dsem_of(R_Bh[p]),
               reads=[R_CQ3s, R_Bh_ms[p]], pwrites=[R_Bh[p]])
        SP.dma(Bh[p][6 * h + 3:6 * h + 6, :].rearrange("r (a t) -> r a t", t=128), ones_src3, dsem_of(R_Bh[p]),
               reads=[R_c["ones_b"], R_Bh_ms[p]], pwrites=[R_Bh[p]])

    ones_src1 = bass.AP(tensor=ones_b.tensor if hasattr(ones_b, "tensor") else ones_b, offset=0, ap=[[128, 3], [1, 128]])

    def e_loads2(h):
        load(SP, GAh1[:, :], GA_s[h, :, :], R_GAh1, extra_reads=[R_GAs])
        ACT.op(lambda: nc.scalar.memzero(As[:, 2176:2304]), writes=[R_As_ms, R_As])
        SP.dma(As[3:6, 0:2176], NCK_s[:, h * 49 + 32:h * 49 + 49, :].rearrange("s b t -> s (b t)"), dsem_of(R_As),
               reads=[R_NCKs, R_As_ms], pwrites=[R_As])
        SP.dma(As[0:3, 2176:2304], CQ3_s[:, h * 17 + 16, :], dsem_of(R_As), reads=[R_CQ3s, R_As_ms], pwrites=[R_As])
        SP.dma(As[3:6, 2176:2304], ones_src1, dsem_of(R_As), reads=[R_c["ones_b"], R_As_ms], pwrites=[R_As])
        load(POOL, kc[:, :, :], ck[:, h * 128:(h + 1) * 128].rearrange("(b t) c -> t b c", t=128), R_kc)
        load(POOL, vc[:, :, :], cv[:, h * 128:(h + 1) * 128].rearrange("(b t) c -> t b c", t=128), R_vc)

    units = []
    tiles_e = []
    for h in range(8):
        for T in range(5):
            tid = len(tiles_e)
            tiles_e.append((h, T))
            if T == 3:
                units.append(dict(h=h, T=T, tid=tid, kind="ktc", j=0, first=False, last=False, smp=False))
                units.append(dict(h=h, T=T, tid=tid, kind="ktc", j=1, first=False, last=False, smp=False))
            if T < 4:
                kbs = []
                for jp in range(4 * T + 4):
                    kbs.append(("own", jp))
                    kbs.append(("oth", jp))
            else:
                kbs = [("cacheall", 0), ("new", 16)]
            for i, (kk, jp) in enumerate(kbs):
                units.append(dict(h=h, T=T, tid=tid, kind=kk, j=jp, first=(i == 0), last=(i == len(kbs) - 1), smp=(T == 4)))

    NSB = 5
    NPB = len(Pb)

    def e_S(i):
        u = units[i]
        h, T, p = u["h"], u["T"], u["h"] % 2
        b = i % NSB
        u["sb"] = b
        if u["kind"] == "ktc":
            half = u["j"]
            for q4 in range(8):
                blk = half * 8 + q4
                PE.op(lambda blk=blk, q4=q4: nc.tensor.transpose(
                    PSB[b][:, :].bitcast(BF16)[:, q4 * 128:(q4 + 1) * 128], kc[:, blk, :], ident_b[:, :]),
                    reads=[R_kc, R_c["ident_b"]], writes=[PSR[b]] if q4 == 0 else [], pwrites=[] if q4 == 0 else [PSR[b]],
                    sig=(q4 == 7))
        elif not u["smp"]:
            jlo = max(u["j"], 4 * T)
            off = (jlo - 4 * T) * 128
            n = 512 - off
            kb = (u["j"] if u["kind"] == "own" else 16 + u["j"])
            kcol = kb * 128
            q0 = 4 * T * 128 + off
            u.update(off=off, n=n, kb=kb, rows=128)
            diag = u["j"] >= 4 * T
            PE.op(lambda: mm(PSB[b][:, off:off + n], KTh[p][:, kcol:kcol + 128], Qh[p][:, q0:q0 + n], start=True, stop=False),
                  reads=[R_KTh[p], R_Qh[p]], writes=[PSR[b]], sig=False)
            PE.op(lambda: mm(PSB[b][:, off:off + n], A_all[:, kcol:kcol + 128], Bh[p][:, q0:q0 + n], start=False, stop=not diag),
                  reads=[R_Aall, R_Bh[p]], pwrites=[PSR[b]], sig=not diag)
            if diag:
                mi = 0 if u["kind"] == "own" else 1 + u["j"] % 2
                PE.op(lambda: mm(PSB[b][:, off:off + 128], ident_b[:, :], mneg_b[:, mi, :], start=False, stop=True),
                      reads=[R_c["ident_b"], R_mnb], pwrites=[PSR[b]])
        elif u["kind"] == "cacheall":
            u.update(off=0, n=16 * NS, rows=128)
            for jj in range(16):
                PE.op(lambda jj=jj: mm(PSB[b][:, jj * NS:(jj + 1) * NS], KTc[:, jj * 128:(jj + 1) * 128], Qh[p][:, 2048:2048 + NS],
                                       start=(jj == 0), stop=False, skip_group_check=True),
                      reads=[R_KTc, R_Qh[p]], writes=[PSR[b]] if jj == 0 else [], pwrites=[] if jj == 0 else [PSR[b]], sig=False)
            for jj in range(16):
                kcol = (1 + jj) * 128
                PE.op(lambda jj=jj, kcol=kcol: mm(PSB[b][:, jj * NS:(jj + 1) * NS], As[:, kcol:kcol + 128], As[:, 2176:2176 + NS],
                                                 start=False, stop=(jj == 15), skip_group_check=True),
                      reads=[R_As], pwrites=[PSR[b]], sig=(jj == 15))
        else:
            u.update(off=0, n=NS, rows=NS)
            PE.op(lambda: mm(PSB[b][0:NS, 0:NS], KTh[p][:, 4096:4096 + NS], Qh[p][:, 2048:2048 + NS], start=True, stop=False),
                  reads=[R_KTh[p], R_Qh[p]], writes=[PSR[b]], sig=False)
            PE.op(lambda: mm(PSB[b][0:NS, 0:NS], As[:, 0:NS], As[:, 2176:2176 + NS], start=False, stop=False),
                  reads=[R_As], pwrites=[PSR[b]], sig=False)
            PE.op(lambda: mm(PSB[b][0:NS, 0:NS], ident_b[0:NS, 0:NS], mneg_b[0:NS, 0, 0:NS], start=False, stop=True),
                  reads=[R_c["ident_b"], R_mnb], pwrites=[PSR[b]])

    def e_soft(i):
        u = units[i]
        b = u["sb"]
        if u["kind"] == "ktc":
            half = u["j"]
            ACT.op(lambda: nc.scalar.activation(out=KTc[:, half * 1024:(half + 1) * 1024], in_=PSB[b][:, :].bitcast(BF16), func=AF.Copy),
                   reads=[PSR[b]], writes=[R_KTc] if half == 0 else [], pwrites=[] if half == 0 else [R_KTc])
            return
        off, n, rows = u["off"], u["n"], u["rows"]
        pi = i % NPB
        u["pi"] = pi
        tp = u["tid"] % 2
        ACT.op(lambda: nc.scalar.activation(out=Pb[pi][0:rows, off:off + n], in_=PSB[b][0:rows, off:off + n], func=AF.Exp, scale=SCALE),
               reads=[PSR[b]], writes=[R_Pb[pi]])
        if not u["smp"]:
            if u["kind"] == "oth":
                pa = units[i - 1]["pi"]
                m = u["j"]
                if m % 2 == 0:
                    DVE.op(lambda: v.tensor_tensor(out=ptmp[:, off:off + n], in0=Pb[pa][:, off:off + n], in1=Pb[pi][:, off:off + n], op=ALU.add),
                           reads=[R_Pb[pa], R_Pb[pi]], writes=[R_ptmp])
                    grp["off"], grp["n"] = off, n
                else:
                    oa, na = grp["off"], grp["n"]
                    DVE.op(lambda: v.tensor_tensor(out=ptmp2[:, off:off + n], in0=Pb[pa][:, off:off + n], in1=Pb[pi][:, off:off + n], op=ALU.add),
                           reads=[R_Pb[pa], R_Pb[pi]], writes=[R_ptmp2])
                    if m == 1 and off == 0:
                        DVE.op(lambda: v.tensor_tensor(out=fin[tp][:, :], in0=ptmp[:, :], in1=ptmp2[:, :], op=ALU.add),
                               reads=[R_ptmp, R_ptmp2], writes=[R_fin[tp]])
                    elif m == 1:
                        ACT.op(lambda: nc.scalar.activation(out=fin[tp][:, :], in_=ptmp[:, :], func=AF.Copy), reads=[R_ptmp], writes=[R_fin[tp]])
                        DVE.op(lambda: v.tensor_tensor(out=fin[tp][:, off:off + n], in0=fin[tp][:, off:off + n], in1=ptmp2[:, off:off + n], op=ALU.add),
                               reads=[R_ptmp2, R_fin[tp]], writes=[R_fin[tp]])
                    else:
                        DVE.op(lambda: v.tensor_tensor(out=ptmp[:, off:off + n], in0=ptmp[:, off:off + n], in1=ptmp2[:, off:off + n], op=ALU.add),
                               reads=[R_ptmp2, R_ptmp], writes=[R_ptmp])
                        DVE.op(lambda: v.tensor_tensor(out=fin[tp][:, oa:oa + na], in0=fin[tp][:, oa:oa + na], in1=ptmp[:, oa:oa + na], op=ALU.add),
                               reads=[R_ptmp, R_fin[tp]], writes=[R_fin[tp]])

    def e_PV(i):
        u = units[i]
        if u["kind"] == "ktc":
            return
        h, T, p = u["h"], u["T"], u["h"] % 2
        off, n, pi, rows = u["off"], u["n"], u["pi"], u["rows"]
        tp = u["tid"] % 2
        ab, db = 5 + tp, 7
        if not u["smp"]:
            PE.op(lambda: mm(PSB[ab][:, off:off + n], Vh[p][:, u["kb"], :], Pb[pi][:, off:off + n], start=u["first"], stop=u["last"]),
                  reads=[R_Vh[p], R_Pb[pi]], writes=[PSR[ab]] if u["first"] else [], pwrites=[] if u["first"] else [PSR[ab]], sig=True)
            if u["last"]:
                def _den(db=db, tp=tp):
                    PE.op(lambda: mm(PSB[db][:, :], ones_f[:, :], fin[tp][:, :], start=True, stop=True),
                          reads=[R_c["ones_f"], R_fin[tp]], writes=[PSR[db]])
                pending.append((i + 4, _den))
        elif u["kind"] == "cacheall":
            for jj in range(16):
                PE.op(lambda jj=jj: mm(PSB[ab][:, 0:NS], vc[:, jj, :], Pb[pi][:, jj * NS:(jj + 1) * NS], start=(jj == 0), stop=False),
                      reads=[R_vc, R_Pb[pi]], writes=[PSR[ab]] if jj == 0 else [], pwrites=[] if jj == 0 else [PSR[ab]], sig=False)
                PE.op(lambda jj=jj: mm(PSB[db][:, 0:NS], ones_b[:, :], Pb[pi][:, jj * NS:(jj + 1) * NS], start=(jj == 0), stop=False),
                      reads=[R_c["ones_b"], R_Pb[pi]], writes=[PSR[db]] if jj == 0 else [], pwrites=[] if jj == 0 else [PSR[db]],
                      sig=(jj == 15))
        else:
            PE.op(lambda: mm(PSB[ab][:, 0:NS], Vh[p][0:NS, 32, :], Pb[pi][0:NS, 0:NS], start=False, stop=True),
                  reads=[R_Vh[p], R_Pb[pi]], pwrites=[PSR[ab]], sig=False)
            PE.op(lambda: mm(PSB[db][:, 0:NS], ones_b[0:NS, :], Pb[pi][0:NS, 0:NS], start=False, stop=True),
                  reads=[R_c["ones_b"], R_Pb[pi]], pwrites=[PSR[db]], sig=True)
        if u["last"]:
            n_t = NS if u["smp"] else 512
            c0 = 2048 if u["smp"] else T * 512
            f = fin[tp]

            def _fin(db=db, ab=ab, tp=tp, n_t=n_t, c0=c0, f=f, h=h, T=T):
                DVE.op(lambda: v.reciprocal(out=f[:, 0:n_t], in_=PSB[db][:, 0:n_t]), reads=[PSR[db]], writes=[R_fin[tp]])
                DVE.op(lambda: v.tensor_tensor(out=f[:, 0:n_t], in0=PSB[ab][:, 0:n_t], in1=f[:, 0:n_t], op=ALU.mult),
                       reads=[PSR[ab], R_fin[tp]], writes=[R_fin[tp]])
                POOL.op(lambda: g.tensor_tensor(out=oTA[:, h, c0:c0 + n_t], in0=f[:, 0:n_t], in1=GAh1[:, c0:c0 + n_t], op=ALU.mult),
                        reads=[R_fin[tp], R_GAh1], writes=[R_oTA[h]] if T == 0 else [], pwrites=[] if T == 0 else [R_oTA[h]])
            if u["smp"]:
                _fin()
            else:
                pending.append((i + 4, _fin))

    pending = []

    def flush(upto):
        keep = []
        for (at, fn) in pending:
            if at <= upto:
                fn()
            else:
                keep.append((at, fn))
        pending[:] = keep

    e_loads(0)
    NU = len(units)
    LA = 4
    seen_heads = set()
    for i in range(NU + LA):
        if i < NU:
            e_S(i)
        jx = i - LA
        if jx >= 0:
            u = units[jx]
            if u["h"] not in seen_heads:
                flush(10 ** 9)
                seen_heads.add(u["h"])
                if u["h"] + 1 < 8:
                    e_loads(u["h"] + 1)
                e_loads2(u["h"])
                if u["h"] >= 1:
                    wo_load(u["h"] - 1)
                if u["h"] == 7:
                    wo_load(7)
                    for k in range(5):
                        R_wo2.append(Res("wo2_%d" % k, prev=[R_KTh[0], R_Vh[0], R_Qh[0]]))
                        R_wo2[k].dsem = ds_wo2[k // 4]
                        load(POOL, wo2[k][:, :], w_out[(8 + k) * 128:(9 + k) * 128, :], R_wo2[k])
            e_soft(jx)
            e_PV(jx)
            flush(jx)
    flush(10 ** 9)

    if stop_after == "E":
        drain(); return nc
    prevE2 = R_KTh[0:1] + R_Vh[0:1] + R_Qh[0:1] + [R_GAh1, R_KTc, R_ptmp, R_ptmp2, R_Pb[5]]
    prevEM = R_KTh[1:2] + R_Vh[1:2] + R_Qh[1:2] + [R_kc, R_vc, R_Aall] + R_Bh + [R_mnb, R_As] + R_Pb + R_fin
    R_wo2.extend([Res("wo2_%d" % k, prev=prevE2) for k in range(5, 8)])
    for k in range(5, 8):
        R_wo2[k].dsem = ds_wo2[k // 4]
        load(POOL, wo2[k][:, :], w_out[(8 + k) * 128:(9 + k) * 128, :], R_wo2[k])
    regroup(R_wo2)
    wo_all = wo + wo2
    R_wo_all = R_wo + R_wo2
    MF = Bump(M_OFF, MB)
    xr = [MF.take([D], F32) for _ in range(2)]
    yb = [MF.take([D], F32) for _ in range(2)]
    fgb = MF.take([D], F32)
    junkF = MF.take([D], BF16)
    R_xr = [Res("xr%d" % i, prev=prevEM) for i in range(2)]
    R_yb = [Res("yb%d" % i, prev=prevEM) for i in range(2)]
    R_fg = Res("fg", prev=prevEM)
    R_junkF = Res("junkF", prev=prevEM)
    R_ssF = [Res("ssF%d" % i) for i in range(2)]
    load(SP, fgb[:, :], final_g.partition_broadcast(128), R_fg)

    def f_load(n):
        rows = 128 if n < NB else NS
        src = x_own[n * 128:(n + 1) * 128, :] if n < NB else x_smp
        load(SP, xr[n % 2][0:rows, :], src, R_xr[n % 2])

    f_load(0)
    for n in range(NB + 1):
        rows = 128 if n < NB else NS
        c0 = n * 128
        p = n % 2
        if n + 1 < NB + 1:
            f_load(n + 1)
        for k in range(KC):
            lhs = oTA[:, k, c0:c0 + rows] if k < 8 else oTB[:, k - 8, c0:c0 + rows]
            rl = R_oTA[k] if k < 8 else R_oTB[k - 8]
            for cg in range(4):
                b = 4 * p + cg
                PE.op(lambda lhs=lhs, k=k, cg=cg, b=b: mm(PSB[b][0:rows, :], lhs, wo_all[k][:, cg * 512:(cg + 1) * 512],
                                                          start=(k == 0), stop=(k == KC - 1)),
                      reads=[rl, R_wo_all[k]], writes=[PSR[b]] if k == 0 else [], pwrites=[] if k == 0 else [PSR[b]],
                      sig=(k == KC - 1))
        for cg in range(4):
            b = 4 * p + cg
            DVE.op(lambda cg=cg, b=b: v.tensor_tensor(out=yb[p][0:rows, cg * 512:(cg + 1) * 512], in0=PSB[b][0:rows, :],
                                                      in1=xr[p][0:rows, cg * 512:(cg + 1) * 512], op=ALU.add),
                   reads=[PSR[b], R_xr[p]], writes=[R_yb[p]] if cg == 0 else [], pwrites=[] if cg == 0 else [R_yb[p]])
        so = 40 + p * 4
        ss, sd, rs = [sml[0:rows, so + i:so + i + 1] for i in range(3)]
        ACT.op(lambda: nc.scalar.activation(out=junkF[0:rows, :], in_=yb[p][0:rows, :], func=AF.Square, accum_out=ss),
               reads=[R_yb[p]], writes=[R_junkF, R_ssF[p]])
        ACT.op(lambda: nc.scalar.activation(out=sd, in_=ss, func=AF.Sqrt, scale=1.0 / D, bias=RMS_EPS), reads=[R_ssF[p]], writes=[R_ssF[p]])
        DVE.op(lambda: v.reciprocal(out=rs, in_=sd), reads=[R_ssF[p]], writes=[R_ssF[p]])
        DVE.op(lambda: v.scalar_tensor_tensor(out=yb[p][0:rows, :], in0=yb[p][0:rows, :], scalar=rs, in1=fgb[0:rows, :],
                                              op0=ALU.mult, op1=ALU.mult),
                reads=[R_yb[p], R_ssF[p], R_fg], writes=[R_yb[p]])
        dst = y_own[n * 128:(n + 1) * 128, :] if n < NB else y_smp
        store(SP, dst, yb[p][0:rows, :], R_yb[p])

    drain()
    return nc


_NC_CACHE = {}


def _own_blocks(p):
    gown = [2 * j + ((j + p) % 2) for j in range(NB)]
    goth = [2 * j + 1 - ((j + p) % 2) for j in range(NB)]
    return gown, goth


def make_in_maps(inputs):
    f32 = np.float32
    xp = np.ascontiguousarray(inputs["x_prompt"], dtype=f32)
    xs = np.ascontiguousarray(inputs["x_sample"], dtype=f32)
    cache_k = np.asarray(inputs["cache_k"], dtype=f32)
    cache_v = np.asarray(inputs["cache_v"], dtype=f32)
    cache_lf = np.asarray(inputs["cache_logf"], dtype=f32)
    shared = dict(
        w_in=np.ascontiguousarray(inputs["w_in"][0], dtype=f32), w_out=np.ascontiguousarray(inputs["w_out"][0], dtype=f32),
        norm_g=np.ascontiguousarray(inputs["norm_g"][0], dtype=f32), b_f=np.ascontiguousarray(inputs["b_f"][0], dtype=f32),
        ln_g=np.ascontiguousarray(inputs["ln_g"][0], dtype=f32), ln_b=np.ascontiguousarray(inputs["ln_b"][0], dtype=f32),
        w_s=np.ascontiguousarray(inputs["w_s"][0], dtype=f32), b_s=np.ascontiguousarray(inputs["b_s"][0], dtype=f32),
        final_g=np.ascontiguousarray(inputs["final_g"], dtype=f32))
    in_maps = []
    for c in range(8):
        b, p = c // 2, c % 2
        gown, goth = _own_blocks(p)
        xb = xp[b].reshape(32, 128, D)
        fo = np.zeros((128, 16, 8), f32)
        mno = np.zeros((2, 128, 128), f32)
        for j in range(NB):
            if (j + p) % 2 == 1:
                fo[:, j, :] = 1.0
        for r in range(2):
            mno[r] = 0.0 if (r + p) % 2 == 1 else NEG
        m = dict(shared)
        m.update(
            x_own=np.ascontiguousarray(xb[gown].reshape(NB * 128, D)),
            x_oth=np.ascontiguousarray(xb[goth].reshape(NB * 128, D)),
            x_smp=np.ascontiguousarray(xs[c]),
            ck=np.ascontiguousarray(cache_k[0, c].reshape(PAST, 1024)),
            cv=np.ascontiguousarray(cache_v[0, c].reshape(PAST, 1024)),
            clf=np.ascontiguousarray(cache_lf[0, c]),
            fo=fo.reshape(128, 128), fe=(1.0 - fo).reshape(128, 128), mno=mno)
        in_maps.append(m)
    return in_maps


def assemble(results):
    f32 = np.float32
    y_prompt = np.zeros((4, 32, 128, D), f32)
    k_prompt = np.zeros((1, 4, 32, 128, 8, 128), f32)
    v_prompt = np.zeros((1, 4, 32, 128, 8, 128), f32)
    lf_prompt = np.zeros((1, 4, 32, 128, 8), f32)
    y_sample = np.zeros((8, NS, D), f32)
    k_sample = np.zeros((1, 8, NS, 8, 128), f32)
    v_sample = np.zeros((1, 8, NS, 8, 128), f32)
    lf_sample = np.zeros((1, 8, NS, 8), f32)
    gv_sample = np.zeros((1, 8, NS, 1024), f32)
    for c in range(8):
        r = results[c]
        b, p = c // 2, c % 2
        gown, _ = _own_blocks(p)
        y_prompt[b, gown] = np.asarray(r["y_own"]).reshape(NB, 128, D)
        k_prompt[0, b, gown] = np.asarray(r["k_own"]).reshape(NB, 128, 8, 128)
        v_prompt[0, b, gown] = np.asarray(r["v_own"]).reshape(NB, 128, 8, 128)
        lf_prompt[0, b, gown] = np.asarray(r["lf_own"]).reshape(NB, 128, 8)
        y_sample[c] = np.asarray(r["y_smp"])
        k_sample[0, c] = np.asarray(r["k_smp"]).reshape(NS, 8, 128)
        v_sample[0, c] = np.asarray(r["v_smp"]).reshape(NS, 8, 128)
        lf_sample[0, c] = np.asarray(r["lf_smp"])
        gv_sample[0, c] = np.asarray(r["gv_smp"])
    return (y_prompt.reshape(4, 4096, D), y_sample, k_prompt.reshape(1, 4, 4096, 8, 128),
            v_prompt.reshape(1, 4, 4096, 8, 128), lf_prompt.reshape(1, 4, 4096, 8),
            k_sample, v_sample, lf_sample, gv_sample)


def kernel(**inputs):
    if "nc" not in _NC_CACHE:
        _NC_CACHE["nc"] = build_program()
    nc = _NC_CACHE["nc"]
    in_maps = make_in_maps(inputs)
    res = run_bass_kernel_spmd(nc, in_maps, core_ids=list(range(8)))
    return assemble(res.results)
```
